# Optimizing a Trainium2 kernel written in Bass

```python
import jax, jax.numpy as jnp
from jax import lax
import numpy as np

D_MODEL = 1024
BATCH = 8
SEQ = 4096
DEPTH = 4

N_MIXERS = 3
ROPE_THETA = 10000.0
NORM_EPS = 1e-6
NEG_INF = -1e30
ATTN_BLOCK = 128
D_FF = 4 * D_MODEL

MLA_HEADS = 16
MLA_Q_RANK = 384
MLA_KV_RANK = 256
MLA_NOPE_DIM = 64
MLA_ROPE_DIM = 32
MLA_V_DIM = 64

FOX_HEADS = 16
FOX_HEAD_DIM = 64
FORGET_BIAS_CENTER = 2.0

DIL_PATTERNS = ((128, 1), (512, 4), (2048, 16))
DIL_GROUPS = len(DIL_PATTERNS)
DIL_HEADS = 16
DIL_HEAD_DIM = 64
DIL_KEYS = max(w // d for w, d in DIL_PATTERNS) + 1
DIL_BLOCK = 32

kernel_name = "hybrid_mla_fox_dilated_interleaved"


def rmsnorm(x, g):
    xf = x.astype(jnp.float32)
    y = xf * lax.rsqrt(jnp.mean(xf * xf, axis=-1, keepdims=True) + NORM_EPS)
    return (y * g.astype(jnp.float32)).astype(x.dtype)


def rope_tables(seq_len, dim):
    inv = 1.0 / (ROPE_THETA ** (jnp.arange(0, dim, 2, dtype=jnp.float32) / dim))
    ang = jnp.arange(seq_len, dtype=jnp.float32)[:, None] * inv[None, :]
    return jnp.cos(ang), jnp.sin(ang)


def apply_rope(x, cos, sin):
    half = x.shape[-1] // 2
    x1 = x[..., :half].astype(jnp.float32)
    x2 = x[..., half:].astype(jnp.float32)
    c = cos[None, :, None, :]
    s = sin[None, :, None, :]
    return jnp.concatenate([x1 * c - x2 * s, x1 * s + x2 * c], axis=-1).astype(x.dtype)


def causal_block_attention(q, k, v, scale, cum_log_forget=None):
    seq = q.shape[1]
    outs = []
    for start in range(0, seq, ATTN_BLOCK):
        end = start + ATTN_BLOCK
        s = jnp.einsum('bqhd,bkhd->bhqk', q[:, start:end], k[:, :end]).astype(jnp.float32) * scale
        if cum_log_forget is not None:
            d_q = jnp.transpose(cum_log_forget[:, start:end], (0, 2, 1))
            d_k = jnp.transpose(cum_log_forget[:, :end], (0, 2, 1))
            s = s + (d_q[..., :, None] - d_k[..., None, :])
        q_pos = jnp.arange(start, end)
        k_pos = jnp.arange(end)
        s = jnp.where(k_pos[None, :] <= q_pos[:, None], s, NEG_INF)
        p = jax.nn.softmax(s, axis=-1).astype(v.dtype)
        outs.append(jnp.einsum('bhqk,bkhd->bqhd', p, v[:, :end]))
    return jnp.concatenate(outs, axis=1)


def mla_mixer(h, cos, sin, wq_a, q_norm, wq_b, wkv_a, kv_norm, wkv_b, wo):
    b, s, _ = h.shape
    c_q = rmsnorm(h @ wq_a, q_norm)
    q = (c_q @ wq_b).reshape(b, s, MLA_HEADS, MLA_NOPE_DIM + MLA_ROPE_DIM)
    q = jnp.concatenate([q[..., :MLA_NOPE_DIM], apply_rope(q[..., MLA_NOPE_DIM:], cos, sin)], axis=-1)
    kv_a = h @ wkv_a
    c_kv = rmsnorm(kv_a[..., :MLA_KV_RANK], kv_norm)
    k_pe = apply_rope(kv_a[..., None, MLA_KV_RANK:], cos, sin)
    kv = (c_kv @ wkv_b).reshape(b, s, MLA_HEADS, MLA_NOPE_DIM + MLA_V_DIM)
    k = jnp.concatenate([kv[..., :MLA_NOPE_DIM],
                         jnp.broadcast_to(k_pe, (b, s, MLA_HEADS, MLA_ROPE_DIM))], axis=-1)
    v = kv[..., MLA_NOPE_DIM:]
    o = causal_block_attention(q, k, v, (MLA_NOPE_DIM + MLA_ROPE_DIM) ** -0.5)
    return o.reshape(b, s, MLA_HEADS * MLA_V_DIM) @ wo


def fox_mixer(h, w_qkv, w_f, b_f, wo):
    b, s, _ = h.shape
    qkv = (h @ w_qkv).reshape(b, s, 3, FOX_HEADS, FOX_HEAD_DIM)
    q, k, v = qkv[:, :, 0], qkv[:, :, 1], qkv[:, :, 2]
    log_f = jax.nn.log_sigmoid((h @ w_f).astype(jnp.float32) + b_f.astype(jnp.float32))
    cum = lax.cumsum(log_f, axis=1)
    o = causal_block_attention(q, k, v, FOX_HEAD_DIM ** -0.5, cum)
    return o.reshape(b, s, FOX_HEADS * FOX_HEAD_DIM) @ wo


def dilated_mixer(h, cos, sin, w_qkv, wo):
    b, s, _ = h.shape
    g_n, hd, dh = DIL_GROUPS, DIL_HEADS, DIL_HEAD_DIM
    qkv = (h @ w_qkv).reshape(b, s, 3, g_n, hd, dh)
    q = apply_rope(qkv[:, :, 0].reshape(b, s, g_n * hd, dh), cos, sin).reshape(b, s, g_n, hd, dh)
    k = apply_rope(qkv[:, :, 1].reshape(b, s, g_n * hd, dh), cos, sin).reshape(b, s, g_n, hd, dh)
    v = qkv[:, :, 2]
    dil = jnp.array([d for _, d in DIL_PATTERNS], dtype=jnp.int32)
    win = jnp.array([w for w, _ in DIL_PATTERNS], dtype=jnp.int32)
    jj = jnp.arange(DIL_KEYS, dtype=jnp.int32)
    offsets = jj[None, :] * dil[:, None]
    in_window = offsets <= win[:, None]
    g_idx = jnp.arange(g_n, dtype=jnp.int32)[:, None, None]
    scale = dh ** -0.5

    def block(start):
        qb = lax.dynamic_slice_in_dim(q, start, DIL_BLOCK, axis=1)
        t = start + jnp.arange(DIL_BLOCK, dtype=jnp.int32)
        idx = t[None, :, None] - offsets[:, None, :]
        valid = (idx >= 0) & in_window[:, None, :]
        idx = jnp.maximum(idx, 0)
        kg = k[:, idx, g_idx]
        vg = v[:, idx, g_idx]
        sc = jnp.einsum('bqghd,bgqjhd->bghqj', qb, kg).astype(jnp.float32) * scale
        sc = jnp.where(valid[None, :, None], sc, NEG_INF)
        m = jnp.max(sc, axis=-1, keepdims=True)
        e = jnp.exp(sc - m)
        den = jnp.sum(e, axis=-1, keepdims=True)
        lse = (m + jnp.log(den))[..., 0]
        o_g = jnp.einsum('bghqj,bgqjhd->bqghd', (e / den).astype(v.dtype), vg)
        w_g = jax.nn.softmax(lse, axis=1)
        return jnp.einsum('bghq,bqghd->bqhd', w_g.astype(v.dtype), o_g)

    starts = jnp.arange(0, s, DIL_BLOCK, dtype=jnp.int32)
    o = lax.map(block, starts)
    o = jnp.moveaxis(o, 0, 1).reshape(b, s, hd * dh)
    return o @ wo


def squared_relu_mlp(h, w_up, w_down):
    return jnp.square(jax.nn.relu(h @ w_up)) @ w_down


def _dense(key, fan_in, fan_out):
    return jax.random.normal(key, (fan_in, fan_out), jnp.float32) * fan_in ** -0.5


def _gain(key, n):
    return 1.0 + 0.05 * jax.random.normal(key, (n,), jnp.float32)


def setup_inputs(seed: int = 0) -> dict:
    key = jax.random.key(seed)
    keys = jax.random.split(key, DEPTH + 2)
    p = {'x': jax.random.normal(keys[0], (BATCH, SEQ, D_MODEL), jnp.float32)}
    for i in range(DEPTH):
        lk = jax.random.split(keys[i + 1], 12)
        pre = 'l%d_' % i
        p[pre + 'attn_norm'] = _gain(lk[0], D_MODEL)
        kind = i % N_MIXERS
        if kind == 0:
            p[pre + 'mla_wq_a'] = _dense(lk[1], D_MODEL, MLA_Q_RANK)
            p[pre + 'mla_q_norm'] = _gain(lk[2], MLA_Q_RANK)
            p[pre + 'mla_wq_b'] = _dense(lk[3], MLA_Q_RANK, MLA_HEADS * (MLA_NOPE_DIM + MLA_ROPE_DIM))
            p[pre + 'mla_wkv_a'] = _dense(lk[4], D_MODEL, MLA_KV_RANK + MLA_ROPE_DIM)
            p[pre + 'mla_kv_norm'] = _gain(lk[5], MLA_KV_RANK)
            p[pre + 'mla_wkv_b'] = _dense(lk[6], MLA_KV_RANK, MLA_HEADS * (MLA_NOPE_DIM + MLA_V_DIM))
            p[pre + 'mla_wo'] = _dense(lk[7], MLA_HEADS * MLA_V_DIM, D_MODEL)
        elif kind == 1:
            p[pre + 'fox_w_qkv'] = _dense(lk[1], D_MODEL, 3 * FOX_HEADS * FOX_HEAD_DIM)
            p[pre + 'fox_w_f'] = _dense(lk[2], D_MODEL, FOX_HEADS)
            p[pre + 'fox_b_f'] = FORGET_BIAS_CENTER + 0.5 * jax.random.normal(lk[3], (FOX_HEADS,), jnp.float32)
            p[pre + 'fox_wo'] = _dense(lk[4], FOX_HEADS * FOX_HEAD_DIM, D_MODEL)
        else:
            p[pre + 'dil_w_qkv'] = _dense(lk[1], D_MODEL, 3 * DIL_GROUPS * DIL_HEADS * DIL_HEAD_DIM)
            p[pre + 'dil_wo'] = _dense(lk[2], DIL_HEADS * DIL_HEAD_DIM, D_MODEL)
        p[pre + 'mlp_norm'] = _gain(lk[8], D_MODEL)
        p[pre + 'w_up'] = _dense(lk[9], D_MODEL, D_FF)
        p[pre + 'w_down'] = _dense(lk[10], D_FF, D_MODEL)
    p['final_norm'] = _gain(keys[DEPTH + 1], D_MODEL)
    return p


def reference(x,
              l0_attn_norm, l0_mla_wq_a, l0_mla_q_norm, l0_mla_wq_b, l0_mla_wkv_a, l0_mla_kv_norm,
              l0_mla_wkv_b, l0_mla_wo, l0_mlp_norm, l0_w_up, l0_w_down,
              l1_attn_norm, l1_fox_w_qkv, l1_fox_w_f, l1_fox_b_f, l1_fox_wo,
              l1_mlp_norm, l1_w_up, l1_w_down,
              l2_attn_norm, l2_dil_w_qkv, l2_dil_wo, l2_mlp_norm, l2_w_up, l2_w_down,
              l3_attn_norm, l3_mla_wq_a, l3_mla_q_norm, l3_mla_wq_b, l3_mla_wkv_a, l3_mla_kv_norm,
              l3_mla_wkv_b, l3_mla_wo, l3_mlp_norm, l3_w_up, l3_w_down,
              final_norm):
    seq = x.shape[1]
    cos_mla, sin_mla = rope_tables(seq, MLA_ROPE_DIM)
    cos_dil, sin_dil = rope_tables(seq, DIL_HEAD_DIM)
    layers = [
        (l0_attn_norm, (l0_mla_wq_a, l0_mla_q_norm, l0_mla_wq_b, l0_mla_wkv_a, l0_mla_kv_norm,
                        l0_mla_wkv_b, l0_mla_wo), l0_mlp_norm, l0_w_up, l0_w_down),
        (l1_attn_norm, (l1_fox_w_qkv, l1_fox_w_f, l1_fox_b_f, l1_fox_wo), l1_mlp_norm, l1_w_up, l1_w_down),
        (l2_attn_norm, (l2_dil_w_qkv, l2_dil_wo), l2_mlp_norm, l2_w_up, l2_w_down),
        (l3_attn_norm, (l3_mla_wq_a, l3_mla_q_norm, l3_mla_wq_b, l3_mla_wkv_a, l3_mla_kv_norm,
                        l3_mla_wkv_b, l3_mla_wo), l3_mlp_norm, l3_w_up, l3_w_down),
    ]
    h = x
    for i in range(DEPTH):
        attn_norm, mixer_params, mlp_norm, w_up, w_down = layers[i]
        a = rmsnorm(h, attn_norm)
        kind = i % N_MIXERS
        if kind == 0:
            a = mla_mixer(a, cos_mla, sin_mla, *mixer_params)
        elif kind == 1:
            a = fox_mixer(a, *mixer_params)
        else:
            a = dilated_mixer(a, cos_dil, sin_dil, *mixer_params)
        h = h + a
        h = h + squared_relu_mlp(rmsnorm(h, mlp_norm), w_up, w_down)
    return rmsnorm(h, final_norm)
```

```python
import numpy as np
from contextlib import ExitStack
import ml_dtypes
import concourse.bass as bass
import concourse.mybir as mybir
from concourse.bass_utils import run_bass_kernel_spmd

F32 = mybir.dt.float32
BF16 = mybir.dt.bfloat16
AF = mybir.ActivationFunctionType
ALU = mybir.AluOpType
AX = mybir.AxisListType

S = 4096
D = 1024
DFF = 4096
EPS = 1e-6
NCORES = 8


class Sem:
    def __init__(self, h, name):
        self.h = h
        self.val = 0
        self.name = name


class Buf:
    __slots__ = ("w", "r", "rp", "excl")

    def __init__(self, excl=False):
        self.excl = excl
        self.w = {}
        self.r = {}
        self.rp = {}


class Eng:
    def __init__(self, name, sem):
        self.name = name
        self.sem = sem
        self.waited = {}
        self.ops = []


class KB:
    def __init__(self, nc, es):
        self.nc = nc
        self.es = es
        self.nsem = 0
        self.pe = Eng("pe", self.new_sem("pe"))
        self.act = Eng("act", self.new_sem("act"))
        self.dve = Eng("dve", self.new_sem("dve"))
        self.pool = Eng("pool", self.new_sem("pool"))
        self.sp = Eng("sp", self.new_sem("sp"))
        self.engs = [self.pe, self.act, self.dve, self.pool, self.sp]
        self.bar = self.new_sem("bar")
        self.dsems = []
        self.banks = []
        self.bank_bufs = []
        self.wide = []
        for i in range(4):
            t = es.enter_context(nc.psum_tensor("psw%d" % i, [128, 1024], F32))
            self.wide.append(t)
            for k in range(2):
                self.banks.append(t[:, k * 512:(k + 1) * 512])
                self.bank_bufs.append(Buf(excl=True))
        self.bank_rr = 0
        self.nb = 6
        self.cast_engs = (self.dve, self.act)
        self.piece_q = None

    def new_sem(self, name):
        h = self.es.enter_context(self.nc.semaphore(name))
        self.nsem += 1
        return Sem(h, name)

    def dsem(self, name):
        s = self.new_sem(name)
        self.dsems.append(s)
        return s

    def next_bank(self, lo=0, hi=None):
        n = (hi or self.nb) - lo
        i = lo + (self.bank_rr % n)
        self.bank_rr += 1
        return self.banks[i], self.bank_bufs[i]

    def _waits(self, eng, deps):
        for sem, val in deps.items():
            if eng.waited.get(sem, 0) >= val:
                continue
            eng.waited[sem] = val
            eng.ops.append(("wait", sem.h, val))

    def _deps(self, eng_sem, reads, writes, add, skip_same_raw):
        deps = {}

        def put(s, v):
            if v > deps.get(s, 0):
                deps[s] = v

        for b in reads:
            for s, v in b.w.items():
                if s is eng_sem and skip_same_raw:
                    continue
                put(s, v)
            if b.excl:
                for s, v in b.r.items():
                    if s is not eng_sem:
                        put(s, v)
        for b in writes:
            if not add:
                for s, v in b.w.items():
                    if s is not eng_sem or not skip_same_raw:
                        put(s, v)
            else:
                for s, v in b.rp.items():
                    if s is not eng_sem or not skip_same_raw:
                        put(s, v)
            for s, v in b.r.items():
                if s is not eng_sem or not skip_same_raw:
                    put(s, v)
        return deps

    def _mark(self, sem, reads, writes, add):
        for b in writes:
            if add:
                b.w[sem] = sem.val
            else:
                b.rp = b.r
                b.r = {}
                b.w = {sem: sem.val}
        for b in reads:
            b.r[sem] = sem.val

    def op(self, eng, fn, reads=(), writes=(), add=False):
        deps = self._deps(eng.sem, reads, writes, add, eng is self.pe)
        self._waits(eng, deps)
        eng.sem.val += 1
        eng.ops.append(("op", fn, eng.sem.h, 1))
        self._mark(eng.sem, reads, writes, add)

    def dma(self, out, in_, sem, reads=(), writes=(), add=False, q=None, **kw):
        q = q or self.sp
        deps = self._deps(sem, reads, writes, add, True)
        if not add and sem.val > 0:
            deps[sem] = sem.val
        self._waits(q, deps)
        sem.val += 16
        q.ops.append(("op", lambda h: h.dma_start(out=out, in_=in_, **kw), sem.h, 16))
        self._mark(sem, reads, writes, add)

    def barrier(self):
        sp = self.sp
        deps = {e.sem: e.sem.val for e in self.engs if e is not sp and e.sem.val > 0}
        for s in self.dsems:
            if s.val > 0:
                deps[s] = s.val
        self._waits(sp, deps)
        self.bar.val += 1
        sp.ops.append(("seminc", self.bar.h, 1))
        for e in self.engs:
            if e is not sp:
                e.ops.append(("wait", self.bar.h, self.bar.val))
            e.waited = {x.sem: x.sem.val for x in self.engs}
            for s in self.dsems:
                e.waited[s] = s.val
            e.waited[self.bar] = self.bar.val

    def flush(self):
        nc = self.nc
        with nc.Block() as block:
            pairs = ((self.sp, block.sync), (self.act, block.scalar), (self.pe, block.tensor),
                     (self.dve, block.vector), (self.pool, block.gpsimd))
            for eng, deco in pairs:
                ops = eng.ops
                eng.ops = []

                def body(h, ops=ops):
                    for o in ops:
                        if o[0] == "wait":
                            h.wait_ge(o[1], o[2])
                        elif o[0] == "op":
                            o[1](h).then_inc(o[2], o[3])
                        else:
                            h.sem_inc(o[1], o[2])

                deco(body)

    def mm(self, out, lhsT, rhs, start, stop, reads, writes):
        self.op(self.pe, lambda h: h.matmul(out, lhsT=lhsT, rhs=rhs, start=start, stop=stop),
                reads, writes)

    def tr(self, out, in_, ident, reads, writes):
        self.op(self.pe, lambda h: h.transpose(out, in_, ident), reads, writes)

    def actf(self, out, in_, func, reads, writes, add=False, **kw):
        self.op(self.act, lambda h: h.activation(out=out, in_=in_, func=func, **kw), reads, writes, add)

    def ts(self, eng, out, in0, s1, s2, op0, op1, reads, writes, add=False):
        if op1 is None:
            self.op(eng, lambda h: h.tensor_scalar(out=out, in0=in0, scalar1=s1, scalar2=None, op0=op0),
                    reads, writes, add)
        else:
            self.op(eng, lambda h: h.tensor_scalar(out=out, in0=in0, scalar1=s1, scalar2=s2, op0=op0, op1=op1),
                    reads, writes, add)

    def tt(self, eng, out, in0, in1, op, reads, writes, add=False):
        self.op(eng, lambda h: h.tensor_tensor(out=out, in0=in0, in1=in1, op=op), reads, writes, add)

    def cp(self, eng, out, in_, reads, writes, add=False):
        self.op(eng, lambda h: h.tensor_copy(out=out, in_=in_), reads, writes, add)


class Ctx:
    N = 0

    def __init__(self, kb):
        self.kb = kb
        self.es = ExitStack()
        self.n = 0

    def __enter__(self):
        self.es.__enter__()
        return self

    def __exit__(self, *a):
        return self.es.__exit__(*a)

    def sb(self, shape, dt):
        Ctx.N += 1
        t = self.es.enter_context(self.kb.nc.sbuf_tensor("t%d" % Ctx.N, shape, dt))
        return t, Buf()


def load_cast_weight(kb, cx, stg, w_dram, rows0, nrows_chunks, ncols, dst, dst_buf, gain=None, rr=[0]):
    for c in range(nrows_chunks):
        for q0 in range(0, ncols, 1024):
            qn = min(1024, ncols - q0)
            st, sbuf, ssem = stg[rr[0] % len(stg)]
            eng = (kb.dve, kb.pool, kb.act)[rr[0] % 3]
            rr[0] += 1
            kb.dma(st[:, 0:qn], w_dram[rows0 + c * 128: rows0 + (c + 1) * 128, q0:q0 + qn], ssem, writes=[sbuf])
            o = dst[:, c, q0:q0 + qn]
            if gain is None:
                if eng is kb.act:
                    kb.actf(o, st[:, 0:qn], AF.Copy, [sbuf], [dst_buf], add=True)
                else:
                    kb.cp(eng, o, st[:, 0:qn], [sbuf], [dst_buf], add=True)
            else:
                g, gbuf = gain
                if eng is kb.act:
                    kb.actf(o, st[:, 0:qn], AF.Copy, [sbuf, gbuf], [dst_buf], add=True, scale=g[:, c:c + 1])
                else:
                    kb.ts(eng, o, st[:, 0:qn], g[:, c:c + 1], None, ALU.mult, None, [sbuf, gbuf], [dst_buf], add=True)


def make_stg(kb, cx, n=3):
    stg = []
    for i in range(n):
        t, b = cx.sb([128, 1024], F32)
        stg.append((t, b, kb.stg_sems[i]))
    return stg


def load_gain(kb, cx, g_dram, nchunks):
    g, gb = cx.sb([128, nchunks], F32)
    kb.dma(g[:, :], g_dram.rearrange("(c p) -> p c", p=128), kb.misc_sem, writes=[gb],
           allow_slow_non_contiguous=True)
    return g, gb


def rms_rstd(kb, x_ap, n, junk_ap, junk_b, ss, ss_b, col, xb, add=False):
    kb.actf(junk_ap, x_ap, AF.Square, [xb], [junk_b, ss_b], accum_out=ss[:, col:col + 1], add=add)


def rstd_finish(kb, ss, ss_b, c0, c1, n):
    kb.actf(ss[:, c0:c1], ss[:, c0:c1], AF.Sqrt, [ss_b], [ss_b], scale=1.0 / n, bias=EPS)
    kb.op(kb.dve, lambda h: h.reciprocal(out=ss[:, c0:c1], in_=ss[:, c0:c1]), [ss_b], [ss_b])


def mlp_weight_pieces(kb, stg, w_up, w_down, g, wu, wu_b, wd, wd_b):
    for c in range(8):
        for q0 in range(0, DFF, 1024):
            yield lambda c=c, q0=q0: load_cast_piece(kb, stg, w_up[c * 128:(c + 1) * 128, q0:q0 + 1024],
                                                     wu[:, c, q0:q0 + 1024], wu_b, (g[0][:, c:c + 1], g[1]))
    for c in range(32):
        yield lambda c=c: load_cast_piece(kb, stg, w_down[c * 128:(c + 1) * 128, :], wd[:, c, :], wd_b, None)


def load_cast_piece(kb, stg, src, dst, dst_b, gain, rr=[0]):
    st, sbuf, ssem = stg[rr[0] % len(stg)]
    eng = kb.cast_engs[rr[0] % len(kb.cast_engs)]
    rr[0] += 1
    n = dst.shape[-1]
    kb.dma(st[:, 0:n], src, ssem, writes=[sbuf], q=kb.piece_q)
    if gain is None:
        if eng is kb.act:
            kb.actf(dst, st[:, 0:n], AF.Copy, [sbuf], [dst_b], add=True)
        else:
            kb.cp(eng, dst, st[:, 0:n], [sbuf], [dst_b], add=True)
    else:
        g, gbuf = gain
        if eng is kb.act:
            kb.actf(dst, st[:, 0:n], AF.Copy, [sbuf, gbuf], [dst_b], add=True, scale=g)
        else:
            kb.ts(eng, dst, st[:, 0:n], g, None, ALU.mult, None, [sbuf, gbuf], [dst_b], add=True)


def phase_mlp(kb, H_in, H_out, w_up, w_down, g_norm, final=None, ntiles=16, pre=None):
    kb.nb = 8
    T = 256
    with Ctx(kb) as cx:
        if pre is None:
            wu, wu_b = cx.sb([128, 8, DFF], BF16)
            wd, wd_b = cx.sb([128, 32, D], BF16)
            stg = make_stg(kb, cx)
            g = load_gain(kb, cx, g_norm, 8)
            pieces = mlp_weight_pieces(kb, stg, w_up, w_down, g, wu, wu_b, wd, wd_b)
        else:
            wu, wu_b, wd, wd_b, pieces = pre
        for p in pieces:
            p()
        hx = [cx.sb([128, 2, D], F32) for _ in range(3)]
        a2, a2_b = cx.sb([128, 2, D], BF16)
        a2T = [cx.sb([128, 8, T], BF16) for _ in range(2)]
        uT, uT_b = cx.sb([128, 32, T], BF16)
        rl = [cx.sb([128, 512], F32) for _ in range(3)]
        ss, ss_b = cx.sb([128, 8], F32)
        if final is not None:
            gfin, gfin_b = cx.sb([128, D], F32)
            kb.dma(gfin[:, :], final[0].partition_broadcast(128), kb.misc_sem, writes=[gfin_b])

        def load(t):
            s = t % 3
            kb.dma(hx[s][0][:, :, :], H_in[t * T:(t + 1) * T, :].rearrange("(j p) f -> p j f", p=128),
                   kb.ld_sems[s], writes=[hx[s][1]])

        def norm_part(t):
            h_t, h_b = hx[t % 3]
            o = (t % 2) * 2
            for j in range(2):
                rms_rstd(kb, h_t[:, j, :], D, a2[:, j, :], a2_b, ss, ss_b, o + j, h_b, add=(j > 0))
            rstd_finish(kb, ss, ss_b, o, o + 2, D)
            kb.ts(kb.dve, a2[:, 0, :], h_t[:, 0, :], ss[:, o:o + 1], None, ALU.mult, None, [h_b, ss_b], [a2_b])
            kb.actf(a2[:, 1, :], h_t[:, 1, :], AF.Copy, [h_b, ss_b], [a2_b], add=True, scale=ss[:, o + 1:o + 2])

        def transpose_part(t):
            aT, aT_b = a2T[t % 2]
            for j in range(2):
                bk, bk_b = kb.next_bank()
                bkv = bk[:, :].bitcast(BF16)
                for c in range(8):
                    kb.tr(bkv[:, c * 128:(c + 1) * 128], a2[:, j, c * 128:(c + 1) * 128], kb.ident[:, :],
                          [a2_b, kb.const_b], [bk_b])
                src = bkv.rearrange("p (c t) -> p c t", c=8)
                if j == 0:
                    kb.cp(kb.dve, aT[:, :, j * 128:(j + 1) * 128], src, [bk_b], [aT_b])
                else:
                    kb.actf(aT[:, :, j * 128:(j + 1) * 128], src, AF.Copy, [bk_b], [aT_b], add=True)

        load(0)
        if ntiles > 1:
            load(1)
        norm_part(0)
        transpose_part(0)
        rli = 0
        for t in range(ntiles):
            s = t % 2
            h_t, h_b = hx[t % 3]
            aT, aT_b = a2T[s]
            if t + 2 < ntiles:
                load(t + 2)
            if t + 1 < ntiles:
                norm_part(t + 1)
            for fp in range(16):
                bk, bk_b = kb.next_bank()
                for hf in range(2):
                    ft = 2 * fp + hf
                    for c in range(8):
                        kb.mm(bk[:, hf * T:(hf + 1) * T], wu[:, c, ft * 128:(ft + 1) * 128], aT[:, c, :],
                              c == 0, c == 7, [wu_b, aT_b], [bk_b])
                r_t, r_b = rl[rli % 3]
                rli += 1
                uo = uT[:, 2 * fp:2 * fp + 2, :].rearrange("p a t -> p (a t)")
                if fp % 2 == 0:
                    kb.actf(r_t[:, :], bk[:, :], AF.Relu, [bk_b], [r_b])
                    kb.tt(kb.dve, uo, r_t[:, :], r_t[:, :], ALU.mult, [r_b], [uT_b], add=(fp > 0))
                else:
                    kb.ts(kb.dve, r_t[:, :], bk[:, :], 0.0, None, ALU.max, None, [bk_b], [r_b])
                    kb.actf(uo, r_t[:, :], AF.Square, [r_b], [uT_b], add=True)
            if t + 1 < ntiles:
                transpose_part(t + 1)
            for j in range(2):
                for hf in range(2):
                    bk, bk_b = kb.next_bank()
                    for ft in range(32):
                        kb.mm(bk[:, :], uT[:, ft, j * 128:(j + 1) * 128], wd[:, ft, hf * 512:(hf + 1) * 512],
                              ft == 0, ft == 31, [uT_b, wd_b], [bk_b])
                    kb.tt(kb.dve, h_t[:, j, hf * 512:(hf + 1) * 512], bk[:, :], h_t[:, j, hf * 512:(hf + 1) * 512],
                          ALU.add, [bk_b, h_b], [h_b])
            dst = H_out[t * T:(t + 1) * T, :].rearrange("(j p) f -> p j f", p=128)
            if final is None:
                kb.dma(dst, h_t[:, :, :], kb.st_sems[t % 3], reads=[h_b])
            else:
                for j in range(2):
                    rms_rstd(kb, h_t[:, j, :], D, a2[:, j, :], a2_b, ss, ss_b, 4 + j, h_b, add=(j > 0))
                rstd_finish(kb, ss, ss_b, 4, 6, D)
                for j in range(2):
                    kb.op(kb.dve,
                          lambda h, j=j, h_t=h_t: h.scalar_tensor_tensor(
                              out=h_t[:, j, :], in0=h_t[:, j, :], scalar=ss[:, 4 + j:5 + j], in1=gfin[:, :],
                              op0=ALU.mult, op1=ALU.mult),
                          [h_b, ss_b, gfin_b], [h_b], add=True)
                kb.dma(final[1][t * T:(t + 1) * T, :].rearrange("(j p) f -> p j f", p=128), h_t[:, :, :],
                       kb.st_sems[t % 3], reads=[h_b])
        kb.barrier()
        kb.flush()


def setup_kb(nc, es, consts_bf):
    kb = KB(nc, es)
    kb.stg_sems = [kb.dsem("stg%d" % i) for i in range(6)]
    kb.ld_sems = [kb.dsem("ld%d" % i) for i in range(4)]
    kb.st_sems = [kb.dsem("st%d" % i) for i in range(8)]
    kb.misc_sem = kb.dsem("misc")
    cb = es.enter_context(nc.sbuf_tensor("constbf", [128, 128], BF16))
    kb.ident = cb
    kb.const_b = Buf()
    kb.dma(cb[:, :], consts_bf[:, 0:128], kb.misc_sem, writes=[kb.const_b])
    return kb


def host_consts():
    ident = np.eye(128, dtype=np.float32)
    return ident.astype(ml_dtypes.bfloat16)


def norm_transpose(kb, h_t, h_b, J, a, a_b, aT, aT_b, junk, junk_b, ss, ss_b):
    for j in range(J):
        rms_rstd(kb, h_t[:, j, :], D, a[:, j, :], a_b, ss, ss_b, j, h_b, add=(j > 0))
    rstd_finish(kb, ss, ss_b, 0, J, D)
    for j in range(J):
        if j % 2 == 0:
            kb.ts(kb.dve, a[:, j, :], h_t[:, j, :], ss[:, j:j + 1], None, ALU.mult, None,
                  [h_b, ss_b], [a_b], add=(j > 0))
        else:
            kb.actf(a[:, j, :], h_t[:, j, :], AF.Copy, [h_b, ss_b], [a_b], add=True, scale=ss[:, j:j + 1])
    for j in range(J):
        bk, bk_b = kb.next_bank()
        bkv = bk[:, :].bitcast(BF16)
        for c in range(8):
            kb.tr(bkv[:, c * 128:(c + 1) * 128], a[:, j, c * 128:(c + 1) * 128], kb.ident[:, :],
                  [a_b, kb.const_b], [bk_b])
        src = bkv.rearrange("p (c t) -> p c t", c=8)
        if j % 2 == 0:
            kb.cp(kb.dve, aT[:, :, j * 128:(j + 1) * 128], src, [bk_b], [aT_b], add=(j > 0))
        else:
            kb.actf(aT[:, :, j * 128:(j + 1) * 128], src, AF.Copy, [bk_b], [aT_b], add=True)


class OutStage:
    def __init__(self, kb, cx, n=6, shape=(128, 512), dt=BF16):
        self.kb = kb
        self.tiles = [cx.sb(list(shape), dt) for _ in range(n)]
        self.i = 0

    def next(self):
        k = self.i % len(self.tiles)
        self.i += 1
        t, b = self.tiles[k]
        return t, b, self.kb.st_sems[k]


def evac(kb, k, out, in_, reads, writes, add=False):
    if k % 2 == 0:
        kb.actf(out, in_, AF.Copy, reads, writes, add=add)
    else:
        kb.cp(kb.dve, out, in_, reads, writes, add=add)


def norm_stats_scale(kb, h_t, h_b, J, a, a_b, ss, ss_b, c0=0):
    for j in range(J):
        rms_rstd(kb, h_t[:, j, :], D, a[:, j, :], a_b, ss, ss_b, c0 + j, h_b, add=(j > 0))
    rstd_finish(kb, ss, ss_b, c0, c0 + J, D)
    for j in range(J):
        if j % 2 == 0:
            kb.ts(kb.dve, a[:, j, :], h_t[:, j, :], ss[:, c0 + j:c0 + j + 1], None, ALU.mult, None,
                  [h_b, ss_b], [a_b], add=(j > 0))
        else:
            kb.actf(a[:, j, :], h_t[:, j, :], AF.Copy, [h_b, ss_b], [a_b], add=True, scale=ss[:, c0 + j:c0 + j + 1])


def transpose_evac(kb, J, a, a_b, aT, aT_b):
    for j in range(J):
        bk, bk_b = kb.next_bank()
        bkv = bk[:, :].bitcast(BF16)
        for c in range(8):
            kb.tr(bkv[:, c * 128:(c + 1) * 128], a[:, j, c * 128:(c + 1) * 128], kb.ident[:, :],
                  [a_b, kb.const_b], [bk_b])
        src = bkv.rearrange("p (c t) -> p c t", c=8)
        if j % 2 == 0:
            kb.cp(kb.dve, aT[:, :, j * 128:(j + 1) * 128], src, [bk_b], [aT_b], add=(j > 0))
        else:
            kb.actf(aT[:, :, j * 128:(j + 1) * 128], src, AF.Copy, [bk_b], [aT_b], add=True)


def phase_mla_proj(kb, H_in, W, scr, rope_tab, ntiles=8):
    T = 512
    kb.nb = 8
    with Ctx(kb) as cx:
        wqa, wqa_b = cx.sb([128, 8, 384], BF16)
        wkva, wkva_b = cx.sb([128, 8, 288], BF16)
        wqb, wqb_b = cx.sb([128, 3, 1536], BF16)
        wkvb, wkvb_b = cx.sb([128, 2, 2048], BF16)
        stg = make_stg(kb, cx, 6)
        g_attn = load_gain(kb, cx, W['attn_norm'], 8)
        g_q = load_gain(kb, cx, W['q_norm'], 3)
        g_kv = load_gain(kb, cx, W['kv_norm'], 2)
        hx = [cx.sb([128, 4, D], F32) for _ in range(2)]
        cs = [cx.sb([128, 4, 64], F32) for _ in range(2)]
        a, a_b = cx.sb([128, 4, D], BF16)
        aTs = [cx.sb([128, 8, T], BF16) for _ in range(2)]
        ss, ss_b = cx.sb([128, 8], F32)
        sq = [cx.sb([128, 2], F32) for _ in range(2)]
        cqh = [cx.sb([128, 384], BF16) for _ in range(2)]
        ckvh = [cx.sb([128, 256], BF16) for _ in range(2)]
        kpe = [cx.sb([128, 32], BF16) for _ in range(2)]
        tka = [cx.sb([128, 64], F32) for _ in range(2)]
        cqT, cqT_b = cx.sb([128, 3, T], BF16)
        ckvT, ckvT_b = cx.sb([128, 2, T], BF16)
        kpeT, kpeT_b = cx.sb([32, T], BF16)
        ta = [cx.sb([128, 512], F32) for _ in range(2)]
        tb = [cx.sb([128, 512], F32) for _ in range(2)]
        qr = [cx.sb([128, 512], BF16) for _ in range(2)]
        qrT, qrT_b = cx.sb([128, 4, T], BF16)
        vsb, vsb_b = cx.sb([128, 4, D], BF16)
        ost = OutStage(kb, cx, 4)
        st = {"ek": 0}

        def load_h(t):
            s = t % 2
            kb.dma(hx[s][0][:, :, :], H_in[t * T:(t + 1) * T, :].rearrange("(j p) f -> p j f", p=128),
                   kb.ld_sems[s], writes=[hx[s][1]])

        def load_cs(t):
            s = t % 2
            kb.dma(cs[s][0][:, :, :], rope_tab[t * T:(t + 1) * T, :].rearrange("(j p) f -> p j f", p=128),
                   kb.ld_sems[2 + s], writes=[cs[s][1]])

        def load(t):
            load_h(t)
            load_cs(t)

        load(0)
        if ntiles > 1:
            load(1)
        load_cast_weight(kb, cx, stg, W['wq_a'], 0, 8, 384, wqa, wqa_b, gain=g_attn)
        load_cast_weight(kb, cx, stg, W['wkv_a'], 0, 8, 288, wkva, wkva_b, gain=g_attn)
        load_cast_weight(kb, cx, stg, W['wq_b'], 0, 3, 1536, wqb, wqb_b, gain=g_q)
        load_cast_weight(kb, cx, stg, W['wkv_b'], 0, 2, 2048, wkvb, wkvb_b, gain=g_kv)
        norm_stats_scale(kb, hx[0][0], hx[0][1], 4, a, a_b, ss, ss_b, 0)
        transpose_evac(kb, 4, a, a_b, aTs[0][0], aTs[0][1])
        for t in range(ntiles):
            s = t % 2
            cs_t, cs_b = cs[s]
            aT, aT_b = aTs[s]
            keep = {}

            def stage1(j):
                tk = slice(j * 128, (j + 1) * 128)
                p2 = j % 2
                bq, bq_b = kb.next_bank()
                for c in range(8):
                    kb.mm(bq[:, 0:384], aT[:, c, tk], wqa[:, c, :], c == 0, c == 7, [aT_b, wqa_b], [bq_b])
                bkv_, bkv_b = kb.next_bank()
                for c in range(8):
                    kb.mm(bkv_[:, 0:288], aT[:, c, tk], wkva[:, c, :], c == 0, c == 7, [aT_b, wkva_b], [bkv_b])
                sq_t, sq_b = sq[p2]
                cq_t, cq_b = cqh[p2]
                ck_t, ck_b = ckvh[p2]
                kp_t, kp_b = kpe[p2]
                tk_t, tk_b = tka[p2]
                kb.actf(cq_t[:, :], bq[:, 0:384], AF.Square, [bq_b], [cq_b, sq_b], accum_out=sq_t[:, 0:1])
                kb.actf(ck_t[:, :], bkv_[:, 0:256], AF.Square, [bkv_b], [ck_b, sq_b], accum_out=sq_t[:, 1:2], add=True)
                kb.actf(sq_t[:, 0:1], sq_t[:, 0:1], AF.Sqrt, [sq_b], [sq_b], scale=1.0 / 384, bias=EPS)
                kb.actf(sq_t[:, 1:2], sq_t[:, 1:2], AF.Sqrt, [sq_b], [sq_b], scale=1.0 / 256, bias=EPS)
                kb.op(kb.dve, lambda h: h.reciprocal(out=sq_t[:, 0:2], in_=sq_t[:, 0:2]), [sq_b], [sq_b])
                kb.ts(kb.dve, cq_t[:, :], bq[:, 0:384], sq_t[:, 0:1], None, ALU.mult, None, [bq_b, sq_b], [cq_b])
                kb.ts(kb.dve, ck_t[:, :], bkv_[:, 0:256], sq_t[:, 1:2], None, ALU.mult, None, [bkv_b, sq_b], [ck_b])
                kb.tt(kb.dve, tk_t[:, 0:32], bkv_[:, 256:288], cs_t[:, j, 0:32], ALU.mult, [bkv_b, cs_b], [tk_b])
                kb.tt(kb.dve, tk_t[:, 32:64], bkv_[:, 256:288], cs_t[:, j, 32:64], ALU.mult, [bkv_b, cs_b], [tk_b],
                      add=True)
                kb.tt(kb.dve, kp_t[:, 0:16], tk_t[:, 0:16], tk_t[:, 48:64], ALU.subtract, [tk_b], [kp_b])
                kb.tt(kb.dve, kp_t[:, 16:32], tk_t[:, 32:48], tk_t[:, 16:32], ALU.add, [tk_b], [kp_b], add=True)

            def stage2(j):
                tk = slice(j * 128, (j + 1) * 128)
                p2 = j % 2
                cq_t, cq_b = cqh[p2]
                ck_t, ck_b = ckvh[p2]
                kp_t, kp_b = kpe[p2]
                bt, bt_b = kb.next_bank()
                btv = bt[:, :].bitcast(BF16)
                for c in range(3):
                    kb.tr(btv[:, c * 128:(c + 1) * 128], cq_t[:, c * 128:(c + 1) * 128], kb.ident[:, :],
                          [cq_b, kb.const_b], [bt_b])
                for c in range(2):
                    kb.tr(btv[:, 384 + c * 128:384 + (c + 1) * 128], ck_t[:, c * 128:(c + 1) * 128], kb.ident[:, :],
                          [ck_b, kb.const_b], [bt_b])
                kb.tr(btv[0:32, 640:768], kp_t[:, :], kb.ident[:, :], [kp_b, kb.const_b], [bt_b])
                kb.actf(cqT[:, :, tk], btv[:, 0:384].rearrange("p (c t) -> p c t", c=3), AF.Copy, [bt_b], [cqT_b],
                        add=(j > 0))
                kb.actf(ckvT[:, :, tk], btv[:, 384:640].rearrange("p (c t) -> p c t", c=2), AF.Copy, [bt_b], [ckvT_b],
                        add=(j > 0))
                kb.actf(kpeT[:, tk], btv[0:32, 640:768], AF.Copy, [bt_b], [kpeT_b], add=(j > 0))
                br, br_b = kb.next_bank()
                for c in range(3):
                    kb.mm(br[:, :], cqT[:, c, tk], wqb[:, c, 1024:1536], c == 0, c == 2, [cqT_b, wqb_b], [br_b])
                ta_t, ta_b = ta[p2]
                tb_t, tb_b = tb[p2]
                qr_t, qr_b = qr[p2]
                x3 = br[:, :].rearrange("p (h e) -> p h e", h=16)
                cc = cs_t[:, j:j + 1, 0:32].broadcast_to([128, 16, 32])
                sn = cs_t[:, j:j + 1, 32:64].broadcast_to([128, 16, 32])
                ta3 = ta_t[:, :].rearrange("p (h e) -> p h e", h=16)
                tb3 = tb_t[:, :].rearrange("p (h e) -> p h e", h=16)
                qr3 = qr_t[:, :].rearrange("p (h e) -> p h e", h=16)
                kb.tt(kb.dve, ta3, x3, cc, ALU.mult, [br_b, cs_b], [ta_b])
                kb.tt(kb.dve, tb3, x3, sn, ALU.mult, [br_b, cs_b], [tb_b])
                kb.tt(kb.pool, qr3[:, :, 0:16], ta3[:, :, 0:16], tb3[:, :, 16:32], ALU.subtract, [ta_b, tb_b], [qr_b])
                kb.tt(kb.dve, qr3[:, :, 16:32], tb3[:, :, 0:16], ta3[:, :, 16:32], ALU.add, [ta_b, tb_b], [qr_b],
                      add=True)

            def stage3(j):
                tk = slice(j * 128, (j + 1) * 128)
                qr_t, qr_b = qr[j % 2]
                bt2, bt2_b = kb.next_bank()
                bt2v = bt2[:, :].bitcast(BF16)
                for c in range(4):
                    kb.tr(bt2v[:, c * 128:(c + 1) * 128], qr_t[:, c * 128:(c + 1) * 128], kb.ident[:, :],
                          [qr_b, kb.const_b], [bt2_b])
                kb.actf(qrT[:, :, tk], bt2v[:, 0:512].rearrange("p (c t) -> p c t", c=4), AF.Copy, [bt2_b], [qrT_b],
                        add=(j > 0))

            if t + 1 < ntiles:
                norm_stats_scale(kb, hx[1 - s][0], hx[1 - s][1], 4, a, a_b, ss, ss_b, 4 * (1 - s))
            if t + 2 < ntiles:
                load_h(t + 2)
            for step in range(6):
                if step < 4:
                    stage1(step)
                if 0 <= step - 1 < 4:
                    stage2(step - 1)
                if 0 <= step - 2 < 4:
                    stage3(step - 2)
            tcol = slice(t * T, (t + 1) * T)
            for ft in range(8):
                bk, bk_b = kb.next_bank()
                for c in range(3):
                    kb.mm(bk[:, :], wqb[:, c, ft * 128:(ft + 1) * 128], cqT[:, c, :], c == 0, c == 2,
                          [wqb_b, cqT_b], [bk_b])
                o_t, o_b, o_s = ost.next()
                evac(kb, st["ek"], o_t[:, :], bk[:, :], [bk_b], [o_b]); st["ek"] += 1
                kb.dma(scr['QN'][ft * 128:(ft + 1) * 128, tcol], o_t[:, :], o_s, reads=[o_b])
            for ft in range(8):
                bk, bk_b = kb.next_bank()
                for c in range(2):
                    kb.mm(bk[:, :], wkvb[:, c, ft * 128:(ft + 1) * 128], ckvT[:, c, :], c == 0, c == 1,
                          [wkvb_b, ckvT_b], [bk_b])
                o_t, o_b, o_s = ost.next()
                evac(kb, st["ek"], o_t[:, :], bk[:, :], [bk_b], [o_b]); st["ek"] += 1
                kb.dma(scr['KN'][ft * 128:(ft + 1) * 128, tcol], o_t[:, :], o_s, reads=[o_b])
            for j in range(4):
                for hf in range(2):
                    bk, bk_b = kb.next_bank()
                    for c in range(2):
                        kb.mm(bk[:, :], ckvT[:, c, j * 128:(j + 1) * 128],
                              wkvb[:, c, 1024 + hf * 512:1024 + (hf + 1) * 512], c == 0, c == 1,
                              [wkvb_b, ckvT_b], [bk_b])
                    evac(kb, st["ek"], vsb[:, j, hf * 512:(hf + 1) * 512], bk[:, :], [bk_b], [vsb_b], add=(j + hf > 0))
                    st["ek"] += 1
            if t + 1 < ntiles:
                transpose_evac(kb, 4, a, a_b, aTs[1 - s][0], aTs[1 - s][1])
            kb.dma(scr['V'][tcol, :].rearrange("(j p) f -> p j f", p=128), vsb[:, :, :], kb.st_sems[4], reads=[vsb_b])
            kb.dma(scr['QR'].rearrange("(c p) t -> p c t", p=128)[:, :, tcol], qrT[:, :, :], kb.st_sems[5],
                   reads=[qrT_b])
            kb.dma(scr['KPE'][:, tcol], kpeT[:, :], kb.st_sems[6], reads=[kpeT_b])
            if t + 2 < ntiles:
                load_cs(t + 2)
        kb.barrier()
        kb.flush()
    kb.nb = 6


def phase_attn(kb, scr, KD, scale, loader, nheads=16, nqt=8, DEPTH=3, pieces=None):
    kb.nb = 6
    from collections import deque
    with Ctx(kb) as cx:
        QT = [cx.sb([KD, S], BF16) for _ in range(2)]
        KT = [cx.sb([KD, S], BF16) for _ in range(2)]
        VA = [cx.sb([128, 32, 128], BF16) for _ in range(2)]
        pT = [cx.sb([128, 1024], BF16) for _ in range(4)]
        rd, rd_b = cx.sb([128, 512], F32)
        osb = [cx.sb([64, 512], BF16) for _ in range(2)]
        mask, mask_b = cx.sb([128, 128], BF16)
        kb.dma(mask[:, :], scr['cbf'][:, 128:256], kb.misc_sem, writes=[mask_b])
        for i in range(2):
            kb.op(kb.pool, lambda h, i=i: h.memset(VA[i][0][:, :, 64:128], 1.0), [], [VA[i][1]])

        def load(h):
            s = h % 2
            loader(kb, h, QT[s], KT[s], kb.at_sems[s], kb.at_sems[2 + s])
            kb.dma(VA[s][0][:, :, 0:64], scr['V'][:, h * 64:(h + 1) * 64].rearrange("(c p) d -> p c d", p=128),
                   kb.at_sems[4 + s], writes=[VA[s][1]], add=True)

        items = []
        for h in range(nheads):
            for qt in range(nqt):
                first = True
                for kc in range(0, 4 * qt, 2):
                    items.append((h, qt, 'pair', kc, first))
                    first = False
                for c in range(4):
                    items.append((h, qt, 'diag', c, first))
                    first = False

        state = {"wi": 0, "pi": 0}

        def emit(item):
            h, qt, kind, k, first = item
            s = h % 2
            q_t, q_b = QT[s]
            k_t, k_b = KT[s]
            v_t, v_b = VA[s]
            oi = 6 + (qt % 2)
            oacc, oacc_b = kb.banks[oi], kb.bank_bufs[oi]
            w = state["wi"] % 3
            state["wi"] += 1
            W = kb.wide[w]
            b0, b1 = kb.bank_bufs[2 * w], kb.bank_bufs[2 * w + 1]
            p_t, p_b = pT[state["pi"] % 4]
            state["pi"] += 1
            qs = slice(qt * 512, (qt + 1) * 512)
            if kind == 'pair':
                for u in range(2):
                    kcs = slice((k + u) * 128, (k + u + 1) * 128)
                    kb.mm(W[:, u * 512:(u + 1) * 512], k_t[:, kcs], q_t[:, qs], True, True, [k_b, q_b], [(b0, b1)[u]])
                kb.actf(p_t[:, :], W[:, :], AF.Exp, [b0, b1], [p_b], scale=scale)

                def pv():
                    for u in range(2):
                        kb.mm(oacc[:, :], v_t[:, k + u, :], p_t[:, u * 512:(u + 1) * 512], first and u == 0, False,
                              [v_b, p_b], [oacc_b])
                return pv
            c = k
            kc = 4 * qt + c
            kcs = slice(kc * 128, (kc + 1) * 128)
            c0 = c * 128
            q0 = qt * 512 + c0
            kb.mm(W[:, c0:c0 + 128], k_t[:, kcs], q_t[:, q0:q0 + 128], True, False, [k_b, q_b], [b0])
            kb.mm(W[:, c0:c0 + 128], kb.ident[:, :], mask[:, :], False, True, [kb.const_b, mask_b], [b0])
            if c < 3:
                kb.mm(W[:, c0 + 128:512], k_t[:, kcs], q_t[:, q0 + 128:(qt + 1) * 512], True, True, [k_b, q_b], [b0])
            kb.actf(p_t[:, c0:512], W[:, c0:512], AF.Exp, [b0], [p_b], scale=scale)

            def pv():
                kb.mm(oacc[:, c0:512], v_t[:, kc, :], p_t[:, c0:512], first, c == 3, [v_b, p_b], [oacc_b])
                if c == 3:
                    kb.op(kb.dve, lambda h_: h_.reciprocal(out=rd[64:128, :], in_=oacc[64:128, :]), [oacc_b], [rd_b])
                    o_t, o_b = osb[qt % 2]
                    kb.tt(kb.dve, o_t[:, :], oacc[0:64, :], rd[64:128, :], ALU.mult, [oacc_b, rd_b], [o_b])
                    kb.dma(scr['OT'][h * 64:(h + 1) * 64, qt * 512:(qt + 1) * 512], o_t[:, :], kb.st_sems[qt % 2],
                           reads=[o_b])
            return pv

        load(0)
        pend = deque()
        cur_h, since = -1, 0
        for it in items:
            if it[0] != cur_h:
                cur_h, since = it[0], 0
            since += 1
            if since == DEPTH + 2 and cur_h + 1 < nheads:
                load(cur_h + 1)
            if since in (DEPTH + 6, DEPTH + 26, DEPTH + 46, DEPTH + 66):
                take(pieces, 1)
            pend.append(emit(it))
            if len(pend) > DEPTH:
                pend.popleft()()
        while pend:
            pend.popleft()()
        kb.barrier()
        kb.flush()


def mla_loader(scr):
    def f(kb, h, QTt, KTt, qs, ks):
        q_t, q_b = QTt
        k_t, k_b = KTt
        kb.dma(q_t[0:64, :], scr['QN'][h * 64:(h + 1) * 64, :], qs, writes=[q_b])
        kb.dma(q_t[64:96, :], scr['QR'][h * 32:(h + 1) * 32, :], qs, writes=[q_b], add=True)
        kb.dma(k_t[0:64, :], scr['KN'][h * 64:(h + 1) * 64, :], ks, writes=[k_b])
        kb.dma(k_t[64:96, :], scr['KPE'][:, :], ks, writes=[k_b], add=True)
    return f


def fox_loader(scr):
    def f(kb, h, QTt, KTt, qs, ks):
        q_t, q_b = QTt
        k_t, k_b = KTt
        kb.dma(q_t[0:64, :], scr['QN'][h * 64:(h + 1) * 64, :], qs, writes=[q_b])
        kb.dma(q_t[64:70, :], scr['QAUG'][:, h, :], qs, writes=[q_b], add=True)
        kb.dma(k_t[0:64, :], scr['KN'][h * 64:(h + 1) * 64, :], ks, writes=[k_b])
        kb.dma(k_t[64:70, :], scr['KAUG'][:, h, :], ks, writes=[k_b], add=True)
    return f


def take(pieces, n):
    if pieces is None:
        return
    for _ in range(n):
        p = next(pieces, None)
        if p is None:
            return
        p()


def phase_oproj(kb, H_in, H_out, wo_d, scr, ntiles=8, stg=None, pieces=None):
    kb.nb = 8
    T = 512
    with Ctx(kb) as cx:
        wo, wo_b = cx.sb([128, 8, D], BF16)
        if stg is None:
            stg = make_stg(kb, cx)
        load_cast_weight(kb, cx, stg, wo_d, 0, 8, D, wo, wo_b)
        hx = [cx.sb([128, 4, D], F32) for _ in range(2)]
        ot = [cx.sb([128, 8, T], BF16) for _ in range(2)]

        def load(t):
            s = t % 2
            kb.dma(hx[s][0][:, :, :], H_in[t * T:(t + 1) * T, :].rearrange("(j p) f -> p j f", p=128),
                   kb.ld_sems[s], writes=[hx[s][1]])
            kb.dma(ot[s][0][:, :, :], scr['OT'].rearrange("(c p) t -> p c t", p=128)[:, :, t * T:(t + 1) * T],
                   kb.ld_sems[2 + s], writes=[ot[s][1]])

        load(0)
        for t in range(ntiles):
            s = t % 2
            h_t, h_b = hx[s]
            o_t, o_b = ot[s]
            if t + 1 < ntiles:
                load(t + 1)
            take(pieces, 8)
            for j in range(4):
                for hf in range(2):
                    bk, bk_b = kb.next_bank()
                    for c in range(8):
                        kb.mm(bk[:, :], o_t[:, c, j * 128:(j + 1) * 128], wo[:, c, hf * 512:(hf + 1) * 512],
                              c == 0, c == 7, [o_b, wo_b], [bk_b])
                    kb.tt(kb.dve, h_t[:, j, hf * 512:(hf + 1) * 512], bk[:, :], h_t[:, j, hf * 512:(hf + 1) * 512],
                          ALU.add, [bk_b, h_b], [h_b])
            kb.dma(H_out[t * T:(t + 1) * T, :].rearrange("(j p) f -> p j f", p=128), h_t[:, :, :], kb.st_sems[s],
                   reads=[h_b])
        kb.barrier()
        kb.flush()


LAYER_KINDS = ["mla", "fox", "dil", "mla"]
WSHAPES = {
    "mla": [("attn_norm", [D]), ("mla_wq_a", [D, 384]), ("mla_q_norm", [384]), ("mla_wq_b", [384, 1536]),
            ("mla_wkv_a", [D, 288]), ("mla_kv_norm", [256]), ("mla_wkv_b", [256, 2048]), ("mla_wo", [D, D])],
    "fox": [("attn_norm", [D]), ("fox_w_qkv", [D, 3072]), ("fox_w_f", [D, 16]), ("fox_b_f", [16]),
            ("fox_wo", [D, D])],
    "dil": [("attn_norm", [D]), ("dil_w_qkv", [D, 9216]), ("dil_wo", [D, D])],
}
MLP_W = [("mlp_norm", [D]), ("w_up", [D, DFF]), ("w_down", [DFF, D])]
CBF_W = 128 + 128 + 256 + 256


def build_program(nlayers=4, stop_after=None):
    nc = bass.Bass("TRN2", target_bir_lowering=False)
    inp = {}

    def din(name, shape, dt=F32):
        inp[name] = nc.dram_tensor(name, shape, dt, kind="ExternalInput").ap()
        return inp[name]

    x = din("x", [S, D])
    for i in range(nlayers):
        for n, shp in WSHAPES[LAYER_KINDS[i]] + MLP_W:
            din("l%d_%s" % (i, n), shp)
    din("final_norm", [D])
    cbf = din("cbf", [128, CBF_W], BF16)
    rope_mla = din("rope_mla", [S, 64])
    rope_dil = din("rope_dil", [3, 2, 32, S])
    cf32 = din("cf32", [128, 128])
    y = nc.dram_tensor("y", [S, D], F32, kind="ExternalOutput").ap()

    def scratch(name, shape, dt):
        return nc.dram_tensor(name, shape, dt, kind="Internal").ap()

    scr = {
        "cbf": cbf, "cf32": cf32,
        "H": scratch("H", [S, D], F32),
        "QN": scratch("QN", [1024, S], BF16), "QR": scratch("QR", [512, S], BF16),
        "KN": scratch("KN", [1024, S], BF16), "KPE": scratch("KPE", [32, S], BF16),
        "V": scratch("V", [S, 1024], BF16), "OT": scratch("OT", [1024, S], BF16),
        "QAUG": scratch("QAUG", [6, 16, S], BF16), "KAUG": scratch("KAUG", [6, 16, S], BF16),
        "LSP": scratch("LSP", [16, S], F32),
        "QG": scratch("QG", [3, 4, 2, 128, S], BF16), "KG": scratch("KG", [3, 4, 2, 128, S], BF16),
        "VG": scratch("VG", [3, S, 1024], BF16), "OG": scratch("OG", [3, S, 16 * 65], F32),
    }
    with ExitStack() as es:
        kb = setup_kb(nc, es, cbf)
        kb.at_sems = [kb.dsem("at%d" % i) for i in range(6)]
        H = scr["H"]
        for i in range(nlayers):
            kind = LAYER_KINDS[i]
            W = {n: inp["l%d_%s" % (i, n)] for n, _ in WSHAPES[kind] + MLP_W}
            h_in = x if i == 0 else H
            last = (i == nlayers - 1)
            if kind == "mla":
                Wm = {"attn_norm": W["attn_norm"], "wq_a": W["mla_wq_a"], "q_norm": W["mla_q_norm"],
                      "wq_b": W["mla_wq_b"], "wkv_a": W["mla_wkv_a"], "kv_norm": W["mla_kv_norm"],
                      "wkv_b": W["mla_wkv_b"]}
                phase_mla_proj(kb, h_in, Wm, scr, rope_mla)
            elif kind == "fox":
                phase_fox_proj(kb, h_in, W, scr)
            else:
                phase_dil_proj(kb, h_in, W, scr, rope_dil)
            with Ctx(kb) as wcx:
                wu, wu_b = wcx.sb([128, 8, DFF], BF16)
                wd, wd_b = wcx.sb([128, 32, D], BF16)
                stg = make_stg(kb, wcx)
                g = load_gain(kb, wcx, W["mlp_norm"], 8)
                pieces = mlp_weight_pieces(kb, stg, W["w_up"], W["w_down"], g, wu, wu_b, wd, wd_b)
                kb.cast_engs = (kb.pool,)
                kb.piece_q = kb.pool
                if kind == "mla":
                    phase_attn(kb, scr, 96, 96 ** -0.5, mla_loader(scr), pieces=pieces)
                elif kind == "fox":
                    phase_attn(kb, scr, 70, 0.125, fox_loader(scr), pieces=pieces)
                else:
                    phase_dil_attn(kb, scr)
                kb.cast_engs = (kb.dve, kb.act)
                kb.piece_q = None
                if kind == "dil":
                    kb.piece_q = kb.pool
                    phase_dil_oproj(kb, h_in, H, W["dil_wo"], scr, stg=stg, pieces=pieces)
                    kb.piece_q = None
                else:
                    phase_oproj(kb, h_in, H, W[kind + "_wo"], scr, stg=stg, pieces=pieces)
                if stop_after == (i, "attn"):
                    break
                phase_mlp(kb, H, y if last else H, W["w_up"], W["w_down"], W["mlp_norm"],
                          final=(inp["final_norm"], y) if last else None, pre=(wu, wu_b, wd, wd_b, pieces))
        if stop_after is not None:
            copy_out(kb, H, y)
    return nc


def copy_out(kb, H, y):
    with Ctx(kb) as cx:
        t, b = cx.sb([128, 8, D], F32)
        for i in range(4):
            kb.dma(t[:, :, :], H[i * 1024:(i + 1) * 1024, :].rearrange("(j p) f -> p j f", p=128), kb.ld_sems[0],
                   writes=[b])
            kb.dma(y[i * 1024:(i + 1) * 1024, :].rearrange("(j p) f -> p j f", p=128), t[:, :, :], kb.st_sems[0],
                   reads=[b])
        kb.barrier()
        kb.flush()


def dil_perm(g):
    d = (1, 4, 16)[g]
    n = np.arange(S)
    L = S // d
    return (n // L) + d * (n % L)


def host_constants():
    bf = ml_dtypes.bfloat16
    k = np.arange(128)[:, None]
    q = np.arange(128)[None, :]
    NEG = -30000.0
    ident = np.eye(128, dtype=np.float32)
    causal = np.where(k <= q, 0.0, NEG).astype(np.float32)
    prev = np.where(k >= q, 0.0, NEG).astype(np.float32)
    prev01 = (k >= q).astype(np.float32)
    diag01 = (k <= q).astype(np.float32)
    cbf = np.concatenate([ident, causal, prev, causal, prev01, diag01], axis=1).astype(bf)
    t = np.arange(S, dtype=np.float32)[:, None]
    inv = (1.0 / (np.float32(10000.0) ** (np.arange(0, 32, 2, dtype=np.float32) / np.float32(32)))).astype(np.float32)
    ang = (t * inv[None, :]).astype(np.float32)
    c, s = np.cos(ang).astype(np.float32), np.sin(ang).astype(np.float32)
    rope_mla = np.concatenate([c, c, s, s], axis=1).astype(np.float32)
    inv2 = (1.0 / (np.float32(10000.0) ** (np.arange(0, 64, 2, dtype=np.float32) / np.float32(64)))).astype(np.float32)
    ang2 = (t * inv2[None, :]).astype(np.float32)
    c2, s2 = np.cos(ang2).astype(np.float32), np.sin(ang2).astype(np.float32)
    rope_dil = np.zeros((3, 2, 32, S), np.float32)
    for g in range(3):
        p = dil_perm(g)
        rope_dil[g, 0] = c2[p].T
        rope_dil[g, 1] = s2[p].T
    kk = np.arange(128)[:, None]
    mm_ = np.arange(128)[None, :]
    segtri = ((kk // 8 == mm_ // 8) & (kk % 8 < mm_ % 8)).astype(np.float32)
    return {"cbf": cbf, "rope_mla": rope_mla, "rope_dil": rope_dil, "cf32": segtri}


def host_weights(inputs, nlayers=4):
    out = {}
    for i in range(nlayers):
        kind = LAYER_KINDS[i]
        for n, _ in WSHAPES[kind] + MLP_W:
            key = "l%d_%s" % (i, n)
            w = np.asarray(inputs[key], dtype=np.float32)
            if n == "mla_wq_b":
                w3 = w.reshape(384, 16, 96)
                w = np.concatenate([w3[:, :, :64].reshape(384, 1024), w3[:, :, 64:].reshape(384, 512)], axis=1)
            elif n == "mla_wkv_b":
                w3 = w.reshape(256, 16, 128)
                w = np.concatenate([w3[:, :, :64].reshape(256, 1024), w3[:, :, 64:].reshape(256, 1024)], axis=1)
            elif n == "dil_w_qkv":
                w6 = w.reshape(D, 3, 3, 4, 4, 2, 32)
                qk = w6[:, 0:2].transpose(0, 1, 2, 3, 5, 4, 6)
                w = np.concatenate([qk.reshape(D, 2 * 3072), w6[:, 2].reshape(D, 3072)], axis=1)
            out[key] = np.ascontiguousarray(w)
    out["final_norm"] = np.asarray(inputs["final_norm"], dtype=np.float32)
    return out


INPUT_NAMES = (
    "x", "l0_attn_norm", "l0_mla_wq_a", "l0_mla_q_norm",
    "l0_mla_wq_b", "l0_mla_wkv_a", "l0_mla_kv_norm", "l0_mla_wkv_b",
    "l0_mla_wo", "l0_mlp_norm", "l0_w_up", "l0_w_down",
    "l1_attn_norm", "l1_fox_w_qkv", "l1_fox_w_f", "l1_fox_b_f",
    "l1_fox_wo", "l1_mlp_norm", "l1_w_up", "l1_w_down",
    "l2_attn_norm", "l2_dil_w_qkv", "l2_dil_wo", "l2_mlp_norm",
    "l2_w_up", "l2_w_down", "l3_attn_norm", "l3_mla_wq_a",
    "l3_mla_q_norm", "l3_mla_wq_b", "l3_mla_wkv_a", "l3_mla_kv_norm",
    "l3_mla_wkv_b", "l3_mla_wo", "l3_mlp_norm", "l3_w_up",
    "l3_w_down", "final_norm",
)


_PROG = {}


def kernel(**inputs):
    missing = [n for n in INPUT_NAMES if n not in inputs]
    assert not missing, missing
    if "full" not in _PROG:
        _PROG["full"] = build_program()
    nc = _PROG["full"]
    consts = host_constants()
    wts = host_weights(inputs)
    x = np.asarray(inputs["x"], dtype=np.float32)
    in_maps = []
    for b in range(NCORES):
        m = dict(wts)
        m.update(consts)
        m["x"] = np.ascontiguousarray(x[b])
        in_maps.append(m)
    res = run_bass_kernel_spmd(nc, in_maps, core_ids=list(range(NCORES)))
    return np.stack([np.asarray(r["y"], dtype=np.float32) for r in res.results], axis=0)


def phase_fox_proj(kb, H_in, W, scr, ntiles=8):
    kb.nb = 8
    T = 512
    with Ctx(kb) as cx:
        wqkv, wqkv_b = cx.sb([128, 8, 3072], BF16)
        wf, wf_b = cx.sb([128, 8, 16], BF16)
        stg = make_stg(kb, cx, 6)
        g_attn = load_gain(kb, cx, W['attn_norm'], 8)
        negb, negb_b = cx.sb([16, 1], F32)
        kb.dma(negb[:, :], W['fox_b_f'].rearrange("(p o) -> p o", o=1), kb.misc_sem, writes=[negb_b],
               allow_slow_non_contiguous=True)
        kb.ts(kb.dve, negb[:, :], negb[:, :], -1.0, None, ALU.mult, None, [negb_b], [negb_b])
        hx = [cx.sb([128, 4, D], F32) for _ in range(2)]
        a, a_b = cx.sb([128, 4, D], BF16)
        aTs = [cx.sb([128, 8, T], BF16) for _ in range(2)]
        ss, ss_b = cx.sb([128, 8], F32)
        vsb, vsb_b = cx.sb([128, 4, D], BF16)
        ef, ef_b = cx.sb([16, T], F32)
        lsp = [cx.sb([16, T], F32) for _ in range(2)]
        ost = OutStage(kb, cx, 4)
        ek = 0

        def load(t):
            s = t % 2
            kb.dma(hx[s][0][:, :, :], H_in[t * T:(t + 1) * T, :].rearrange("(j p) f -> p j f", p=128),
                   kb.ld_sems[s], writes=[hx[s][1]])

        load(0)
        if ntiles > 1:
            load(1)
        load_cast_weight(kb, cx, stg, W['fox_w_qkv'], 0, 8, 3072, wqkv, wqkv_b, gain=g_attn)
        load_cast_weight(kb, cx, stg, W['fox_w_f'], 0, 8, 16, wf, wf_b, gain=g_attn)
        norm_stats_scale(kb, hx[0][0], hx[0][1], 4, a, a_b, ss, ss_b, 0)
        transpose_evac(kb, 4, a, a_b, aTs[0][0], aTs[0][1])
        for t in range(ntiles):
            s = t % 2
            aT, aT_b = aTs[s]
            if t + 1 < ntiles:
                norm_stats_scale(kb, hx[1 - s][0], hx[1 - s][1], 4, a, a_b, ss, ss_b, 4 * (1 - s))
            if t + 2 < ntiles:
                load(t + 2)
            tcol = slice(t * T, (t + 1) * T)
            for which, dst in ((0, 'QN'), (1, 'KN')):
                for ft in range(8):
                    bk, bk_b = kb.next_bank()
                    c0 = which * 1024 + ft * 128
                    for c in range(8):
                        kb.mm(bk[:, :], wqkv[:, c, c0:c0 + 128], aT[:, c, :], c == 0, c == 7, [wqkv_b, aT_b], [bk_b])
                    o_t, o_b, o_s = ost.next()
                    evac(kb, ek, o_t[:, :], bk[:, :], [bk_b], [o_b]); ek += 1
                    kb.dma(scr[dst][ft * 128:(ft + 1) * 128, tcol], o_t[:, :], o_s, reads=[o_b])
            for j in range(4):
                for hf in range(2):
                    bk, bk_b = kb.next_bank()
                    for c in range(8):
                        kb.mm(bk[:, :], aT[:, c, j * 128:(j + 1) * 128], wqkv[:, c, 2048 + hf * 512:2048 + (hf + 1) * 512],
                              c == 0, c == 7, [wqkv_b, aT_b], [bk_b])
                    evac(kb, ek, vsb[:, j, hf * 512:(hf + 1) * 512], bk[:, :], [bk_b], [vsb_b], add=(j + hf > 0)); ek += 1
            kb.dma(scr['V'][tcol, :].rearrange("(j p) f -> p j f", p=128), vsb[:, :, :], kb.st_sems[4], reads=[vsb_b])
            bk, bk_b = kb.next_bank()
            for c in range(8):
                kb.mm(bk[0:16, :], wf[:, c, :], aT[:, c, :], c == 0, c == 7, [wf_b, aT_b], [bk_b])
            kb.actf(ef[:, :], bk[0:16, :], AF.Exp, [bk_b, negb_b], [ef_b], scale=-1.0, bias=negb[:, 0:1])
            l_t, l_b = lsp[s]
            kb.actf(l_t[:, :], ef[:, :], AF.Ln, [ef_b], [l_b], bias=1.0)
            kb.dma(scr['LSP'][:, tcol], l_t[:, :], kb.st_sems[5 + s], reads=[l_b])
            if t + 1 < ntiles:
                transpose_evac(kb, 4, a, a_b, aTs[1 - s][0], aTs[1 - s][1])
        kb.barrier()
        kb.flush()
    with Ctx(kb) as cx:
        def seg(ap2d):
            return ap2d.rearrange("h (s t) -> (h s) t", s=8)
        L, L_b = cx.sb([128, 512], F32)
        R, R_b = cx.sb([128, 512], F32)
        t2, t2_b = cx.sb([128, 512], F32)
        tri, tri_b = cx.sb([128, 128], F32)
        off, off_b = cx.sb([128, 1], F32)
        P = [cx.sb([128, 512], BF16) for _ in range(3)]
        Nn = [cx.sb([128, 512], BF16) for _ in range(3)]
        ones, ones_b = cx.sb([128, 512], BF16)
        kb.dma(L[:, :], seg(scr['LSP'][:, :]), kb.ld_sems[0], writes=[L_b])
        kb.dma(tri[:, :], scr['cf32'][:, :], kb.ld_sems[1], writes=[tri_b])
        kb.op(kb.pool, lambda h: h.memset(ones[:, :], 1.0), [], [ones_b])
        kb.op(kb.dve, lambda h: h.tensor_tensor_scan(out=R[:, :], data0=L[:, :], data1=L[:, :], initial=0.0,
                                                     op0=ALU.add, op1=ALU.add), [L_b], [R_b])
        bk, bk_b = kb.next_bank()
        kb.mm(bk[:, 0:1], tri[:, :], R[:, 511:512], True, True, [tri_b, R_b], [bk_b])
        kb.cp(kb.dve, off[:, :], bk[:, 0:1], [bk_b], [off_b])
        kb.ts(kb.dve, R[:, :], R[:, :], off[:, 0:1], -4.0, ALU.add, ALU.mult, [R_b, off_b], [R_b])
        for k in range(3):
            p_t, p_b = P[k]
            n_t, n_b = Nn[k]
            kb.cp(kb.dve, p_t[:, :], R[:, :], [R_b], [p_b])
            kb.dma(seg(scr['QAUG'][k, :, :]), p_t[:, :], kb.st_sems[k], reads=[p_b])
            kb.ts(kb.pool, n_t[:, :], p_t[:, :], -1.0, None, ALU.mult, None, [p_b], [n_b])
            kb.dma(seg(scr['KAUG'][3 + k, :, :]), n_t[:, :], kb.st_sems[3 + k], reads=[n_b])
            kb.dma(seg(scr['QAUG'][3 + k, :, :]), ones[:, :], kb.st_sems[6], reads=[ones_b])
            kb.dma(seg(scr['KAUG'][k, :, :]), ones[:, :], kb.st_sems[7], reads=[ones_b])
            if k < 2:
                kb.cp(kb.dve, t2[:, :], p_t[:, :], [p_b], [t2_b])
                kb.tt(kb.dve, R[:, :], R[:, :], t2[:, :], ALU.subtract, [R_b, t2_b], [R_b])
        kb.barrier()
        kb.flush()


DIL_D = (1, 4, 16)


def phase_dil_proj(kb, H_in, W, scr, rope_dil, ntiles=8):
    kb.nb = 8
    T = 512
    with Ctx(kb) as cx:
        wsets = [[cx.sb([128, 8, 1024], BF16) for _ in range(3)] for _ in range(2)]
        stg = make_stg(kb, cx, 4)
        g_attn = load_gain(kb, cx, W['attn_norm'], 8)
        wall = W['dil_w_qkv']
        hx = [cx.sb([128, 4, D], F32) for _ in range(2)]
        ct = [cx.sb([128, T], F32) for _ in range(2)]
        st_ = [cx.sb([128, T], F32) for _ in range(2)]
        a, a_b = cx.sb([128, 4, D], BF16)
        aTs = [cx.sb([128, 8, T], BF16) for _ in range(2)]
        ss, ss_b = cx.sb([128, 8], F32)
        vsb, vsb_b = cx.sb([128, 4, D], BF16)
        m8 = [cx.sb([128, T], F32) for _ in range(8)]
        mi = [0]
        ost = OutStage(kb, cx, 4)

        def weight_pieces(g):
            ws = wsets[g % 2]
            for which in range(3):
                w_t, w_b = ws[which]
                col0 = which * 3072 + g * 1024
                for c in range(8):
                    yield lambda c=c, w_t=w_t, w_b=w_b, col0=col0: load_cast_piece(
                        kb, stg, wall[c * 128:(c + 1) * 128, col0:col0 + 1024], w_t[:, c, :], w_b,
                        (g_attn[0][:, c:c + 1], g_attn[1]))

        seq = [(g, t) for g in range(3) for t in range(ntiles)]

        def load(i):
            g, t = seq[i]
            d = DIL_D[g]
            L = S // d
            Hg = H_in.rearrange("(i r) f -> r i f", r=d)
            s = i % 2
            for j in range(4):
                n = t * T + j * 128
                kb.dma(hx[s][0][:, j, :], Hg[n // L, (n % L):(n % L) + 128, :], kb.ld_sems[s],
                       writes=[hx[s][1]], add=(j > 0))
            for rep in range(4):
                kb.dma(ct[s][0][rep * 32:(rep + 1) * 32, :], rope_dil[g, 0, :, t * T:(t + 1) * T], kb.ld_sems[2 + s],
                       writes=[ct[s][1]], add=(rep > 0))
                kb.dma(st_[s][0][rep * 32:(rep + 1) * 32, :], rope_dil[g, 1, :, t * T:(t + 1) * T], kb.ld_sems[2 + s],
                       writes=[st_[s][1]], add=True)

        load(0)
        if len(seq) > 1:
            load(1)
        for p in weight_pieces(0):
            p()
        norm_stats_scale(kb, hx[0][0], hx[0][1], 4, a, a_b, ss, ss_b, 0)
        transpose_evac(kb, 4, a, a_b, aTs[0][0], aTs[0][1])
        nxt = None
        for i, (g, t) in enumerate(seq):
            s = i % 2
            if t == 0:
                nxt = weight_pieces(g + 1) if g < 2 else None
            (wq, wq_b), (wk, wk_b), (wv, wv_b) = wsets[g % 2]
            aT, aT_b = aTs[s]
            C, C_b = ct[s]
            Sn, Sn_b = st_[s]
            if i + 1 < len(seq):
                norm_stats_scale(kb, hx[1 - s][0], hx[1 - s][1], 4, a, a_b, ss, ss_b, 4 * (1 - s))
            take(nxt, 3)
            tcol = slice(t * T, (t + 1) * T)
            for w_t, w_b, dst in ((wq, wq_b, 'QG'), (wk, wk_b, 'KG')):
                for pr in range(4):
                    bA, bA_b = kb.next_bank()
                    for c in range(8):
                        kb.mm(bA[:, :], w_t[:, c, pr * 256:pr * 256 + 128], aT[:, c, :], c == 0, c == 7,
                              [w_b, aT_b], [bA_b])
                    bB, bB_b = kb.next_bank()
                    for c in range(8):
                        kb.mm(bB[:, :], w_t[:, c, pr * 256 + 128:pr * 256 + 256], aT[:, c, :], c == 0, c == 7,
                              [w_b, aT_b], [bB_b])
                    m = m8[4 * (mi[0] % 2):4 * (mi[0] % 2) + 4]
                    mi[0] += 1
                    kb.tt(kb.dve, m[0][0][:, :], bA[:, :], C[:, :], ALU.mult, [bA_b, C_b], [m[0][1]])
                    kb.tt(kb.dve, m[1][0][:, :], bB[:, :], Sn[:, :], ALU.mult, [bB_b, Sn_b], [m[1][1]])
                    kb.tt(kb.dve, m[2][0][:, :], bA[:, :], Sn[:, :], ALU.mult, [bA_b, Sn_b], [m[2][1]])
                    kb.tt(kb.dve, m[3][0][:, :], bB[:, :], C[:, :], ALU.mult, [bB_b, C_b], [m[3][1]])
                    o_t, o_b, o_s = ost.next()
                    kb.tt(kb.pool, o_t[:, :], m[0][0][:, :], m[1][0][:, :], ALU.subtract, [m[0][1], m[1][1]], [o_b])
                    kb.dma(scr[dst][g, pr, 0, :, tcol], o_t[:, :], o_s, reads=[o_b])
                    o_t, o_b, o_s = ost.next()
                    kb.tt(kb.pool, o_t[:, :], m[2][0][:, :], m[3][0][:, :], ALU.add, [m[2][1], m[3][1]], [o_b])
                    kb.dma(scr[dst][g, pr, 1, :, tcol], o_t[:, :], o_s, reads=[o_b])
            for j in range(4):
                for hf in range(2):
                    bk, bk_b = kb.next_bank()
                    for c in range(8):
                        kb.mm(bk[:, :], aT[:, c, j * 128:(j + 1) * 128], wv[:, c, hf * 512:(hf + 1) * 512],
                              c == 0, c == 7, [wv_b, aT_b], [bk_b])
                    kb.actf(vsb[:, j, hf * 512:(hf + 1) * 512], bk[:, :], AF.Copy, [bk_b], [vsb_b], add=(j + hf > 0))
            kb.dma(scr['VG'][g, tcol, :].rearrange("(j p) f -> p j f", p=128), vsb[:, :, :], kb.st_sems[4],
                   reads=[vsb_b])
            if i + 1 < len(seq):
                transpose_evac(kb, 4, a, a_b, aTs[1 - s][0], aTs[1 - s][1])
            if i + 2 < len(seq):
                load(i + 2)
            if t == ntiles - 1 and nxt is not None:
                for p in nxt:
                    p()
        kb.barrier()
        kb.flush()


def phase_dil_attn(kb, scr, nidx=48, DEPTH=3, pieces=None):
    from collections import deque
    kb.nb = 6
    with Ctx(kb) as cx:
        QT = [cx.sb([64, S], BF16) for _ in range(2)]
        KT = [cx.sb([64, S], BF16) for _ in range(2)]
        VA = [cx.sb([128, 32, 66], BF16) for _ in range(2)]
        pT = [cx.sb([128, 256], BF16) for _ in range(6)]
        UG = [cx.sb([128, 32, 65], F32) for _ in range(2)]
        mask, mask_b = cx.sb([128, 256], BF16)
        kb.dma(mask[:, :], scr['cbf'][:, 512:768], kb.misc_sem, writes=[mask_b])
        for i in range(2):
            kb.op(kb.pool, lambda h, i=i: h.memset(VA[i][0][:, :, 64:65], 1.0), [], [VA[i][1]])

        def load(idx):
            s = idx % 2
            g, h = idx // 16, idx % 16
            hb, hh = h // 4, h % 4
            rows = slice(hh * 32, (hh + 1) * 32)
            q_t, q_b = QT[s]
            k_t, k_b = KT[s]
            kb.dma(q_t[0:32, :], scr['QG'][g, hb, 0, rows, :], kb.at_sems[s], writes=[q_b])
            kb.dma(q_t[32:64, :], scr['QG'][g, hb, 1, rows, :], kb.at_sems[s], writes=[q_b], add=True)
            kb.dma(k_t[0:32, :], scr['KG'][g, hb, 0, rows, :], kb.at_sems[2 + s], writes=[k_b])
            kb.dma(k_t[32:64, :], scr['KG'][g, hb, 1, rows, :], kb.at_sems[2 + s], writes=[k_b], add=True)
            kb.dma(VA[s][0][:, :, 0:64], scr['VG'][g, :, h * 64:(h + 1) * 64].rearrange("(c p) d -> p c d", p=128),
                   kb.at_sems[4 + s], writes=[VA[s][1]], add=True)

        st = {"pi": 0, "ek": 0, "obi": 0}

        def emit(idx, B):
            s = idx % 2
            g, h = idx // 16, idx % 16
            d = DIL_D[g]
            Lb = S // d // 128
            q_t, q_b = QT[s]
            k_t, k_b = KT[s]
            v_t, v_b = VA[s]
            u_t, u_b = UG[s]
            j = B % Lb
            cur = slice(B * 128, (B + 1) * 128)
            prv = slice((B - 1) * 128, B * 128)
            sb_, sb_b = kb.next_bank()
            p_t, p_b = pT[st["pi"] % 6]
            st["pi"] += 1
            c0 = 0 if j > 0 else 128
            if j > 0:
                kb.mm(sb_[:, 0:128], k_t[:, prv], q_t[:, cur], True, True, [k_b, q_b], [sb_b])
            kb.mm(sb_[:, 128:256], k_t[:, cur], q_t[:, cur], True, True, [k_b, q_b], [sb_b])
            kb.actf(p_t[:, c0:256], sb_[:, c0:256], AF.Exp, [sb_b], [p_b], scale=0.125)
            kb.tt(kb.dve, p_t[:, c0:256], p_t[:, c0:256], mask[:, c0:256], ALU.mult, [p_b, mask_b], [p_b])
            B0 = (B // 7) * 7
            nb = min(7, 32 - B0)
            obi = (B // 7) % 2

            def pv():
                ob, ob_b = kb.banks[6 + obi], kb.bank_bufs[6 + obi]
                reg = ob[:, (B - B0) * 65:(B - B0 + 1) * 65]
                if j > 0:
                    kb.mm(reg, p_t[:, 0:128], v_t[:, B - 1, 0:65], True, False, [p_b, v_b], [ob_b])
                    kb.mm(reg, p_t[:, 128:256], v_t[:, B, 0:65], False, True, [p_b, v_b], [ob_b])
                else:
                    kb.mm(reg, p_t[:, 128:256], v_t[:, B, 0:65], True, True, [p_b, v_b], [ob_b])
                if B - B0 == nb - 1:
                    evac(kb, st["ek"], u_t[:, B0:B0 + nb, :], ob[:, 0:nb * 65].rearrange("p (b e) -> p b e", e=65),
                         [ob_b], [u_b], add=(B0 > 0))
                    st["ek"] += 1
                if B == 31:
                    OGv = scr['OG'][g].rearrange("(i r) (h e) -> r i h e", r=d, e=65)
                    for r in range(d):
                        kb.dma(OGv[r, :, h, :].rearrange("(j p) e -> p j e", p=128), u_t[:, r * Lb:(r + 1) * Lb, :],
                               kb.st_sems[s], reads=[u_b], add=(r > 0))
            return pv

        load(0)
        pend = deque()
        for idx in range(nidx):
            for B in range(32):
                if B == DEPTH + 1 and idx + 1 < nidx:
                    load(idx + 1)
                if B in (10, 24):
                    take(pieces, 1)
                pend.append(emit(idx, B))
                if len(pend) > DEPTH:
                    pend.popleft()()
        while pend:
            pend.popleft()()
        kb.barrier()
        kb.flush()


def phase_dil_oproj(kb, H_in, H_out, wo_d, scr, ntiles=32, stg=None, pieces=None):
    kb.nb = 8
    T = 128
    J = T // 128
    with Ctx(kb) as cx:
        wo, wo_b = cx.sb([128, 8, D], BF16)
        if stg is None:
            stg = make_stg(kb, cx)
        load_cast_weight(kb, cx, stg, wo_d, 0, 8, D, wo, wo_b)
        hx = [cx.sb([128, J, D], F32) for _ in range(2)]
        og = [cx.sb([128, 3, 1040], F32) for _ in range(2)]
        rden, rden_b = cx.sb([128, 2, 16], F32)
        ob, ob_b = cx.sb([128, J, D], BF16)
        oTs = [cx.sb([128, 8, T], BF16) for _ in range(2)]
        nsub = ntiles * J

        def load_h(t):
            s = t % 2
            kb.dma(hx[s][0][:, :, :], H_in[t * T:(t + 1) * T, :].rearrange("(j p) f -> p j f", p=128),
                   kb.ld_sems[s], writes=[hx[s][1]])

        def load_og(k):
            s = k % 2
            for g in range(3):
                kb.dma(og[s][0][:, g, :], scr['OG'][g, k * 128:(k + 1) * 128, :], kb.ld_sems[2 + s], writes=[og[s][1]],
                       add=(g > 0))

        def prep1(t):
            for j in range(J):
                k = t * J + j
                g_t, g_b = og[k % 2]
                kb.tt(kb.dve, g_t[:, 0, :], g_t[:, 0, :], g_t[:, 1, :], ALU.add, [g_b], [g_b])
                kb.tt(kb.dve, g_t[:, 0, :], g_t[:, 0, :], g_t[:, 2, :], ALU.add, [g_b], [g_b])
                g3 = g_t[:, 0, :].rearrange("p (h e) -> p h e", e=65)
                kb.op(kb.dve, lambda h_, g3=g3, j=j: h_.reciprocal(out=rden[:, j, :], in_=g3[:, :, 64]), [g_b], [rden_b],
                      add=(j > 0))
                kb.tt(kb.dve, ob[:, j, :].rearrange("p (h e) -> p h e", e=64), g3[:, :, 0:64],
                      rden[:, j, :].unsqueeze(2).broadcast_to([128, 16, 64]), ALU.mult, [g_b, rden_b], [ob_b],
                      add=(j > 0))
                if k + 2 < nsub:
                    load_og(k + 2)

        def prep2(t):
            oT, oT_b = oTs[t % 2]
            for j in range(J):
                bk, bk_b = kb.next_bank()
                bkv = bk[:, :].bitcast(BF16)
                for c in range(8):
                    kb.tr(bkv[:, c * 128:(c + 1) * 128], ob[:, j, c * 128:(c + 1) * 128], kb.ident[:, :],
                          [ob_b, kb.const_b], [bk_b])
                kb.actf(oT[:, :, j * 128:(j + 1) * 128], bkv.rearrange("p (c t) -> p c t", c=8), AF.Copy, [bk_b], [oT_b],
                        add=(j > 0))

        load_h(0)
        load_og(0)
        if nsub > 1:
            load_og(1)
        if ntiles > 1:
            load_h(1)
        prep1(0)
        prep2(0)
        for t in range(ntiles):
            s = t % 2
            h_t, h_b = hx[s]
            oT, oT_b = oTs[s]
            take(pieces, 2)
            if t + 1 < ntiles:
                prep1(t + 1)
            for j in range(J):
                for hf in range(2):
                    bk, bk_b = kb.next_bank()
                    for c in range(8):
                        kb.mm(bk[:, :], oT[:, c, j * 128:(j + 1) * 128], wo[:, c, hf * 512:(hf + 1) * 512],
                              c == 0, c == 7, [oT_b, wo_b], [bk_b])
                    kb.tt(kb.dve, h_t[:, j, hf * 512:(hf + 1) * 512], bk[:, :], h_t[:, j, hf * 512:(hf + 1) * 512],
                          ALU.add, [bk_b, h_b], [h_b])
            if t + 1 < ntiles:
                prep2(t + 1)
            kb.dma(H_out[t * T:(t + 1) * T, :].rearrange("(j p) f -> p j f", p=128), h_t[:, :, :], kb.st_sems[s],
                   reads=[h_b])
            if t + 2 < ntiles:
                load_h(t + 2)
        kb.barrier()
        kb.flush()
```

```python
import numpy as np
from contextlib import ExitStack
import ml_dtypes
import concourse.bass as bass
import concourse.mybir as mybir
from concourse.bass_utils import run_bass_kernel_spmd

F32 = mybir.dt.float32
BF16 = mybir.dt.bfloat16
AF = mybir.ActivationFunctionType
ALU = mybir.AluOpType
AX = mybir.AxisListType

S = 4096
D = 1024
DFF = 4096
EPS = 1e-6
NCORES = 8


class Sem:
    def __init__(self, h, name):
        self.h = h
        self.val = 0
        self.name = name


class Buf:
    __slots__ = ("w", "r", "rp", "excl")

    def __init__(self, excl=False):
        self.excl = excl
        self.w = {}
        self.r = {}
        self.rp = {}


class Eng:
    def __init__(self, name, sem):
        self.name = name
        self.sem = sem
        self.waited = {}
        self.ops = []


class KB:
    def __init__(self, nc, es):
        self.nc = nc
        self.es = es
        self.nsem = 0
        self.pe = Eng("pe", self.new_sem("pe"))
        self.act = Eng("act", self.new_sem("act"))
        self.dve = Eng("dve", self.new_sem("dve"))
        self.pool = Eng("pool", self.new_sem("pool"))
        self.sp = Eng("sp", self.new_sem("sp"))
        self.engs = [self.pe, self.act, self.dve, self.pool, self.sp]
        self.bar = self.new_sem("bar")
        self.dsems = []
        self.banks = []
        self.bank_bufs = []
        self.wide = []
        for i in range(4):
            t = es.enter_context(nc.psum_tensor("psw%d" % i, [128, 1024], F32))
            self.wide.append(t)
            for k in range(2):
                self.banks.append(t[:, k * 512:(k + 1) * 512])
                self.bank_bufs.append(Buf(excl=True))
        self.bank_rr = 0
        self.nb = 6
        self.cast_engs = (self.dve, self.act)
        self.piece_q = None

    def new_sem(self, name):
        h = self.es.enter_context(self.nc.semaphore(name))
        self.nsem += 1
        return Sem(h, name)

    def dsem(self, name):
        s = self.new_sem(name)
        self.dsems.append(s)
        return s

    def next_bank(self, lo=0, hi=None):
        n = (hi or self.nb) - lo
        i = lo + (self.bank_rr % n)
        self.bank_rr += 1
        return self.banks[i], self.bank_bufs[i]

    def _waits(self, eng, deps):
        for sem, val in deps.items():
            if eng.waited.get(sem, 0) >= val:
                continue
            eng.waited[sem] = val
            eng.ops.append(("wait", sem.h, val))

    def _deps(self, eng_sem, reads, writes, add, skip_same_raw):
        deps = {}

        def put(s, v):
            if v > deps.get(s, 0):
                deps[s] = v

        for b in reads:
            for s, v in b.w.items():
                if s is eng_sem and skip_same_raw:
                    continue
                put(s, v)
            if b.excl:
                for s, v in b.r.items():
                    if s is not eng_sem:
                        put(s, v)
        for b in writes:
            if not add:
                for s, v in b.w.items():
                    if s is not eng_sem or not skip_same_raw:
                        put(s, v)
            else:
                for s, v in b.rp.items():
                    if s is not eng_sem or not skip_same_raw:
                        put(s, v)
            for s, v in b.r.items():
                if s is not eng_sem or not skip_same_raw:
                    put(s, v)
        return deps

    def _mark(self, sem, reads, writes, add):
        for b in writes:
            if add:
                b.w[sem] = sem.val
            else:
                b.rp = b.r
                b.r = {}
                b.w = {sem: sem.val}
        for b in reads:
            b.r[sem] = sem.val

    def op(self, eng, fn, reads=(), writes=(), add=False):
        deps = self._deps(eng.sem, reads, writes, add, eng is self.pe)
        self._waits(eng, deps)
        eng.sem.val += 1
        eng.ops.append(("op", fn, eng.sem.h, 1))
        self._mark(eng.sem, reads, writes, add)

    def dma(self, out, in_, sem, reads=(), writes=(), add=False, q=None, **kw):
        q = q or self.sp
        deps = self._deps(sem, reads, writes, add, True)
        if not add and sem.val > 0:
            deps[sem] = sem.val
        self._waits(q, deps)
        sem.val += 16
        q.ops.append(("op", lambda h: h.dma_start(out=out, in_=in_, **kw), sem.h, 16))
        self._mark(sem, reads, writes, add)

    def barrier(self):
        sp = self.sp
        deps = {e.sem: e.sem.val for e in self.engs if e is not sp and e.sem.val > 0}
        for s in self.dsems:
            if s.val > 0:
                deps[s] = s.val
        self._waits(sp, deps)
        self.bar.val += 1
        sp.ops.append(("seminc", self.bar.h, 1))
        for e in self.engs:
            if e is not sp:
                e.ops.append(("wait", self.bar.h, self.bar.val))
            e.waited = {x.sem: x.sem.val for x in self.engs}
            for s in self.dsems:
                e.waited[s] = s.val
            e.waited[self.bar] = self.bar.val

    def flush(self):
        nc = self.nc
        with nc.Block() as block:
            pairs = ((self.sp, block.sync), (self.act, block.scalar), (self.pe, block.tensor),
                     (self.dve, block.vector), (self.pool, block.gpsimd))
            for eng, deco in pairs:
                ops = eng.ops
                eng.ops = []

                def body(h, ops=ops):
                    for o in ops:
                        if o[0] == "wait":
                            h.wait_ge(o[1], o[2])
                        elif o[0] == "op":
                            o[1](h).then_inc(o[2], o[3])
                        else:
                            h.sem_inc(o[1], o[2])

                deco(body)

    def mm(self, out, lhsT, rhs, start, stop, reads, writes):
        self.op(self.pe, lambda h: h.matmul(out, lhsT=lhsT, rhs=rhs, start=start, stop=stop),
                reads, writes)

    def tr(self, out, in_, ident, reads, writes):
        self.op(self.pe, lambda h: h.transpose(out, in_, ident), reads, writes)

    def actf(self, out, in_, func, reads, writes, add=False, **kw):
        self.op(self.act, lambda h: h.activation(out=out, in_=in_, func=func, **kw), reads, writes, add)

    def ts(self, eng, out, in0, s1, s2, op0, op1, reads, writes, add=False):
        if op1 is None:
            self.op(eng, lambda h: h.tensor_scalar(out=out, in0=in0, scalar1=s1, scalar2=None, op0=op0),
                    reads, writes, add)
        else:
            self.op(eng, lambda h: h.tensor_scalar(out=out, in0=in0, scalar1=s1, scalar2=s2, op0=op0, op1=op1),
                    reads, writes, add)

    def tt(self, eng, out, in0, in1, op, reads, writes, add=False):
        self.op(eng, lambda h: h.tensor_tensor(out=out, in0=in0, in1=in1, op=op), reads, writes, add)

    def cp(self, eng, out, in_, reads, writes, add=False):
        self.op(eng, lambda h: h.tensor_copy(out=out, in_=in_), reads, writes, add)


class Ctx:
    N = 0

    def __init__(self, kb):
        self.kb = kb
        self.es = ExitStack()
        self.n = 0

    def __enter__(self):
        self.es.__enter__()
        return self

    def __exit__(self, *a):
        return self.es.__exit__(*a)

    def sb(self, shape, dt):
        Ctx.N += 1
        t = self.es.enter_context(self.kb.nc.sbuf_tensor("t%d" % Ctx.N, shape, dt))
        return t, Buf()


def load_cast_weight(kb, cx, stg, w_dram, rows0, nrows_chunks, ncols, dst, dst_buf, gain=None, rr=[0]):
    for c in range(nrows_chunks):
        for q0 in range(0, ncols, 1024):
            qn = min(1024, ncols - q0)
            st, sbuf, ssem = stg[rr[0] % len(stg)]
            eng = (kb.dve, kb.pool, kb.act)[rr[0] % 3]
            rr[0] += 1
            kb.dma(st[:, 0:qn], w_dram[rows0 + c * 128: rows0 + (c + 1) * 128, q0:q0 + qn], ssem, writes=[sbuf])
            o = dst[:, c, q0:q0 + qn]
            if gain is None:
                if eng is kb.act:
                    kb.actf(o, st[:, 0:qn], AF.Copy, [sbuf], [dst_buf], add=True)
                else:
                    kb.cp(eng, o, st[:, 0:qn], [sbuf], [dst_buf], add=True)
            else:
                g, gbuf = gain
                if eng is kb.act:
                    kb.actf(o, st[:, 0:qn], AF.Copy, [sbuf, gbuf], [dst_buf], add=True, scale=g[:, c:c + 1])
                else:
                    kb.ts(eng, o, st[:, 0:qn], g[:, c:c + 1], None, ALU.mult, None, [sbuf, gbuf], [dst_buf], add=True)


def make_stg(kb, cx, n=3):
    stg = []
    for i in range(n):
        t, b = cx.sb([128, 1024], F32)
        stg.append((t, b, kb.stg_sems[i]))
    return stg


def load_gain(kb, cx, g_dram, nchunks):
    g, gb = cx.sb([128, nchunks], F32)
    kb.dma(g[:, :], g_dram.rearrange("(c p) -> p c", p=128), kb.misc_sem, writes=[gb],
           allow_slow_non_contiguous=True)
    return g, gb


def rms_rstd(kb, x_ap, n, junk_ap, junk_b, ss, ss_b, col, xb, add=False):
    kb.actf(junk_ap, x_ap, AF.Square, [xb], [junk_b, ss_b], accum_out=ss[:, col:col + 1], add=add)


def rstd_finish(kb, ss, ss_b, c0, c1, n):
    kb.actf(ss[:, c0:c1], ss[:, c0:c1], AF.Sqrt, [ss_b], [ss_b], scale=1.0 / n, bias=EPS)
    kb.op(kb.dve, lambda h: h.reciprocal(out=ss[:, c0:c1], in_=ss[:, c0:c1]), [ss_b], [ss_b])


def mlp_weight_pieces(kb, stg, w_up, w_down, g, wu, wu_b, wd, wd_b):
    for c in range(8):
        for q0 in range(0, DFF, 1024):
            yield lambda c=c, q0=q0: load_cast_piece(kb, stg, w_up[c * 128:(c + 1) * 128, q0:q0 + 1024],
                                                     wu[:, c, q0:q0 + 1024], wu_b, (g[0][:, c:c + 1], g[1]))
    for c in range(32):
        yield lambda c=c: load_cast_piece(kb, stg, w_down[c * 128:(c + 1) * 128, :], wd[:, c, :], wd_b, None)


def load_cast_piece(kb, stg, src, dst, dst_b, gain, rr=[0]):
    st, sbuf, ssem = stg[rr[0] % len(stg)]
    eng = kb.cast_engs[rr[0] % len(kb.cast_engs)]
    rr[0] += 1
    n = dst.shape[-1]
    kb.dma(st[:, 0:n], src, ssem, writes=[sbuf], q=kb.piece_q)
    if gain is None:
        if eng is kb.act:
            kb.actf(dst, st[:, 0:n], AF.Copy, [sbuf], [dst_b], add=True)
        else:
            kb.cp(eng, dst, st[:, 0:n], [sbuf], [dst_b], add=True)
    else:
        g, gbuf = gain
        if eng is kb.act:
            kb.actf(dst, st[:, 0:n], AF.Copy, [sbuf, gbuf], [dst_b], add=True, scale=g)
        else:
            kb.ts(eng, dst, st[:, 0:n], g, None, ALU.mult, None, [sbuf, gbuf], [dst_b], add=True)


def phase_mlp(kb, H_in, H_out, w_up, w_down, g_norm, final=None, ntiles=16, pre=None):
    kb.nb = 8
    T = 256
    with Ctx(kb) as cx:
        if pre is None:
            wu, wu_b = cx.sb([128, 8, DFF], BF16)
            wd, wd_b = cx.sb([128, 32, D], BF16)
            stg = make_stg(kb, cx)
            g = load_gain(kb, cx, g_norm, 8)
            pieces = mlp_weight_pieces(kb, stg, w_up, w_down, g, wu, wu_b, wd, wd_b)
        else:
            wu, wu_b, wd, wd_b, pieces = pre
        for p in pieces:
            p()
        hx = [cx.sb([128, 2, D], F32) for _ in range(3)]
        a2, a2_b = cx.sb([128, 2, D], BF16)
        a2T = [cx.sb([128, 8, T], BF16) for _ in range(2)]
        uT, uT_b = cx.sb([128, 32, T], BF16)
        rl = [cx.sb([128, 512], F32) for _ in range(3)]
        ss, ss_b = cx.sb([128, 8], F32)
        if final is not None:
            gfin, gfin_b = cx.sb([128, D], F32)
            kb.dma(gfin[:, :], final[0].partition_broadcast(128), kb.misc_sem, writes=[gfin_b])

        def load(t):
            s = t % 3
            kb.dma(hx[s][0][:, :, :], H_in[t * T:(t + 1) * T, :].rearrange("(j p) f -> p j f", p=128),
                   kb.ld_sems[s], writes=[hx[s][1]])

        def norm_part(t):
            h_t, h_b = hx[t % 3]
            o = (t % 2) * 2
            for j in range(2):
                rms_rstd(kb, h_t[:, j, :], D, a2[:, j, :], a2_b, ss, ss_b, o + j, h_b, add=(j > 0))
            rstd_finish(kb, ss, ss_b, o, o + 2, D)
            kb.ts(kb.dve, a2[:, 0, :], h_t[:, 0, :], ss[:, o:o + 1], None, ALU.mult, None, [h_b, ss_b], [a2_b])
            kb.actf(a2[:, 1, :], h_t[:, 1, :], AF.Copy, [h_b, ss_b], [a2_b], add=True, scale=ss[:, o + 1:o + 2])

        def transpose_part(t):
            aT, aT_b = a2T[t % 2]
            for j in range(2):
                bk, bk_b = kb.next_bank()
                bkv = bk[:, :].bitcast(BF16)
                for c in range(8):
                    kb.tr(bkv[:, c * 128:(c + 1) * 128], a2[:, j, c * 128:(c + 1) * 128], kb.ident[:, :],
                          [a2_b, kb.const_b], [bk_b])
                src = bkv.rearrange("p (c t) -> p c t", c=8)
                if j == 0:
                    kb.cp(kb.dve, aT[:, :, j * 128:(j + 1) * 128], src, [bk_b], [aT_b])
                else:
                    kb.actf(aT[:, :, j * 128:(j + 1) * 128], src, AF.Copy, [bk_b], [aT_b], add=True)

        load(0)
        if ntiles > 1:
            load(1)
        norm_part(0)
        transpose_part(0)
        rli = 0
        for t in range(ntiles):
            s = t % 2
            h_t, h_b = hx[t % 3]
            aT, aT_b = a2T[s]
            if t + 2 < ntiles:
                load(t + 2)
            if t + 1 < ntiles:
                norm_part(t + 1)
            for fp in range(16):
                bk, bk_b = kb.next_bank()
                for hf in range(2):
                    ft = 2 * fp + hf
                    for c in range(8):
                        kb.mm(bk[:, hf * T:(hf + 1) * T], wu[:, c, ft * 128:(ft + 1) * 128], aT[:, c, :],
                              c == 0, c == 7, [wu_b, aT_b], [bk_b])
                r_t, r_b = rl[rli % 3]
                rli += 1
                uo = uT[:, 2 * fp:2 * fp + 2, :].rearrange("p a t -> p (a t)")
                if fp % 2 == 0:
                    kb.actf(r_t[:, :], bk[:, :], AF.Relu, [bk_b], [r_b])
                    kb.tt(kb.dve, uo, r_t[:, :], r_t[:, :], ALU.mult, [r_b], [uT_b], add=(fp > 0))
                else:
                    kb.ts(kb.dve, r_t[:, :], bk[:, :], 0.0, None, ALU.max, None, [bk_b], [r_b])
                    kb.actf(uo, r_t[:, :], AF.Square, [r_b], [uT_b], add=True)
            if t + 1 < ntiles:
                transpose_part(t + 1)
            for j in range(2):
                for hf in range(2):
                    bk, bk_b = kb.next_bank()
                    for ft in range(32):
                        kb.mm(bk[:, :], uT[:, ft, j * 128:(j + 1) * 128], wd[:, ft, hf * 512:(hf + 1) * 512],
                              ft == 0, ft == 31, [uT_b, wd_b], [bk_b])
                    kb.tt(kb.dve, h_t[:, j, hf * 512:(hf + 1) * 512], bk[:, :], h_t[:, j, hf * 512:(hf + 1) * 512],
                          ALU.add, [bk_b, h_b], [h_b])
            dst = H_out[t * T:(t + 1) * T, :].rearrange("(j p) f -> p j f", p=128)
            if final is None:
                kb.dma(dst, h_t[:, :, :], kb.st_sems[t % 3], reads=[h_b])
            else:
                for j in range(2):
                    rms_rstd(kb, h_t[:, j, :], D, a2[:, j, :], a2_b, ss, ss_b, 4 + j, h_b, add=(j > 0))
                rstd_finish(kb, ss, ss_b, 4, 6, D)
                for j in range(2):
                    kb.op(kb.dve,
                          lambda h, j=j, h_t=h_t: h.scalar_tensor_tensor(
                              out=h_t[:, j, :], in0=h_t[:, j, :], scalar=ss[:, 4 + j:5 + j], in1=gfin[:, :],
                              op0=ALU.mult, op1=ALU.mult),
                          [h_b, ss_b, gfin_b], [h_b], add=True)
                kb.dma(final[1][t * T:(t + 1) * T, :].rearrange("(j p) f -> p j f", p=128), h_t[:, :, :],
                       kb.st_sems[t % 3], reads=[h_b])
        kb.barrier()
        kb.flush()


def setup_kb(nc, es, consts_bf):
    kb = KB(nc, es)
    kb.stg_sems = [kb.dsem("stg%d" % i) for i in range(6)]
    kb.ld_sems = [kb.dsem("ld%d" % i) for i in range(4)]
    kb.st_sems = [kb.dsem("st%d" % i) for i in range(8)]
    kb.misc_sem = kb.dsem("misc")
    cb = es.enter_context(nc.sbuf_tensor("constbf", [128, 128], BF16))
    kb.ident = cb
    kb.const_b = Buf()
    kb.dma(cb[:, :], consts_bf[:, 0:128], kb.misc_sem, writes=[kb.const_b])
    return kb


def host_consts():
    ident = np.eye(128, dtype=np.float32)
    return ident.astype(ml_dtypes.bfloat16)


def norm_transpose(kb, h_t, h_b, J, a, a_b, aT, aT_b, junk, junk_b, ss, ss_b):
    for j in range(J):
        rms_rstd(kb, h_t[:, j, :], D, a[:, j, :], a_b, ss, ss_b, j, h_b, add=(j > 0))
    rstd_finish(kb, ss, ss_b, 0, J, D)
    for j in range(J):
        if j % 2 == 0:
            kb.ts(kb.dve, a[:, j, :], h_t[:, j, :], ss[:, j:j + 1], None, ALU.mult, None,
                  [h_b, ss_b], [a_b], add=(j > 0))
        else:
            kb.actf(a[:, j, :], h_t[:, j, :], AF.Copy, [h_b, ss_b], [a_b], add=True, scale=ss[:, j:j + 1])
    for j in range(J):
        bk, bk_b = kb.next_bank()
        bkv = bk[:, :].bitcast(BF16)
        for c in range(8):
            kb.tr(bkv[:, c * 128:(c + 1) * 128], a[:, j, c * 128:(c + 1) * 128], kb.ident[:, :],
                  [a_b, kb.const_b], [bk_b])
        src = bkv.rearrange("p (c t) -> p c t", c=8)
        if j % 2 == 0:
            kb.cp(kb.dve, aT[:, :, j * 128:(j + 1) * 128], src, [bk_b], [aT_b], add=(j > 0))
        else:
            kb.actf(aT[:, :, j * 128:(j + 1) * 128], src, AF.Copy, [bk_b], [aT_b], add=True)


class OutStage:
    def __init__(self, kb, cx, n=6, shape=(128, 512), dt=BF16):
        self.kb = kb
        self.tiles = [cx.sb(list(shape), dt) for _ in range(n)]
        self.i = 0

    def next(self):
        k = self.i % len(self.tiles)
        self.i += 1
        t, b = self.tiles[k]
        return t, b, self.kb.st_sems[k]


def evac(kb, k, out, in_, reads, writes, add=False):
    if k % 2 == 0:
        kb.actf(out, in_, AF.Copy, reads, writes, add=add)
    else:
        kb.cp(kb.dve, out, in_, reads, writes, add=add)


def norm_stats_scale(kb, h_t, h_b, J, a, a_b, ss, ss_b, c0=0):
    for j in range(J):
        rms_rstd(kb, h_t[:, j, :], D, a[:, j, :], a_b, ss, ss_b, c0 + j, h_b, add=(j > 0))
    rstd_finish(kb, ss, ss_b, c0, c0 + J, D)
    for j in range(J):
        if j % 2 == 0:
            kb.ts(kb.dve, a[:, j, :], h_t[:, j, :], ss[:, c0 + j:c0 + j + 1], None, ALU.mult, None,
                  [h_b, ss_b], [a_b], add=(j > 0))
        else:
            kb.actf(a[:, j, :], h_t[:, j, :], AF.Copy, [h_b, ss_b], [a_b], add=True, scale=ss[:, c0 + j:c0 + j + 1])


def transpose_evac(kb, J, a, a_b, aT, aT_b):
    for j in range(J):
        bk, bk_b = kb.next_bank()
        bkv = bk[:, :].bitcast(BF16)
        for c in range(8):
            kb.tr(bkv[:, c * 128:(c + 1) * 128], a[:, j, c * 128:(c + 1) * 128], kb.ident[:, :],
                  [a_b, kb.const_b], [bk_b])
        src = bkv.rearrange("p (c t) -> p c t", c=8)
        if j % 2 == 0:
            kb.cp(kb.dve, aT[:, :, j * 128:(j + 1) * 128], src, [bk_b], [aT_b], add=(j > 0))
        else:
            kb.actf(aT[:, :, j * 128:(j + 1) * 128], src, AF.Copy, [bk_b], [aT_b], add=True)


def phase_mla_proj(kb, H_in, W, scr, rope_tab, ntiles=8):
    T = 512
    kb.nb = 8
    with Ctx(kb) as cx:
        wqa, wqa_b = cx.sb([128, 8, 384], BF16)
        wkva, wkva_b = cx.sb([128, 8, 288], BF16)
        wqb, wqb_b = cx.sb([128, 3, 1536], BF16)
        wkvb, wkvb_b = cx.sb([128, 2, 2048], BF16)
        stg = make_stg(kb, cx, 6)
        g_attn = load_gain(kb, cx, W['attn_norm'], 8)
        g_q = load_gain(kb, cx, W['q_norm'], 3)
        g_kv = load_gain(kb, cx, W['kv_norm'], 2)
        hx = [cx.sb([128, 4, D], F32) for _ in range(2)]
        cs = [cx.sb([128, 4, 64], F32) for _ in range(2)]
        a, a_b = cx.sb([128, 4, D], BF16)
        aTs = [cx.sb([128, 8, T], BF16) for _ in range(2)]
        ss, ss_b = cx.sb([128, 8], F32)
        sq = [cx.sb([128, 2], F32) for _ in range(2)]
        cqh = [cx.sb([128, 384], BF16) for _ in range(2)]
        ckvh = [cx.sb([128, 256], BF16) for _ in range(2)]
        kpe = [cx.sb([128, 32], BF16) for _ in range(2)]
        tka = [cx.sb([128, 64], F32) for _ in range(2)]
        cqT, cqT_b = cx.sb([128, 3, T], BF16)
        ckvT, ckvT_b = cx.sb([128, 2, T], BF16)
        kpeT, kpeT_b = cx.sb([32, T], BF16)
        ta = [cx.sb([128, 512], F32) for _ in range(2)]
        tb = [cx.sb([128, 512], F32) for _ in range(2)]
        qr = [cx.sb([128, 512], BF16) for _ in range(2)]
        qrT, qrT_b = cx.sb([128, 4, T], BF16)
        vsb, vsb_b = cx.sb([128, 4, D], BF16)
        ost = OutStage(kb, cx, 4)
        st = {"ek": 0}

        def load_h(t):
            s = t % 2
            kb.dma(hx[s][0][:, :, :], H_in[t * T:(t + 1) * T, :].rearrange("(j p) f -> p j f", p=128),
                   kb.ld_sems[s], writes=[hx[s][1]])

        def load_cs(t):
            s = t % 2
            kb.dma(cs[s][0][:, :, :], rope_tab[t * T:(t + 1) * T, :].rearrange("(j p) f -> p j f", p=128),
                   kb.ld_sems[2 + s], writes=[cs[s][1]])

        def load(t):
            load_h(t)
            load_cs(t)

        load(0)
        if ntiles > 1:
            load(1)
        load_cast_weight(kb, cx, stg, W['wq_a'], 0, 8, 384, wqa, wqa_b, gain=g_attn)
        load_cast_weight(kb, cx, stg, W['wkv_a'], 0, 8, 288, wkva, wkva_b, gain=g_attn)
        load_cast_weight(kb, cx, stg, W['wq_b'], 0, 3, 1536, wqb, wqb_b, gain=g_q)
        load_cast_weight(kb, cx, stg, W['wkv_b'], 0, 2, 2048, wkvb, wkvb_b, gain=g_kv)
        norm_stats_scale(kb, hx[0][0], hx[0][1], 4, a, a_b, ss, ss_b, 0)
        transpose_evac(kb, 4, a, a_b, aTs[0][0], aTs[0][1])
        for t in range(ntiles):
            s = t % 2
            cs_t, cs_b = cs[s]
            aT, aT_b = aTs[s]
            keep = {}

            def stage1(j):
                tk = slice(j * 128, (j + 1) * 128)
                p2 = j % 2
                bq, bq_b = kb.next_bank()
                for c in range(8):
                    kb.mm(bq[:, 0:384], aT[:, c, tk], wqa[:, c, :], c == 0, c == 7, [aT_b, wqa_b], [bq_b])
                bkv_, bkv_b = kb.next_bank()
                for c in range(8):
                    kb.mm(bkv_[:, 0:288], aT[:, c, tk], wkva[:, c, :], c == 0, c == 7, [aT_b, wkva_b], [bkv_b])
                sq_t, sq_b = sq[p2]
                cq_t, cq_b = cqh[p2]
                ck_t, ck_b = ckvh[p2]
                kp_t, kp_b = kpe[p2]
                tk_t, tk_b = tka[p2]
                kb.actf(cq_t[:, :], bq[:, 0:384], AF.Square, [bq_b], [cq_b, sq_b], accum_out=sq_t[:, 0:1])
                kb.actf(ck_t[:, :], bkv_[:, 0:256], AF.Square, [bkv_b], [ck_b, sq_b], accum_out=sq_t[:, 1:2], add=True)
                kb.actf(sq_t[:, 0:1], sq_t[:, 0:1], AF.Sqrt, [sq_b], [sq_b], scale=1.0 / 384, bias=EPS)
                kb.actf(sq_t[:, 1:2], sq_t[:, 1:2], AF.Sqrt, [sq_b], [sq_b], scale=1.0 / 256, bias=EPS)
                kb.op(kb.dve, lambda h: h.reciprocal(out=sq_t[:, 0:2], in_=sq_t[:, 0:2]), [sq_b], [sq_b])
                kb.ts(kb.dve, cq_t[:, :], bq[:, 0:384], sq_t[:, 0:1], None, ALU.mult, None, [bq_b, sq_b], [cq_b])
                kb.ts(kb.dve, ck_t[:, :], bkv_[:, 0:256], sq_t[:, 1:2], None, ALU.mult, None, [bkv_b, sq_b], [ck_b])
                kb.tt(kb.dve, tk_t[:, 0:32], bkv_[:, 256:288], cs_t[:, j, 0:32], ALU.mult, [bkv_b, cs_b], [tk_b])
                kb.tt(kb.dve, tk_t[:, 32:64], bkv_[:, 256:288], cs_t[:, j, 32:64], ALU.mult, [bkv_b, cs_b], [tk_b],
                      add=True)
                kb.tt(kb.dve, kp_t[:, 0:16], tk_t[:, 0:16], tk_t[:, 48:64], ALU.subtract, [tk_b], [kp_b])
                kb.tt(kb.dve, kp_t[:, 16:32], tk_t[:, 32:48], tk_t[:, 16:32], ALU.add, [tk_b], [kp_b], add=True)

            def stage2(j):
                tk = slice(j * 128, (j + 1) * 128)
                p2 = j % 2
                cq_t, cq_b = cqh[p2]
                ck_t, ck_b = ckvh[p2]
                kp_t, kp_b = kpe[p2]
                bt, bt_b = kb.next_bank()
                btv = bt[:, :].bitcast(BF16)
                for c in range(3):
                    kb.tr(btv[:, c * 128:(c + 1) * 128], cq_t[:, c * 128:(c + 1) * 128], kb.ident[:, :],
                          [cq_b, kb.const_b], [bt_b])
                for c in range(2):
                    kb.tr(btv[:, 384 + c * 128:384 + (c + 1) * 128], ck_t[:, c * 128:(c + 1) * 128], kb.ident[:, :],
                          [ck_b, kb.const_b], [bt_b])
                kb.tr(btv[0:32, 640:768], kp_t[:, :], kb.ident[:, :], [kp_b, kb.const_b], [bt_b])
                kb.actf(cqT[:, :, tk], btv[:, 0:384].rearrange("p (c t) -> p c t", c=3), AF.Copy, [bt_b], [cqT_b],
                        add=(j > 0))
                kb.actf(ckvT[:, :, tk], btv[:, 384:640].rearrange("p (c t) -> p c t", c=2), AF.Copy, [bt_b], [ckvT_b],
                        add=(j > 0))
                kb.actf(kpeT[:, tk], btv[0:32, 640:768], AF.Copy, [bt_b], [kpeT_b], add=(j > 0))
                br, br_b = kb.next_bank()
                for c in range(3):
                    kb.mm(br[:, :], cqT[:, c, tk], wqb[:, c, 1024:1536], c == 0, c == 2, [cqT_b, wqb_b], [br_b])
                ta_t, ta_b = ta[p2]
                tb_t, tb_b = tb[p2]
                qr_t, qr_b = qr[p2]
                x3 = br[:, :].rearrange("p (h e) -> p h e", h=16)
                cc = cs_t[:, j:j + 1, 0:32].broadcast_to([128, 16, 32])
                sn = cs_t[:, j:j + 1, 32:64].broadcast_to([128, 16, 32])
                ta3 = ta_t[:, :].rearrange("p (h e) -> p h e", h=16)
                tb3 = tb_t[:, :].rearrange("p (h e) -> p h e", h=16)
                qr3 = qr_t[:, :].rearrange("p (h e) -> p h e", h=16)
                kb.tt(kb.dve, ta3, x3, cc, ALU.mult, [br_b, cs_b], [ta_b])
                kb.tt(kb.dve, tb3, x3, sn, ALU.mult, [br_b, cs_b], [tb_b])
                kb.tt(kb.pool, qr3[:, :, 0:16], ta3[:, :, 0:16], tb3[:, :, 16:32], ALU.subtract, [ta_b, tb_b], [qr_b])
                kb.tt(kb.dve, qr3[:, :, 16:32], tb3[:, :, 0:16], ta3[:, :, 16:32], ALU.add, [ta_b, tb_b], [qr_b],
                      add=True)

            def stage3(j):
                tk = slice(j * 128, (j + 1) * 128)
                qr_t, qr_b = qr[j % 2]
                bt2, bt2_b = kb.next_bank()
                bt2v = bt2[:, :].bitcast(BF16)
                for c in range(4):
                    kb.tr(bt2v[:, c * 128:(c + 1) * 128], qr_t[:, c * 128:(c + 1) * 128], kb.ident[:, :],
                          [qr_b, kb.const_b], [bt2_b])
                kb.actf(qrT[:, :, tk], bt2v[:, 0:512].rearrange("p (c t) -> p c t", c=4), AF.Copy, [bt2_b], [qrT_b],
                        add=(j > 0))

            if t + 1 < ntiles:
                norm_stats_scale(kb, hx[1 - s][0], hx[1 - s][1], 4, a, a_b, ss, ss_b, 4 * (1 - s))
            if t + 2 < ntiles:
                load_h(t + 2)
            for step in range(6):
                if step < 4:
                    stage1(step)
                if 0 <= step - 1 < 4:
                    stage2(step - 1)
                if 0 <= step - 2 < 4:
                    stage3(step - 2)
            tcol = slice(t * T, (t + 1) * T)
            for ft in range(8):
                bk, bk_b = kb.next_bank()
                for c in range(3):
                    kb.mm(bk[:, :], wqb[:, c, ft * 128:(ft + 1) * 128], cqT[:, c, :], c == 0, c == 2,
                          [wqb_b, cqT_b], [bk_b])
                o_t, o_b, o_s = ost.next()
                evac(kb, st["ek"], o_t[:, :], bk[:, :], [bk_b], [o_b]); st["ek"] += 1
                kb.dma(scr['QN'][ft * 128:(ft + 1) * 128, tcol], o_t[:, :], o_s, reads=[o_b])
            for ft in range(8):
                bk, bk_b = kb.next_bank()
                for c in range(2):
                    kb.mm(bk[:, :], wkvb[:, c, ft * 128:(ft + 1) * 128], ckvT[:, c, :], c == 0, c == 1,
                          [wkvb_b, ckvT_b], [bk_b])
                o_t, o_b, o_s = ost.next()
                evac(kb, st["ek"], o_t[:, :], bk[:, :], [bk_b], [o_b]); st["ek"] += 1
                kb.dma(scr['KN'][ft * 128:(ft + 1) * 128, tcol], o_t[:, :], o_s, reads=[o_b])
            for j in range(4):
                for hf in range(2):
                    bk, bk_b = kb.next_bank()
                    for c in range(2):
                        kb.mm(bk[:, :], ckvT[:, c, j * 128:(j + 1) * 128],
                              wkvb[:, c, 1024 + hf * 512:1024 + (hf + 1) * 512], c == 0, c == 1,
                              [wkvb_b, ckvT_b], [bk_b])
                    evac(kb, st["ek"], vsb[:, j, hf * 512:(hf + 1) * 512], bk[:, :], [bk_b], [vsb_b], add=(j + hf > 0))
                    st["ek"] += 1
            if t + 1 < ntiles:
                transpose_evac(kb, 4, a, a_b, aTs[1 - s][0], aTs[1 - s][1])
            kb.dma(scr['V'][tcol, :].rearrange("(j p) f -> p j f", p=128), vsb[:, :, :], kb.st_sems[4], reads=[vsb_b])
            kb.dma(scr['QR'].rearrange("(c p) t -> p c t", p=128)[:, :, tcol], qrT[:, :, :], kb.st_sems[5],
                   reads=[qrT_b])
            kb.dma(scr['KPE'][:, tcol], kpeT[:, :], kb.st_sems[6], reads=[kpeT_b])
            if t + 2 < ntiles:
                load_cs(t + 2)
        kb.barrier()
        kb.flush()
    kb.nb = 6


def phase_attn(kb, scr, KD, scale, loader, nheads=16, nqt=8, DEPTH=3, pieces=None):
    kb.nb = 6
    from collections import deque
    with Ctx(kb) as cx:
        QT = [cx.sb([KD, S], BF16) for _ in range(2)]
        KT = [cx.sb([KD, S], BF16) for _ in range(2)]
        VA = [cx.sb([128, 32, 128], BF16) for _ in range(2)]
        pT = [cx.sb([128, 1024], BF16) for _ in range(4)]
        rd, rd_b = cx.sb([128, 512], F32)
        osb = [cx.sb([64, 512], BF16) for _ in range(2)]
        mask, mask_b = cx.sb([128, 128], BF16)
        kb.dma(mask[:, :], scr['cbf'][:, 128:256], kb.misc_sem, writes=[mask_b])
        for i in range(2):
            kb.op(kb.pool, lambda h, i=i: h.memset(VA[i][0][:, :, 64:128], 1.0), [], [VA[i][1]])

        def load(h):
            s = h % 2
            loader(kb, h, QT[s], KT[s], kb.at_sems[s], kb.at_sems[2 + s])
            kb.dma(VA[s][0][:, :, 0:64], scr['V'][:, h * 64:(h + 1) * 64].rearrange("(c p) d -> p c d", p=128),
                   kb.at_sems[4 + s], writes=[VA[s][1]], add=True)

        items = []
        for h in range(nheads):
            for qt in range(nqt):
                first = True
                for kc in range(0, 4 * qt, 2):
                    items.append((h, qt, 'pair', kc, first))
                    first = False
                for c in range(4):
                    items.append((h, qt, 'diag', c, first))
                    first = False

        state = {"wi": 0, "pi": 0}

        def emit(item):
            h, qt, kind, k, first = item
            s = h % 2
            q_t, q_b = QT[s]
            k_t, k_b = KT[s]
            v_t, v_b = VA[s]
            oi = 6 + (qt % 2)
            oacc, oacc_b = kb.banks[oi], kb.bank_bufs[oi]
            w = state["wi"] % 3
            state["wi"] += 1
            W = kb.wide[w]
            b0, b1 = kb.bank_bufs[2 * w], kb.bank_bufs[2 * w + 1]
            p_t, p_b = pT[state["pi"] % 4]
            state["pi"] += 1
            qs = slice(qt * 512, (qt + 1) * 512)
            if kind == 'pair':
                for u in range(2):
                    kcs = slice((k + u) * 128, (k + u + 1) * 128)
                    kb.mm(W[:, u * 512:(u + 1) * 512], k_t[:, kcs], q_t[:, qs], True, True, [k_b, q_b], [(b0, b1)[u]])
                kb.actf(p_t[:, :], W[:, :], AF.Exp, [b0, b1], [p_b], scale=scale)

                def pv():
                    for u in range(2):
                        kb.mm(oacc[:, :], v_t[:, k + u, :], p_t[:, u * 512:(u + 1) * 512], first and u == 0, False,
                              [v_b, p_b], [oacc_b])
                return pv
            c = k
            kc = 4 * qt + c
            kcs = slice(kc * 128, (kc + 1) * 128)
            c0 = c * 128
            q0 = qt * 512 + c0
            kb.mm(W[:, c0:c0 + 128], k_t[:, kcs], q_t[:, q0:q0 + 128], True, False, [k_b, q_b], [b0])
            kb.mm(W[:, c0:c0 + 128], kb.ident[:, :], mask[:, :], False, True, [kb.const_b, mask_b], [b0])
            if c < 3:
                kb.mm(W[:, c0 + 128:512], k_t[:, kcs], q_t[:, q0 + 128:(qt + 1) * 512], True, True, [k_b, q_b], [b0])
            kb.actf(p_t[:, c0:512], W[:, c0:512], AF.Exp, [b0], [p_b], scale=scale)

            def pv():
                kb.mm(oacc[:, c0:512], v_t[:, kc, :], p_t[:, c0:512], first, c == 3, [v_b, p_b], [oacc_b])
                if c == 3:
                    kb.op(kb.dve, lambda h_: h_.reciprocal(out=rd[64:128, :], in_=oacc[64:128, :]), [oacc_b], [rd_b])
                    o_t, o_b = osb[qt % 2]
                    kb.tt(kb.dve, o_t[:, :], oacc[0:64, :], rd[64:128, :], ALU.mult, [oacc_b, rd_b], [o_b])
                    kb.dma(scr['OT'][h * 64:(h + 1) * 64, qt * 512:(qt + 1) * 512], o_t[:, :], kb.st_sems[qt % 2],
                           reads=[o_b])
            return pv

        load(0)
        pend = deque()
        cur_h, since = -1, 0
        for it in items:
            if it[0] != cur_h:
                cur_h, since = it[0], 0
            since += 1
            if since == DEPTH + 2 and cur_h + 1 < nheads:
                load(cur_h + 1)
            if since in (DEPTH + 6, DEPTH + 26, DEPTH + 46, DEPTH + 66):
                take(pieces, 1)
            pend.append(emit(it))
            if len(pend) > DEPTH:
                pend.popleft()()
        while pend:
            pend.popleft()()
        kb.barrier()
        kb.flush()


def mla_loader(scr):
    def f(kb, h, QTt, KTt, qs, ks):
        q_t, q_b = QTt
        k_t, k_b = KTt
        kb.dma(q_t[0:64, :], scr['QN'][h * 64:(h + 1) * 64, :], qs, writes=[q_b])
        kb.dma(q_t[64:96, :], scr['QR'][h * 32:(h + 1) * 32, :], qs, writes=[q_b], add=True)
        kb.dma(k_t[0:64, :], scr['KN'][h * 64:(h + 1) * 64, :], ks, writes=[k_b])
        kb.dma(k_t[64:96, :], scr['KPE'][:, :], ks, writes=[k_b], add=True)
    return f


def fox_loader(scr):
    def f(kb, h, QTt, KTt, qs, ks):
        q_t, q_b = QTt
        k_t, k_b = KTt
        kb.dma(q_t[0:64, :], scr['QN'][h * 64:(h + 1) * 64, :], qs, writes=[q_b])
        kb.dma(q_t[64:70, :], scr['QAUG'][:, h, :], qs, writes=[q_b], add=True)
        kb.dma(k_t[0:64, :], scr['KN'][h * 64:(h + 1) * 64, :], ks, writes=[k_b])
        kb.dma(k_t[64:70, :], scr['KAUG'][:, h, :], ks, writes=[k_b], add=True)
    return f


def take(pieces, n):
    if pieces is None:
        return
    for _ in range(n):
        p = next(pieces, None)
        if p is None:
            return
        p()


def phase_oproj(kb, H_in, H_out, wo_d, scr, ntiles=8, stg=None, pieces=None):
    kb.nb = 8
    T = 512
    with Ctx(kb) as cx:
        wo, wo_b = cx.sb([128, 8, D], BF16)
        if stg is None:
            stg = make_stg(kb, cx)
        load_cast_weight(kb, cx, stg, wo_d, 0, 8, D, wo, wo_b)
        hx = [cx.sb([128, 4, D], F32) for _ in range(2)]
        ot = [cx.sb([128, 8, T], BF16) for _ in range(2)]

        def load(t):
            s = t % 2
            kb.dma(hx[s][0][:, :, :], H_in[t * T:(t + 1) * T, :].rearrange("(j p) f -> p j f", p=128),
                   kb.ld_sems[s], writes=[hx[s][1]])
            kb.dma(ot[s][0][:, :, :], scr['OT'].rearrange("(c p) t -> p c t", p=128)[:, :, t * T:(t + 1) * T],
                   kb.ld_sems[2 + s], writes=[ot[s][1]])

        load(0)
        for t in range(ntiles):
            s = t % 2
            h_t, h_b = hx[s]
            o_t, o_b = ot[s]
            if t + 1 < ntiles:
                load(t + 1)
            take(pieces, 8)
            for j in range(4):
                for hf in range(2):
                    bk, bk_b = kb.next_bank()
                    for c in range(8):
                        kb.mm(bk[:, :], o_t[:, c, j * 128:(j + 1) * 128], wo[:, c, hf * 512:(hf + 1) * 512],
                              c == 0, c == 7, [o_b, wo_b], [bk_b])
                    kb.tt(kb.dve, h_t[:, j, hf * 512:(hf + 1) * 512], bk[:, :], h_t[:, j, hf * 512:(hf + 1) * 512],
                          ALU.add, [bk_b, h_b], [h_b])
            kb.dma(H_out[t * T:(t + 1) * T, :].rearrange("(j p) f -> p j f", p=128), h_t[:, :, :], kb.st_sems[s],
                   reads=[h_b])
        kb.barrier()
        kb.flush()


LAYER_KINDS = ["mla", "fox", "dil", "mla"]
WSHAPES = {
    "mla": [("attn_norm", [D]), ("mla_wq_a", [D, 384]), ("mla_q_norm", [384]), ("mla_wq_b", [384, 1536]),
            ("mla_wkv_a", [D, 288]), ("mla_kv_norm", [256]), ("mla_wkv_b", [256, 2048]), ("mla_wo", [D, D])],
    "fox": [("attn_norm", [D]), ("fox_w_qkv", [D, 3072]), ("fox_w_f", [D, 16]), ("fox_b_f", [16]),
            ("fox_wo", [D, D])],
    "dil": [("attn_norm", [D]), ("dil_w_qkv", [D, 9216]), ("dil_wo", [D, D])],
}
MLP_W = [("mlp_norm", [D]), ("w_up", [D, DFF]), ("w_down", [DFF, D])]
CBF_W = 128 + 128 + 256 + 256


def build_program(nlayers=4, stop_after=None):
    nc = bass.Bass("TRN2", target_bir_lowering=False)
    inp = {}

    def din(name, shape, dt=F32):
        inp[name] = nc.dram_tensor(name, shape, dt, kind="ExternalInput").ap()
        return inp[name]

    x = din("x", [S, D])
    for i in range(nlayers):
        for n, shp in WSHAPES[LAYER_KINDS[i]] + MLP_W:
            din("l%d_%s" % (i, n), shp)
    din("final_norm", [D])
    cbf = din("cbf", [128, CBF_W], BF16)
    rope_mla = din("rope_mla", [S, 64])
    rope_dil = din("rope_dil", [3, 2, 32, S])
    cf32 = din("cf32", [128, 128])
    y = nc.dram_tensor("y", [S, D], F32, kind="ExternalOutput").ap()

    def scratch(name, shape, dt):
        return nc.dram_tensor(name, shape, dt, kind="Internal").ap()

    scr = {
        "cbf": cbf, "cf32": cf32,
        "H": scratch("H", [S, D], F32),
        "QN": scratch("QN", [1024, S], BF16), "QR": scratch("QR", [512, S], BF16),
        "KN": scratch("KN", [1024, S], BF16), "KPE": scratch("KPE", [32, S], BF16),
        "V": scratch("V", [S, 1024], BF16), "OT": scratch("OT", [1024, S], BF16),
        "QAUG": scratch("QAUG", [6, 16, S], BF16), "KAUG": scratch("KAUG", [6, 16, S], BF16),
        "LSP": scratch("LSP", [16, S], F32),
        "QG": scratch("QG", [3, 4, 2, 128, S], BF16), "KG": scratch("KG", [3, 4, 2, 128, S], BF16),
        "VG": scratch("VG", [3, S, 1024], BF16), "OG": scratch("OG", [3, S, 16 * 65], F32),
    }
    with ExitStack() as es:
        kb = setup_kb(nc, es, cbf)
        kb.at_sems = [kb.dsem("at%d" % i) for i in range(6)]
        H = scr["H"]
        for i in range(nlayers):
            kind = LAYER_KINDS[i]
            W = {n: inp["l%d_%s" % (i, n)] for n, _ in WSHAPES[kind] + MLP_W}
            h_in = x if i == 0 else H
            last = (i == nlayers - 1)
            if kind == "mla":
                Wm = {"attn_norm": W["attn_norm"], "wq_a": W["mla_wq_a"], "q_norm": W["mla_q_norm"],
                      "wq_b": W["mla_wq_b"], "wkv_a": W["mla_wkv_a"], "kv_norm": W["mla_kv_norm"],
                      "wkv_b": W["mla_wkv_b"]}
                phase_mla_proj(kb, h_in, Wm, scr, rope_mla)
            elif kind == "fox":
                phase_fox_proj(kb, h_in, W, scr)
            else:
                phase_dil_proj(kb, h_in, W, scr, rope_dil)
            with Ctx(kb) as wcx:
                wu, wu_b = wcx.sb([128, 8, DFF], BF16)
                wd, wd_b = wcx.sb([128, 32, D], BF16)
                stg = make_stg(kb, wcx)
                g = load_gain(kb, wcx, W["mlp_norm"], 8)
                pieces = mlp_weight_pieces(kb, stg, W["w_up"], W["w_down"], g, wu, wu_b, wd, wd_b)
                kb.cast_engs = (kb.pool,)
                kb.piece_q = kb.pool
                if kind == "mla":
                    phase_attn(kb, scr, 96, 96 ** -0.5, mla_loader(scr), pieces=pieces)
                elif kind == "fox":
                    phase_attn(kb, scr, 70, 0.125, fox_loader(scr), pieces=pieces)
                else:
                    phase_dil_attn(kb, scr)
                kb.cast_engs = (kb.dve, kb.act)
                kb.piece_q = None
                if kind == "dil":
                    kb.piece_q = kb.pool
                    phase_dil_oproj(kb, h_in, H, W["dil_wo"], scr, stg=stg, pieces=pieces)
                    kb.piece_q = None
                else:
                    phase_oproj(kb, h_in, H, W[kind + "_wo"], scr, stg=stg, pieces=pieces)
                if stop_after == (i, "attn"):
                    break
                phase_mlp(kb, H, y if last else H, W["w_up"], W["w_down"], W["mlp_norm"],
                          final=(inp["final_norm"], y) if last else None, pre=(wu, wu_b, wd, wd_b, pieces))
        if stop_after is not None:
            copy_out(kb, H, y)
    return nc


def copy_out(kb, H, y):
    with Ctx(kb) as cx:
        t, b = cx.sb([128, 8, D], F32)
        for i in range(4):
            kb.dma(t[:, :, :], H[i * 1024:(i + 1) * 1024, :].rearrange("(j p) f -> p j f", p=128), kb.ld_sems[0],
                   writes=[b])
            kb.dma(y[i * 1024:(i + 1) * 1024, :].rearrange("(j p) f -> p j f", p=128), t[:, :, :], kb.st_sems[0],
                   reads=[b])
        kb.barrier()
        kb.flush()


def dil_perm(g):
    d = (1, 4, 16)[g]
    n = np.arange(S)
    L = S // d
    return (n // L) + d * (n % L)


def host_constants():
    bf = ml_dtypes.bfloat16
    k = np.arange(128)[:, None]
    q = np.arange(128)[None, :]
    NEG = -30000.0
    ident = np.eye(128, dtype=np.float32)
    causal = np.where(k <= q, 0.0, NEG).astype(np.float32)
    prev = np.where(k >= q, 0.0, NEG).astype(np.float32)
    prev01 = (k >= q).astype(np.float32)
    diag01 = (k <= q).astype(np.float32)
    cbf = np.concatenate([ident, causal, prev, causal, prev01, diag01], axis=1).astype(bf)
    t = np.arange(S, dtype=np.float32)[:, None]
    inv = (1.0 / (np.float32(10000.0) ** (np.arange(0, 32, 2, dtype=np.float32) / np.float32(32)))).astype(np.float32)
    ang = (t * inv[None, :]).astype(np.float32)
    c, s = np.cos(ang).astype(np.float32), np.sin(ang).astype(np.float32)
    rope_mla = np.concatenate([c, c, s, s], axis=1).astype(np.float32)
    inv2 = (1.0 / (np.float32(10000.0) ** (np.arange(0, 64, 2, dtype=np.float32) / np.float32(64)))).astype(np.float32)
    ang2 = (t * inv2[None, :]).astype(np.float32)
    c2, s2 = np.cos(ang2).astype(np.float32), np.sin(ang2).astype(np.float32)
    rope_dil = np.zeros((3, 2, 32, S), np.float32)
    for g in range(3):
        p = dil_perm(g)
        rope_dil[g, 0] = c2[p].T
        rope_dil[g, 1] = s2[p].T
    kk = np.arange(128)[:, None]
    mm_ = np.arange(128)[None, :]
    segtri = ((kk // 8 == mm_ // 8) & (kk % 8 < mm_ % 8)).astype(np.float32)
    return {"cbf": cbf, "rope_mla": rope_mla, "rope_dil": rope_dil, "cf32": segtri}


def host_weights(inputs, nlayers=4):
    out = {}
    for i in range(nlayers):
        kind = LAYER_KINDS[i]
        for n, _ in WSHAPES[kind] + MLP_W:
            key = "l%d_%s" % (i, n)
            w = np.asarray(inputs[key], dtype=np.float32)
            if n == "mla_wq_b":
                w3 = w.reshape(384, 16, 96)
                w = np.concatenate([w3[:, :, :64].reshape(384, 1024), w3[:, :, 64:].reshape(384, 512)], axis=1)
            elif n == "mla_wkv_b":
                w3 = w.reshape(256, 16, 128)
                w = np.concatenate([w3[:, :, :64].reshape(256, 1024), w3[:, :, 64:].reshape(256, 1024)], axis=1)
            elif n == "dil_w_qkv":
                w6 = w.reshape(D, 3, 3, 4, 4, 2, 32)
                qk = w6[:, 0:2].transpose(0, 1, 2, 3, 5, 4, 6)
                w = np.concatenate([qk.reshape(D, 2 * 3072), w6[:, 2].reshape(D, 3072)], axis=1)
            out[key] = np.ascontiguousarray(w)
    out["final_norm"] = np.asarray(inputs["final_norm"], dtype=np.float32)
    return out


INPUT_NAMES = (
    "x", "l0_attn_norm", "l0_mla_wq_a", "l0_mla_q_norm",
    "l0_mla_wq_b", "l0_mla_wkv_a", "l0_mla_kv_norm", "l0_mla_wkv_b",
    "l0_mla_wo", "l0_mlp_norm", "l0_w_up", "l0_w_down",
    "l1_attn_norm", "l1_fox_w_qkv", "l1_fox_w_f", "l1_fox_b_f",
    "l1_fox_wo", "l1_mlp_norm", "l1_w_up", "l1_w_down",
    "l2_attn_norm", "l2_dil_w_qkv", "l2_dil_wo", "l2_mlp_norm",
    "l2_w_up", "l2_w_down", "l3_attn_norm", "l3_mla_wq_a",
    "l3_mla_q_norm", "l3_mla_wq_b", "l3_mla_wkv_a", "l3_mla_kv_norm",
    "l3_mla_wkv_b", "l3_mla_wo", "l3_mlp_norm", "l3_w_up",
    "l3_w_down", "final_norm",
)


_PROG = {}


def kernel(**inputs):
    missing = [n for n in INPUT_NAMES if n not in inputs]
    assert not missing, missing
    if "full" not in _PROG:
        _PROG["full"] = build_program()
    nc = _PROG["full"]
    consts = host_constants()
    wts = host_weights(inputs)
    x = np.asarray(inputs["x"], dtype=np.float32)
    in_maps = []
    for b in range(NCORES):
        m = dict(wts)
        m.update(consts)
        m["x"] = np.ascontiguousarray(x[b])
        in_maps.append(m)
    res = run_bass_kernel_spmd(nc, in_maps, core_ids=list(range(NCORES)))
    return np.stack([np.asarray(r["y"], dtype=np.float32) for r in res.results], axis=0)


def phase_fox_proj(kb, H_in, W, scr, ntiles=8):
    kb.nb = 8
    T = 512
    with Ctx(kb) as cx:
        wqkv, wqkv_b = cx.sb([128, 8, 3072], BF16)
        wf, wf_b = cx.sb([128, 8, 16], BF16)
        stg = make_stg(kb, cx, 6)
        g_attn = load_gain(kb, cx, W['attn_norm'], 8)
        negb, negb_b = cx.sb([16, 1], F32)
        kb.dma(negb[:, :], W['fox_b_f'].rearrange("(p o) -> p o", o=1), kb.misc_sem, writes=[negb_b],
               allow_slow_non_contiguous=True)
        kb.ts(kb.dve, negb[:, :], negb[:, :], -1.0, None, ALU.mult, None, [negb_b], [negb_b])
        hx = [cx.sb([128, 4, D], F32) for _ in range(2)]
        a, a_b = cx.sb([128, 4, D], BF16)
        aTs = [cx.sb([128, 8, T], BF16) for _ in range(2)]
        ss, ss_b = cx.sb([128, 8], F32)
        vsb, vsb_b = cx.sb([128, 4, D], BF16)
        ef, ef_b = cx.sb([16, T], F32)
        lsp = [cx.sb([16, T], F32) for _ in range(2)]
        ost = OutStage(kb, cx, 4)
        ek = 0

        def load(t):
            s = t % 2
            kb.dma(hx[s][0][:, :, :], H_in[t * T:(t + 1) * T, :].rearrange("(j p) f -> p j f", p=128),
                   kb.ld_sems[s], writes=[hx[s][1]])

        load(0)
        if ntiles > 1:
            load(1)
        load_cast_weight(kb, cx, stg, W['fox_w_qkv'], 0, 8, 3072, wqkv, wqkv_b, gain=g_attn)
        load_cast_weight(kb, cx, stg, W['fox_w_f'], 0, 8, 16, wf, wf_b, gain=g_attn)
        norm_stats_scale(kb, hx[0][0], hx[0][1], 4, a, a_b, ss, ss_b, 0)
        transpose_evac(kb, 4, a, a_b, aTs[0][0], aTs[0][1])
        for t in range(ntiles):
            s = t % 2
            aT, aT_b = aTs[s]
            if t + 1 < ntiles:
                norm_stats_scale(kb, hx[1 - s][0], hx[1 - s][1], 4, a, a_b, ss, ss_b, 4 * (1 - s))
            if t + 2 < ntiles:
                load(t + 2)
            tcol = slice(t * T, (t + 1) * T)
            for which, dst in ((0, 'QN'), (1, 'KN')):
                for ft in range(8):
                    bk, bk_b = kb.next_bank()
                    c0 = which * 1024 + ft * 128
                    for c in range(8):
                        kb.mm(bk[:, :], wqkv[:, c, c0:c0 + 128], aT[:, c, :], c == 0, c == 7, [wqkv_b, aT_b], [bk_b])
                    o_t, o_b, o_s = ost.next()
                    evac(kb, ek, o_t[:, :], bk[:, :], [bk_b], [o_b]); ek += 1
                    kb.dma(scr[dst][ft * 128:(ft + 1) * 128, tcol], o_t[:, :], o_s, reads=[o_b])
            for j in range(4):
                for hf in range(2):
                    bk, bk_b = kb.next_bank()
                    for c in range(8):
                        kb.mm(bk[:, :], aT[:, c, j * 128:(j + 1) * 128], wqkv[:, c, 2048 + hf * 512:2048 + (hf + 1) * 512],
                              c == 0, c == 7, [wqkv_b, aT_b], [bk_b])
                    evac(kb, ek, vsb[:, j, hf * 512:(hf + 1) * 512], bk[:, :], [bk_b], [vsb_b], add=(j + hf > 0)); ek += 1
            kb.dma(scr['V'][tcol, :].rearrange("(j p) f -> p j f", p=128), vsb[:, :, :], kb.st_sems[4], reads=[vsb_b])
            bk, bk_b = kb.next_bank()
            for c in range(8):
                kb.mm(bk[0:16, :], wf[:, c, :], aT[:, c, :], c == 0, c == 7, [wf_b, aT_b], [bk_b])
            kb.actf(ef[:, :], bk[0:16, :], AF.Exp, [bk_b, negb_b], [ef_b], scale=-1.0, bias=negb[:, 0:1])
            l_t, l_b = lsp[s]
            kb.actf(l_t[:, :], ef[:, :], AF.Ln, [ef_b], [l_b], bias=1.0)
            kb.dma(scr['LSP'][:, tcol], l_t[:, :], kb.st_sems[5 + s], reads=[l_b])
            if t + 1 < ntiles:
                transpose_evac(kb, 4, a, a_b, aTs[1 - s][0], aTs[1 - s][1])
        kb.barrier()
        kb.flush()
    with Ctx(kb) as cx:
        def seg(ap2d):
            return ap2d.rearrange("h (s t) -> (h s) t", s=8)
        L, L_b = cx.sb([128, 512], F32)
        R, R_b = cx.sb([128, 512], F32)
        t2, t2_b = cx.sb([128, 512], F32)
        tri, tri_b = cx.sb([128, 128], F32)
        off, off_b = cx.sb([128, 1], F32)
        P = [cx.sb([128, 512], BF16) for _ in range(3)]
        Nn = [cx.sb([128, 512], BF16) for _ in range(3)]
        ones, ones_b = cx.sb([128, 512], BF16)
        kb.dma(L[:, :], seg(scr['LSP'][:, :]), kb.ld_sems[0], writes=[L_b])
        kb.dma(tri[:, :], scr['cf32'][:, :], kb.ld_sems[1], writes=[tri_b])
        kb.op(kb.pool, lambda h: h.memset(ones[:, :], 1.0), [], [ones_b])
        kb.op(kb.dve, lambda h: h.tensor_tensor_scan(out=R[:, :], data0=L[:, :], data1=L[:, :], initial=0.0,
                                                     op0=ALU.add, op1=ALU.add), [L_b], [R_b])
        bk, bk_b = kb.next_bank()
        kb.mm(bk[:, 0:1], tri[:, :], R[:, 511:512], True, True, [tri_b, R_b], [bk_b])
        kb.cp(kb.dve, off[:, :], bk[:, 0:1], [bk_b], [off_b])
        kb.ts(kb.dve, R[:, :], R[:, :], off[:, 0:1], -4.0, ALU.add, ALU.mult, [R_b, off_b], [R_b])
        for k in range(3):
            p_t, p_b = P[k]
            n_t, n_b = Nn[k]
            kb.cp(kb.dve, p_t[:, :], R[:, :], [R_b], [p_b])
            kb.dma(seg(scr['QAUG'][k, :, :]), p_t[:, :], kb.st_sems[k], reads=[p_b])
            kb.ts(kb.pool, n_t[:, :], p_t[:, :], -1.0, None, ALU.mult, None, [p_b], [n_b])
            kb.dma(seg(scr['KAUG'][3 + k, :, :]), n_t[:, :], kb.st_sems[3 + k], reads=[n_b])
            kb.dma(seg(scr['QAUG'][3 + k, :, :]), ones[:, :], kb.st_sems[6], reads=[ones_b])
            kb.dma(seg(scr['KAUG'][k, :, :]), ones[:, :], kb.st_sems[7], reads=[ones_b])
            if k < 2:
                kb.cp(kb.dve, t2[:, :], p_t[:, :], [p_b], [t2_b])
                kb.tt(kb.dve, R[:, :], R[:, :], t2[:, :], ALU.subtract, [R_b, t2_b], [R_b])
        kb.barrier()
        kb.flush()


DIL_D = (1, 4, 16)


def phase_dil_proj(kb, H_in, W, scr, rope_dil, ntiles=8):
    kb.nb = 8
    T = 512
    with Ctx(kb) as cx:
        wsets = [[cx.sb([128, 8, 1024], BF16) for _ in range(3)] for _ in range(2)]
        stg = make_stg(kb, cx, 4)
        g_attn = load_gain(kb, cx, W['attn_norm'], 8)
        wall = W['dil_w_qkv']
        hx = [cx.sb([128, 4, D], F32) for _ in range(2)]
        ct = [cx.sb([128, T], F32) for _ in range(2)]
        st_ = [cx.sb([128, T], F32) for _ in range(2)]
        a, a_b = cx.sb([128, 4, D], BF16)
        aTs = [cx.sb([128, 8, T], BF16) for _ in range(2)]
        ss, ss_b = cx.sb([128, 8], F32)
        vsb, vsb_b = cx.sb([128, 4, D], BF16)
        m8 = [cx.sb([128, T], F32) for _ in range(8)]
        mi = [0]
        ost = OutStage(kb, cx, 4)

        def weight_pieces(g):
            ws = wsets[g % 2]
            for which in range(3):
                w_t, w_b = ws[which]
                col0 = which * 3072 + g * 1024
                for c in range(8):
                    yield lambda c=c, w_t=w_t, w_b=w_b, col0=col0: load_cast_piece(
                        kb, stg, wall[c * 128:(c + 1) * 128, col0:col0 + 1024], w_t[:, c, :], w_b,
                        (g_attn[0][:, c:c + 1], g_attn[1]))

        seq = [(g, t) for g in range(3) for t in range(ntiles)]

        def load(i):
            g, t = seq[i]
            d = DIL_D[g]
            L = S // d
            Hg = H_in.rearrange("(i r) f -> r i f", r=d)
            s = i % 2
            for j in range(4):
                n = t * T + j * 128
                kb.dma(hx[s][0][:, j, :], Hg[n // L, (n % L):(n % L) + 128, :], kb.ld_sems[s],
                       writes=[hx[s][1]], add=(j > 0))
            for rep in range(4):
                kb.dma(ct[s][0][rep * 32:(rep + 1) * 32, :], rope_dil[g, 0, :, t * T:(t + 1) * T], kb.ld_sems[2 + s],
                       writes=[ct[s][1]], add=(rep > 0))
                kb.dma(st_[s][0][rep * 32:(rep + 1) * 32, :], rope_dil[g, 1, :, t * T:(t + 1) * T], kb.ld_sems[2 + s],
                       writes=[st_[s][1]], add=True)

        load(0)
        if len(seq) > 1:
            load(1)
        kb.piece_q = kb.pool
        for p in weight_pieces(0):
            p()
        norm_stats_scale(kb, hx[0][0], hx[0][1], 4, a, a_b, ss, ss_b, 0)
        transpose_evac(kb, 4, a, a_b, aTs[0][0], aTs[0][1])
        nxt = None
        for i, (g, t) in enumerate(seq):
            s = i % 2
            if t == 0:
                nxt = weight_pieces(g + 1) if g < 2 else None
            (wq, wq_b), (wk, wk_b), (wv, wv_b) = wsets[g % 2]
            aT, aT_b = aTs[s]
            C, C_b = ct[s]
            Sn, Sn_b = st_[s]
            if i + 1 < len(seq):
                norm_stats_scale(kb, hx[1 - s][0], hx[1 - s][1], 4, a, a_b, ss, ss_b, 4 * (1 - s))
            take(nxt, 3)
            tcol = slice(t * T, (t + 1) * T)
            for w_t, w_b, dst in ((wq, wq_b, 'QG'), (wk, wk_b, 'KG')):
                for pr in range(4):
                    bA, bA_b = kb.next_bank()
                    for c in range(8):
                        kb.mm(bA[:, :], w_t[:, c, pr * 256:pr * 256 + 128], aT[:, c, :], c == 0, c == 7,
                              [w_b, aT_b], [bA_b])
                    bB, bB_b = kb.next_bank()
                    for c in range(8):
                        kb.mm(bB[:, :], w_t[:, c, pr * 256 + 128:pr * 256 + 256], aT[:, c, :], c == 0, c == 7,
                              [w_b, aT_b], [bB_b])
                    m = m8[4 * (mi[0] % 2):4 * (mi[0] % 2) + 4]
                    mi[0] += 1
                    kb.tt(kb.dve, m[0][0][:, :], bA[:, :], C[:, :], ALU.mult, [bA_b, C_b], [m[0][1]])
                    kb.tt(kb.dve, m[1][0][:, :], bB[:, :], Sn[:, :], ALU.mult, [bB_b, Sn_b], [m[1][1]])
                    kb.tt(kb.dve, m[2][0][:, :], bA[:, :], Sn[:, :], ALU.mult, [bA_b, Sn_b], [m[2][1]])
                    kb.tt(kb.dve, m[3][0][:, :], bB[:, :], C[:, :], ALU.mult, [bB_b, C_b], [m[3][1]])
                    o_t, o_b, o_s = ost.next()
                    kb.tt(kb.pool, o_t[:, :], m[0][0][:, :], m[1][0][:, :], ALU.subtract, [m[0][1], m[1][1]], [o_b])
                    kb.dma(scr[dst][g, pr, 0, :, tcol], o_t[:, :], o_s, reads=[o_b])
                    o_t, o_b, o_s = ost.next()
                    kb.tt(kb.pool, o_t[:, :], m[2][0][:, :], m[3][0][:, :], ALU.add, [m[2][1], m[3][1]], [o_b])
                    kb.dma(scr[dst][g, pr, 1, :, tcol], o_t[:, :], o_s, reads=[o_b])
            for j in range(4):
                for hf in range(2):
                    bk, bk_b = kb.next_bank()
                    for c in range(8):
                        kb.mm(bk[:, :], aT[:, c, j * 128:(j + 1) * 128], wv[:, c, hf * 512:(hf + 1) * 512],
                              c == 0, c == 7, [wv_b, aT_b], [bk_b])
                    kb.actf(vsb[:, j, hf * 512:(hf + 1) * 512], bk[:, :], AF.Copy, [bk_b], [vsb_b], add=(j + hf > 0))
            kb.dma(scr['VG'][g, tcol, :].rearrange("(j p) f -> p j f", p=128), vsb[:, :, :], kb.st_sems[4],
                   reads=[vsb_b])
            if i + 1 < len(seq):
                transpose_evac(kb, 4, a, a_b, aTs[1 - s][0], aTs[1 - s][1])
            if i + 2 < len(seq):
                load(i + 2)
            if t == ntiles - 1 and nxt is not None:
                for p in nxt:
                    p()
        kb.piece_q = None
        kb.barrier()
        kb.flush()


def phase_dil_attn(kb, scr, nidx=48, DEPTH=3, pieces=None):
    from collections import deque
    kb.nb = 6
    with Ctx(kb) as cx:
        QT = [cx.sb([64, S], BF16) for _ in range(2)]
        KT = [cx.sb([64, S], BF16) for _ in range(2)]
        VA = [cx.sb([128, 32, 66], BF16) for _ in range(2)]
        pT = [cx.sb([128, 256], BF16) for _ in range(6)]
        UG = [cx.sb([128, 32, 65], F32) for _ in range(2)]
        mask, mask_b = cx.sb([128, 256], BF16)
        kb.dma(mask[:, :], scr['cbf'][:, 512:768], kb.misc_sem, writes=[mask_b])
        for i in range(2):
            kb.op(kb.pool, lambda h, i=i: h.memset(VA[i][0][:, :, 64:65], 1.0), [], [VA[i][1]])

        def load(idx):
            s = idx % 2
            g, h = idx // 16, idx % 16
            hb, hh = h // 4, h % 4
            rows = slice(hh * 32, (hh + 1) * 32)
            q_t, q_b = QT[s]
            k_t, k_b = KT[s]
            kb.dma(q_t[0:32, :], scr['QG'][g, hb, 0, rows, :], kb.at_sems[s], writes=[q_b])
            kb.dma(q_t[32:64, :], scr['QG'][g, hb, 1, rows, :], kb.at_sems[s], writes=[q_b], add=True)
            kb.dma(k_t[0:32, :], scr['KG'][g, hb, 0, rows, :], kb.at_sems[2 + s], writes=[k_b])
            kb.dma(k_t[32:64, :], scr['KG'][g, hb, 1, rows, :], kb.at_sems[2 + s], writes=[k_b], add=True)
            kb.dma(VA[s][0][:, :, 0:64], scr['VG'][g, :, h * 64:(h + 1) * 64].rearrange("(c p) d -> p c d", p=128),
                   kb.at_sems[4 + s], writes=[VA[s][1]], add=True)

        st = {"pi": 0, "ek": 0, "obi": 0}

        def emit(idx, B):
            s = idx % 2
            g, h = idx // 16, idx % 16
            d = DIL_D[g]
            Lb = S // d // 128
            q_t, q_b = QT[s]
            k_t, k_b = KT[s]
            v_t, v_b = VA[s]
            u_t, u_b = UG[s]
            j = B % Lb
            cur = slice(B * 128, (B + 1) * 128)
            prv = slice((B - 1) * 128, B * 128)
            sb_, sb_b = kb.next_bank()
            p_t, p_b = pT[st["pi"] % 6]
            st["pi"] += 1
            c0 = 0 if j > 0 else 128
            if j > 0:
                kb.mm(sb_[:, 0:128], k_t[:, prv], q_t[:, cur], True, True, [k_b, q_b], [sb_b])
            kb.mm(sb_[:, 128:256], k_t[:, cur], q_t[:, cur], True, True, [k_b, q_b], [sb_b])
            kb.actf(p_t[:, c0:256], sb_[:, c0:256], AF.Exp, [sb_b], [p_b], scale=0.125)
            kb.tt(kb.dve, p_t[:, c0:256], p_t[:, c0:256], mask[:, c0:256], ALU.mult, [p_b, mask_b], [p_b])
            B0 = (B // 7) * 7
            nb = min(7, 32 - B0)
            obi = (B // 7) % 2

            def pv():
                ob, ob_b = kb.banks[6 + obi], kb.bank_bufs[6 + obi]
                reg = ob[:, (B - B0) * 65:(B - B0 + 1) * 65]
                if j > 0:
                    kb.mm(reg, p_t[:, 0:128], v_t[:, B - 1, 0:65], True, False, [p_b, v_b], [ob_b])
                    kb.mm(reg, p_t[:, 128:256], v_t[:, B, 0:65], False, True, [p_b, v_b], [ob_b])
                else:
                    kb.mm(reg, p_t[:, 128:256], v_t[:, B, 0:65], True, True, [p_b, v_b], [ob_b])
                if B - B0 == nb - 1:
                    evac(kb, st["ek"], u_t[:, B0:B0 + nb, :], ob[:, 0:nb * 65].rearrange("p (b e) -> p b e", e=65),
                         [ob_b], [u_b], add=(B0 > 0))
                    st["ek"] += 1
                if B == 31:
                    OGv = scr['OG'][g].rearrange("(i r) (h e) -> r i h e", r=d, e=65)
                    for r in range(d):
                        kb.dma(OGv[r, :, h, :].rearrange("(j p) e -> p j e", p=128), u_t[:, r * Lb:(r + 1) * Lb, :],
                               kb.st_sems[s], reads=[u_b], add=(r > 0))
            return pv

        load(0)
        pend = deque()
        for idx in range(nidx):
            for B in range(32):
                if B == DEPTH + 1 and idx + 1 < nidx:
                    load(idx + 1)
                if B in (10, 24):
                    take(pieces, 1)
                pend.append(emit(idx, B))
                if len(pend) > DEPTH:
                    pend.popleft()()
        while pend:
            pend.popleft()()
        kb.barrier()
        kb.flush()


def phase_dil_oproj(kb, H_in, H_out, wo_d, scr, ntiles=32, stg=None, pieces=None):
    kb.nb = 8
    T = 128
    J = T // 128
    with Ctx(kb) as cx:
        wo, wo_b = cx.sb([128, 8, D], BF16)
        if stg is None:
            stg = make_stg(kb, cx)
        load_cast_weight(kb, cx, stg, wo_d, 0, 8, D, wo, wo_b)
        hx = [cx.sb([128, J, D], F32) for _ in range(2)]
        og = [cx.sb([128, 3, 1040], F32) for _ in range(2)]
        rden, rden_b = cx.sb([128, 2, 16], F32)
        ob, ob_b = cx.sb([128, J, D], BF16)
        oTs = [cx.sb([128, 8, T], BF16) for _ in range(2)]
        nsub = ntiles * J

        def load_h(t):
            s = t % 2
            kb.dma(hx[s][0][:, :, :], H_in[t * T:(t + 1) * T, :].rearrange("(j p) f -> p j f", p=128),
                   kb.ld_sems[s], writes=[hx[s][1]])

        def load_og(k):
            s = k % 2
            for g in range(3):
                kb.dma(og[s][0][:, g, :], scr['OG'][g, k * 128:(k + 1) * 128, :], kb.ld_sems[2 + s], writes=[og[s][1]],
                       add=(g > 0))

        def prep1(t):
            for j in range(J):
                k = t * J + j
                g_t, g_b = og[k % 2]
                kb.tt(kb.dve, g_t[:, 0, :], g_t[:, 0, :], g_t[:, 1, :], ALU.add, [g_b], [g_b])
                kb.tt(kb.dve, g_t[:, 0, :], g_t[:, 0, :], g_t[:, 2, :], ALU.add, [g_b], [g_b])
                g3 = g_t[:, 0, :].rearrange("p (h e) -> p h e", e=65)
                kb.op(kb.dve, lambda h_, g3=g3, j=j: h_.reciprocal(out=rden[:, j, :], in_=g3[:, :, 64]), [g_b], [rden_b],
                      add=(j > 0))
                kb.tt(kb.dve, ob[:, j, :].rearrange("p (h e) -> p h e", e=64), g3[:, :, 0:64],
                      rden[:, j, :].unsqueeze(2).broadcast_to([128, 16, 64]), ALU.mult, [g_b, rden_b], [ob_b],
                      add=(j > 0))
                if k + 2 < nsub:
                    load_og(k + 2)

        def prep2(t):
            oT, oT_b = oTs[t % 2]
            for j in range(J):
                bk, bk_b = kb.next_bank()
                bkv = bk[:, :].bitcast(BF16)
                for c in range(8):
                    kb.tr(bkv[:, c * 128:(c + 1) * 128], ob[:, j, c * 128:(c + 1) * 128], kb.ident[:, :],
                          [ob_b, kb.const_b], [bk_b])
                kb.actf(oT[:, :, j * 128:(j + 1) * 128], bkv.rearrange("p (c t) -> p c t", c=8), AF.Copy, [bk_b], [oT_b],
                        add=(j > 0))

        load_h(0)
        load_og(0)
        if nsub > 1:
            load_og(1)
        if ntiles > 1:
            load_h(1)
        prep1(0)
        prep2(0)
        for t in range(ntiles):
            s = t % 2
            h_t, h_b = hx[s]
            oT, oT_b = oTs[s]
            take(pieces, 2)
            if t + 1 < ntiles:
                prep1(t + 1)
            for j in range(J):
                for hf in range(2):
                    bk, bk_b = kb.next_bank()
                    for c in range(8):
                        kb.mm(bk[:, :], oT[:, c, j * 128:(j + 1) * 128], wo[:, c, hf * 512:(hf + 1) * 512],
                              c == 0, c == 7, [oT_b, wo_b], [bk_b])
                    kb.tt(kb.dve, h_t[:, j, hf * 512:(hf + 1) * 512], bk[:, :], h_t[:, j, hf * 512:(hf + 1) * 512],
                          ALU.add, [bk_b, h_b], [h_b])
            if t + 1 < ntiles:
                prep2(t + 1)
            kb.dma(H_out[t * T:(t + 1) * T, :].rearrange("(j p) f -> p j f", p=128), h_t[:, :, :], kb.st_sems[s],
                   reads=[h_b])
            if t + 2 < ntiles:
                load_h(t + 2)
        kb.barrier()
        kb.flush()
```

```python
import numpy as np
from contextlib import ExitStack
import ml_dtypes
import concourse.bass as bass
import concourse.mybir as mybir
from concourse.bass_utils import run_bass_kernel_spmd

F32 = mybir.dt.float32
BF16 = mybir.dt.bfloat16
AF = mybir.ActivationFunctionType
ALU = mybir.AluOpType
AX = mybir.AxisListType

S = 4096
D = 1024
DFF = 4096
EPS = 1e-6
NCORES = 8


class Sem:
    def __init__(self, h, name):
        self.h = h
        self.val = 0
        self.name = name


class Buf:
    __slots__ = ("w", "r", "rp", "excl")

    def __init__(self, excl=False):
        self.excl = excl
        self.w = {}
        self.r = {}
        self.rp = {}


class Eng:
    def __init__(self, name, sem):
        self.name = name
        self.sem = sem
        self.waited = {}
        self.ops = []


class KB:
    def __init__(self, nc, es):
        self.nc = nc
        self.es = es
        self.nsem = 0
        self.pe = Eng("pe", self.new_sem("pe"))
        self.act = Eng("act", self.new_sem("act"))
        self.dve = Eng("dve", self.new_sem("dve"))
        self.pool = Eng("pool", self.new_sem("pool"))
        self.sp = Eng("sp", self.new_sem("sp"))
        self.engs = [self.pe, self.act, self.dve, self.pool, self.sp]
        self.bar = self.new_sem("bar")
        self.dsems = []
        self.banks = []
        self.bank_bufs = []
        self.wide = []
        for i in range(4):
            t = es.enter_context(nc.psum_tensor("psw%d" % i, [128, 1024], F32))
            self.wide.append(t)
            for k in range(2):
                self.banks.append(t[:, k * 512:(k + 1) * 512])
                self.bank_bufs.append(Buf(excl=True))
        self.bank_rr = 0
        self.nb = 6
        self.cast_engs = (self.dve, self.act)
        self.piece_q = None

    def new_sem(self, name):
        h = self.es.enter_context(self.nc.semaphore(name))
        self.nsem += 1
        return Sem(h, name)

    def dsem(self, name):
        s = self.new_sem(name)
        self.dsems.append(s)
        return s

    def next_bank(self, lo=0, hi=None):
        n = (hi or self.nb) - lo
        i = lo + (self.bank_rr % n)
        self.bank_rr += 1
        return self.banks[i], self.bank_bufs[i]

    def _waits(self, eng, deps):
        for sem, val in deps.items():
            if eng.waited.get(sem, 0) >= val:
                continue
            eng.waited[sem] = val
            eng.ops.append(("wait", sem.h, val))

    def _deps(self, eng_sem, reads, writes, add, skip_same_raw):
        deps = {}

        def put(s, v):
            if v > deps.get(s, 0):
                deps[s] = v

        for b in reads:
            for s, v in b.w.items():
                if s is eng_sem and skip_same_raw:
                    continue
                put(s, v)
            if b.excl:
                for s, v in b.r.items():
                    if s is not eng_sem:
                        put(s, v)
        for b in writes:
            if not add:
                for s, v in b.w.items():
                    if s is not eng_sem or not skip_same_raw:
                        put(s, v)
            else:
                for s, v in b.rp.items():
                    if s is not eng_sem or not skip_same_raw:
                        put(s, v)
            for s, v in b.r.items():
                if s is not eng_sem or not skip_same_raw:
                    put(s, v)
        return deps

    def _mark(self, sem, reads, writes, add):
        for b in writes:
            if add:
                b.w[sem] = sem.val
            else:
                b.rp = b.r
                b.r = {}
                b.w = {sem: sem.val}
        for b in reads:
            b.r[sem] = sem.val

    def op(self, eng, fn, reads=(), writes=(), add=False):
        deps = self._deps(eng.sem, reads, writes, add, eng is self.pe)
        self._waits(eng, deps)
        eng.sem.val += 1
        eng.ops.append(("op", fn, eng.sem.h, 1))
        self._mark(eng.sem, reads, writes, add)

    def dma(self, out, in_, sem, reads=(), writes=(), add=False, q=None, **kw):
        q = q or self.sp
        deps = self._deps(sem, reads, writes, add, True)
        if not add and sem.val > 0:
            deps[sem] = sem.val
        self._waits(q, deps)
        sem.val += 16
        q.ops.append(("op", lambda h: h.dma_start(out=out, in_=in_, **kw), sem.h, 16))
        self._mark(sem, reads, writes, add)

    def barrier(self):
        sp = self.sp
        deps = {e.sem: e.sem.val for e in self.engs if e is not sp and e.sem.val > 0}
        for s in self.dsems:
            if s.val > 0:
                deps[s] = s.val
        self._waits(sp, deps)
        self.bar.val += 1
        sp.ops.append(("seminc", self.bar.h, 1))
        for e in self.engs:
            if e is not sp:
                e.ops.append(("wait", self.bar.h, self.bar.val))
            e.waited = {x.sem: x.sem.val for x in self.engs}
            for s in self.dsems:
                e.waited[s] = s.val
            e.waited[self.bar] = self.bar.val

    def flush(self):
        nc = self.nc
        with nc.Block() as block:
            pairs = ((self.sp, block.sync), (self.act, block.scalar), (self.pe, block.tensor),
                     (self.dve, block.vector), (self.pool, block.gpsimd))
            for eng, deco in pairs:
                ops = eng.ops
                eng.ops = []

                def body(h, ops=ops):
                    for o in ops:
                        if o[0] == "wait":
                            h.wait_ge(o[1], o[2])
                        elif o[0] == "op":
                            o[1](h).then_inc(o[2], o[3])
                        else:
                            h.sem_inc(o[1], o[2])

                deco(body)

    def mm(self, out, lhsT, rhs, start, stop, reads, writes):
        self.op(self.pe, lambda h: h.matmul(out, lhsT=lhsT, rhs=rhs, start=start, stop=stop),
                reads, writes)

    def tr(self, out, in_, ident, reads, writes):
        self.op(self.pe, lambda h: h.transpose(out, in_, ident), reads, writes)

    def actf(self, out, in_, func, reads, writes, add=False, **kw):
        self.op(self.act, lambda h: h.activation(out=out, in_=in_, func=func, **kw), reads, writes, add)

    def ts(self, eng, out, in0, s1, s2, op0, op1, reads, writes, add=False):
        if op1 is None:
            self.op(eng, lambda h: h.tensor_scalar(out=out, in0=in0, scalar1=s1, scalar2=None, op0=op0),
                    reads, writes, add)
        else:
            self.op(eng, lambda h: h.tensor_scalar(out=out, in0=in0, scalar1=s1, scalar2=s2, op0=op0, op1=op1),
                    reads, writes, add)

    def tt(self, eng, out, in0, in1, op, reads, writes, add=False):
        self.op(eng, lambda h: h.tensor_tensor(out=out, in0=in0, in1=in1, op=op), reads, writes, add)

    def cp(self, eng, out, in_, reads, writes, add=False):
        self.op(eng, lambda h: h.tensor_copy(out=out, in_=in_), reads, writes, add)


class Ctx:
    N = 0

    def __init__(self, kb):
        self.kb = kb
        self.es = ExitStack()
        self.n = 0

    def __enter__(self):
        self.es.__enter__()
        return self

    def __exit__(self, *a):
        return self.es.__exit__(*a)

    def sb(self, shape, dt):
        Ctx.N += 1
        t = self.es.enter_context(self.kb.nc.sbuf_tensor("t%d" % Ctx.N, shape, dt))
        return t, Buf()


def load_cast_weight(kb, cx, stg, w_dram, rows0, nrows_chunks, ncols, dst, dst_buf, gain=None, rr=[0]):
    for c in range(nrows_chunks):
        for q0 in range(0, ncols, 1024):
            qn = min(1024, ncols - q0)
            st, sbuf, ssem = stg[rr[0] % len(stg)]
            eng = (kb.dve, kb.pool, kb.act)[rr[0] % 3]
            rr[0] += 1
            kb.dma(st[:, 0:qn], w_dram[rows0 + c * 128: rows0 + (c + 1) * 128, q0:q0 + qn], ssem, writes=[sbuf])
            o = dst[:, c, q0:q0 + qn]
            if gain is None:
                if eng is kb.act:
                    kb.actf(o, st[:, 0:qn], AF.Copy, [sbuf], [dst_buf], add=True)
                else:
                    kb.cp(eng, o, st[:, 0:qn], [sbuf], [dst_buf], add=True)
            else:
                g, gbuf = gain
                if eng is kb.act:
                    kb.actf(o, st[:, 0:qn], AF.Copy, [sbuf, gbuf], [dst_buf], add=True, scale=g[:, c:c + 1])
                else:
                    kb.ts(eng, o, st[:, 0:qn], g[:, c:c + 1], None, ALU.mult, None, [sbuf, gbuf], [dst_buf], add=True)


def make_stg(kb, cx, n=3):
    stg = []
    for i in range(n):
        t, b = cx.sb([128, 1024], F32)
        stg.append((t, b, kb.stg_sems[i]))
    return stg


def load_gain(kb, cx, g_dram, nchunks):
    g, gb = cx.sb([128, nchunks], F32)
    kb.dma(g[:, :], g_dram.rearrange("(c p) -> p c", p=128), kb.misc_sem, writes=[gb],
           allow_slow_non_contiguous=True)
    return g, gb


def rms_rstd(kb, x_ap, n, junk_ap, junk_b, ss, ss_b, col, xb, add=False):
    kb.actf(junk_ap, x_ap, AF.Square, [xb], [junk_b, ss_b], accum_out=ss[:, col:col + 1], add=add)


def rstd_finish(kb, ss, ss_b, c0, c1, n):
    kb.actf(ss[:, c0:c1], ss[:, c0:c1], AF.Sqrt, [ss_b], [ss_b], scale=1.0 / n, bias=EPS)
    kb.op(kb.dve, lambda h: h.reciprocal(out=ss[:, c0:c1], in_=ss[:, c0:c1]), [ss_b], [ss_b])


def mlp_weight_pieces(kb, stg, w_up, w_down, g, wu, wu_b, wd, wd_b):
    for c in range(8):
        for q0 in range(0, DFF, 1024):
            yield lambda c=c, q0=q0: load_cast_piece(kb, stg, w_up[c * 128:(c + 1) * 128, q0:q0 + 1024],
                                                     wu[:, c, q0:q0 + 1024], wu_b, (g[0][:, c:c + 1], g[1]))
    for c in range(32):
        yield lambda c=c: load_cast_piece(kb, stg, w_down[c * 128:(c + 1) * 128, :], wd[:, c, :], wd_b, None)


def load_cast_piece(kb, stg, src, dst, dst_b, gain, rr=[0]):
    st, sbuf, ssem = stg[rr[0] % len(stg)]
    eng = kb.cast_engs[rr[0] % len(kb.cast_engs)]
    rr[0] += 1
    n = dst.shape[-1]
    kb.dma(st[:, 0:n], src, ssem, writes=[sbuf], q=kb.piece_q)
    if gain is None:
        if eng is kb.act:
            kb.actf(dst, st[:, 0:n], AF.Copy, [sbuf], [dst_b], add=True)
        else:
            kb.cp(eng, dst, st[:, 0:n], [sbuf], [dst_b], add=True)
    else:
        g, gbuf = gain
        if eng is kb.act:
            kb.actf(dst, st[:, 0:n], AF.Copy, [sbuf, gbuf], [dst_b], add=True, scale=g)
        else:
            kb.ts(eng, dst, st[:, 0:n], g, None, ALU.mult, None, [sbuf, gbuf], [dst_b], add=True)


def phase_mlp(kb, H_in, H_out, w_up, w_down, g_norm, final=None, ntiles=16, pre=None):
    kb.nb = 8
    T = 256
    with Ctx(kb) as cx:
        if pre is None:
            wu, wu_b = cx.sb([128, 8, DFF], BF16)
            wd, wd_b = cx.sb([128, 32, D], BF16)
            stg = make_stg(kb, cx)
            g = load_gain(kb, cx, g_norm, 8)
            pieces = mlp_weight_pieces(kb, stg, w_up, w_down, g, wu, wu_b, wd, wd_b)
        else:
            wu, wu_b, wd, wd_b, pieces = pre
        for p in pieces:
            p()
        hx = [cx.sb([128, 2, D], F32) for _ in range(3)]
        a2, a2_b = cx.sb([128, 2, D], BF16)
        a2T = [cx.sb([128, 8, T], BF16) for _ in range(2)]
        uT, uT_b = cx.sb([128, 32, T], BF16)
        rl = [cx.sb([128, 512], F32) for _ in range(3)]
        ss, ss_b = cx.sb([128, 8], F32)
        if final is not None:
            gfin, gfin_b = cx.sb([128, D], F32)
            kb.dma(gfin[:, :], final[0].partition_broadcast(128), kb.misc_sem, writes=[gfin_b])

        def load(t):
            s = t % 3
            kb.dma(hx[s][0][:, :, :], H_in[t * T:(t + 1) * T, :].rearrange("(j p) f -> p j f", p=128),
                   kb.ld_sems[s], writes=[hx[s][1]])

        def norm_part(t):
            h_t, h_b = hx[t % 3]
            o = (t % 2) * 2
            for j in range(2):
                rms_rstd(kb, h_t[:, j, :], D, a2[:, j, :], a2_b, ss, ss_b, o + j, h_b, add=(j > 0))
            rstd_finish(kb, ss, ss_b, o, o + 2, D)
            kb.ts(kb.dve, a2[:, 0, :], h_t[:, 0, :], ss[:, o:o + 1], None, ALU.mult, None, [h_b, ss_b], [a2_b])
            kb.actf(a2[:, 1, :], h_t[:, 1, :], AF.Copy, [h_b, ss_b], [a2_b], add=True, scale=ss[:, o + 1:o + 2])

        def transpose_part(t):
            aT, aT_b = a2T[t % 2]
            for j in range(2):
                bk, bk_b = kb.next_bank()
                bkv = bk[:, :].bitcast(BF16)
                for c in range(8):
                    kb.tr(bkv[:, c * 128:(c + 1) * 128], a2[:, j, c * 128:(c + 1) * 128], kb.ident[:, :],
                          [a2_b, kb.const_b], [bk_b])
                src = bkv.rearrange("p (c t) -> p c t", c=8)
                if j == 0:
                    kb.cp(kb.dve, aT[:, :, j * 128:(j + 1) * 128], src, [bk_b], [aT_b])
                else:
                    kb.actf(aT[:, :, j * 128:(j + 1) * 128], src, AF.Copy, [bk_b], [aT_b], add=True)

        load(0)
        if ntiles > 1:
            load(1)
        norm_part(0)
        transpose_part(0)
        rli = 0
        for t in range(ntiles):
            s = t % 2
            h_t, h_b = hx[t % 3]
            aT, aT_b = a2T[s]
            if t + 2 < ntiles:
                load(t + 2)
            if t + 1 < ntiles:
                norm_part(t + 1)
            for fp in range(16):
                bk, bk_b = kb.next_bank()
                for hf in range(2):
                    ft = 2 * fp + hf
                    for c in range(8):
                        kb.mm(bk[:, hf * T:(hf + 1) * T], wu[:, c, ft * 128:(ft + 1) * 128], aT[:, c, :],
                              c == 0, c == 7, [wu_b, aT_b], [bk_b])
                r_t, r_b = rl[rli % 3]
                rli += 1
                uo = uT[:, 2 * fp:2 * fp + 2, :].rearrange("p a t -> p (a t)")
                if fp % 2 == 0:
                    kb.actf(r_t[:, :], bk[:, :], AF.Relu, [bk_b], [r_b])
                    kb.tt(kb.dve, uo, r_t[:, :], r_t[:, :], ALU.mult, [r_b], [uT_b], add=(fp > 0))
                else:
                    kb.ts(kb.dve, r_t[:, :], bk[:, :], 0.0, None, ALU.max, None, [bk_b], [r_b])
                    kb.actf(uo, r_t[:, :], AF.Square, [r_b], [uT_b], add=True)
            if t + 1 < ntiles:
                transpose_part(t + 1)
            for j in range(2):
                for hf in range(2):
                    bk, bk_b = kb.next_bank()
                    for ft in range(32):
                        kb.mm(bk[:, :], uT[:, ft, j * 128:(j + 1) * 128], wd[:, ft, hf * 512:(hf + 1) * 512],
                              ft == 0, ft == 31, [uT_b, wd_b], [bk_b])
                    kb.tt(kb.dve, h_t[:, j, hf * 512:(hf + 1) * 512], bk[:, :], h_t[:, j, hf * 512:(hf + 1) * 512],
                          ALU.add, [bk_b, h_b], [h_b])
            dst = H_out[t * T:(t + 1) * T, :].rearrange("(j p) f -> p j f", p=128)
            if final is None:
                kb.dma(dst, h_t[:, :, :], kb.st_sems[t % 3], reads=[h_b])
            else:
                for j in range(2):
                    rms_rstd(kb, h_t[:, j, :], D, a2[:, j, :], a2_b, ss, ss_b, 4 + j, h_b, add=(j > 0))
                rstd_finish(kb, ss, ss_b, 4, 6, D)
                for j in range(2):
                    kb.op(kb.dve,
                          lambda h, j=j, h_t=h_t: h.scalar_tensor_tensor(
                              out=h_t[:, j, :], in0=h_t[:, j, :], scalar=ss[:, 4 + j:5 + j], in1=gfin[:, :],
                              op0=ALU.mult, op1=ALU.mult),
                          [h_b, ss_b, gfin_b], [h_b], add=True)
                kb.dma(final[1][t * T:(t + 1) * T, :].rearrange("(j p) f -> p j f", p=128), h_t[:, :, :],
                       kb.st_sems[t % 3], reads=[h_b])
        kb.barrier()
        kb.flush()


def setup_kb(nc, es, consts_bf):
    kb = KB(nc, es)
    kb.stg_sems = [kb.dsem("stg%d" % i) for i in range(6)]
    kb.ld_sems = [kb.dsem("ld%d" % i) for i in range(4)]
    kb.st_sems = [kb.dsem("st%d" % i) for i in range(8)]
    kb.misc_sem = kb.dsem("misc")
    cb = es.enter_context(nc.sbuf_tensor("constbf", [128, 128], BF16))
    kb.ident = cb
    kb.const_b = Buf()
    kb.dma(cb[:, :], consts_bf[:, 0:128], kb.misc_sem, writes=[kb.const_b])
    return kb


def host_consts():
    ident = np.eye(128, dtype=np.float32)
    return ident.astype(ml_dtypes.bfloat16)


def norm_transpose(kb, h_t, h_b, J, a, a_b, aT, aT_b, junk, junk_b, ss, ss_b):
    for j in range(J):
        rms_rstd(kb, h_t[:, j, :], D, a[:, j, :], a_b, ss, ss_b, j, h_b, add=(j > 0))
    rstd_finish(kb, ss, ss_b, 0, J, D)
    for j in range(J):
        if j % 2 == 0:
            kb.ts(kb.dve, a[:, j, :], h_t[:, j, :], ss[:, j:j + 1], None, ALU.mult, None,
                  [h_b, ss_b], [a_b], add=(j > 0))
        else:
            kb.actf(a[:, j, :], h_t[:, j, :], AF.Copy, [h_b, ss_b], [a_b], add=True, scale=ss[:, j:j + 1])
    for j in range(J):
        bk, bk_b = kb.next_bank()
        bkv = bk[:, :].bitcast(BF16)
        for c in range(8):
            kb.tr(bkv[:, c * 128:(c + 1) * 128], a[:, j, c * 128:(c + 1) * 128], kb.ident[:, :],
                  [a_b, kb.const_b], [bk_b])
        src = bkv.rearrange("p (c t) -> p c t", c=8)
        if j % 2 == 0:
            kb.cp(kb.dve, aT[:, :, j * 128:(j + 1) * 128], src, [bk_b], [aT_b], add=(j > 0))
        else:
            kb.actf(aT[:, :, j * 128:(j + 1) * 128], src, AF.Copy, [bk_b], [aT_b], add=True)


class OutStage:
    def __init__(self, kb, cx, n=6, shape=(128, 512), dt=BF16):
        self.kb = kb
        self.tiles = [cx.sb(list(shape), dt) for _ in range(n)]
        self.i = 0

    def next(self):
        k = self.i % len(self.tiles)
        self.i += 1
        t, b = self.tiles[k]
        return t, b, self.kb.st_sems[k]


def evac(kb, k, out, in_, reads, writes, add=False):
    if k % 2 == 0:
        kb.actf(out, in_, AF.Copy, reads, writes, add=add)
    else:
        kb.cp(kb.dve, out, in_, reads, writes, add=add)


def norm_stats_scale(kb, h_t, h_b, J, a, a_b, ss, ss_b, c0=0):
    for j in range(J):
        rms_rstd(kb, h_t[:, j, :], D, a[:, j, :], a_b, ss, ss_b, c0 + j, h_b, add=(j > 0))
    rstd_finish(kb, ss, ss_b, c0, c0 + J, D)
    for j in range(J):
        if j % 2 == 0:
            kb.ts(kb.dve, a[:, j, :], h_t[:, j, :], ss[:, c0 + j:c0 + j + 1], None, ALU.mult, None,
                  [h_b, ss_b], [a_b], add=(j > 0))
        else:
            kb.actf(a[:, j, :], h_t[:, j, :], AF.Copy, [h_b, ss_b], [a_b], add=True, scale=ss[:, c0 + j:c0 + j + 1])


def transpose_evac(kb, J, a, a_b, aT, aT_b):
    for j in range(J):
        bk, bk_b = kb.next_bank()
        bkv = bk[:, :].bitcast(BF16)
        for c in range(8):
            kb.tr(bkv[:, c * 128:(c + 1) * 128], a[:, j, c * 128:(c + 1) * 128], kb.ident[:, :],
                  [a_b, kb.const_b], [bk_b])
        src = bkv.rearrange("p (c t) -> p c t", c=8)
        if j % 2 == 0:
            kb.cp(kb.dve, aT[:, :, j * 128:(j + 1) * 128], src, [bk_b], [aT_b], add=(j > 0))
        else:
            kb.actf(aT[:, :, j * 128:(j + 1) * 128], src, AF.Copy, [bk_b], [aT_b], add=True)


def phase_mla_proj(kb, H_in, W, scr, rope_tab, ntiles=8):
    T = 512
    kb.nb = 8
    with Ctx(kb) as cx:
        wqa, wqa_b = cx.sb([128, 8, 384], BF16)
        wkva, wkva_b = cx.sb([128, 8, 288], BF16)
        wqb, wqb_b = cx.sb([128, 3, 1536], BF16)
        wkvb, wkvb_b = cx.sb([128, 2, 2048], BF16)
        stg = make_stg(kb, cx, 6)
        g_attn = load_gain(kb, cx, W['attn_norm'], 8)
        g_q = load_gain(kb, cx, W['q_norm'], 3)
        g_kv = load_gain(kb, cx, W['kv_norm'], 2)
        hx = [cx.sb([128, 4, D], F32) for _ in range(2)]
        cs = [cx.sb([128, 4, 64], F32) for _ in range(2)]
        a, a_b = cx.sb([128, 4, D], BF16)
        aTs = [cx.sb([128, 8, T], BF16) for _ in range(2)]
        ss, ss_b = cx.sb([128, 8], F32)
        sq = [cx.sb([128, 2], F32) for _ in range(2)]
        cqh = [cx.sb([128, 384], BF16) for _ in range(2)]
        ckvh = [cx.sb([128, 256], BF16) for _ in range(2)]
        kpe = [cx.sb([128, 32], BF16) for _ in range(2)]
        tka = [cx.sb([128, 64], F32) for _ in range(2)]
        cqT, cqT_b = cx.sb([128, 3, T], BF16)
        ckvT, ckvT_b = cx.sb([128, 2, T], BF16)
        kpeT, kpeT_b = cx.sb([32, T], BF16)
        ta = [cx.sb([128, 512], F32) for _ in range(2)]
        tb = [cx.sb([128, 512], F32) for _ in range(2)]
        qr = [cx.sb([128, 512], BF16) for _ in range(2)]
        qrT, qrT_b = cx.sb([128, 4, T], BF16)
        vsb, vsb_b = cx.sb([128, 4, D], BF16)
        ost = OutStage(kb, cx, 4)
        st = {"ek": 0}

        def load_h(t):
            s = t % 2
            kb.dma(hx[s][0][:, :, :], H_in[t * T:(t + 1) * T, :].rearrange("(j p) f -> p j f", p=128),
                   kb.ld_sems[s], writes=[hx[s][1]])

        def load_cs(t):
            s = t % 2
            kb.dma(cs[s][0][:, :, :], rope_tab[t * T:(t + 1) * T, :].rearrange("(j p) f -> p j f", p=128),
                   kb.ld_sems[2 + s], writes=[cs[s][1]])

        def load(t):
            load_h(t)
            load_cs(t)

        load(0)
        if ntiles > 1:
            load(1)
        load_cast_weight(kb, cx, stg, W['wq_a'], 0, 8, 384, wqa, wqa_b, gain=g_attn)
        load_cast_weight(kb, cx, stg, W['wkv_a'], 0, 8, 288, wkva, wkva_b, gain=g_attn)
        load_cast_weight(kb, cx, stg, W['wq_b'], 0, 3, 1536, wqb, wqb_b, gain=g_q)
        load_cast_weight(kb, cx, stg, W['wkv_b'], 0, 2, 2048, wkvb, wkvb_b, gain=g_kv)
        norm_stats_scale(kb, hx[0][0], hx[0][1], 4, a, a_b, ss, ss_b, 0)
        transpose_evac(kb, 4, a, a_b, aTs[0][0], aTs[0][1])
        for t in range(ntiles):
            s = t % 2
            cs_t, cs_b = cs[s]
            aT, aT_b = aTs[s]
            keep = {}

            def stage1(j):
                tk = slice(j * 128, (j + 1) * 128)
                p2 = j % 2
                bq, bq_b = kb.next_bank()
                for c in range(8):
                    kb.mm(bq[:, 0:384], aT[:, c, tk], wqa[:, c, :], c == 0, c == 7, [aT_b, wqa_b], [bq_b])
                bkv_, bkv_b = kb.next_bank()
                for c in range(8):
                    kb.mm(bkv_[:, 0:288], aT[:, c, tk], wkva[:, c, :], c == 0, c == 7, [aT_b, wkva_b], [bkv_b])
                sq_t, sq_b = sq[p2]
                cq_t, cq_b = cqh[p2]
                ck_t, ck_b = ckvh[p2]
                kp_t, kp_b = kpe[p2]
                tk_t, tk_b = tka[p2]
                kb.actf(cq_t[:, :], bq[:, 0:384], AF.Square, [bq_b], [cq_b, sq_b], accum_out=sq_t[:, 0:1])
                kb.actf(ck_t[:, :], bkv_[:, 0:256], AF.Square, [bkv_b], [ck_b, sq_b], accum_out=sq_t[:, 1:2], add=True)
                kb.actf(sq_t[:, 0:1], sq_t[:, 0:1], AF.Sqrt, [sq_b], [sq_b], scale=1.0 / 384, bias=EPS)
                kb.actf(sq_t[:, 1:2], sq_t[:, 1:2], AF.Sqrt, [sq_b], [sq_b], scale=1.0 / 256, bias=EPS)
                kb.op(kb.dve, lambda h: h.reciprocal(out=sq_t[:, 0:2], in_=sq_t[:, 0:2]), [sq_b], [sq_b])
                kb.ts(kb.dve, cq_t[:, :], bq[:, 0:384], sq_t[:, 0:1], None, ALU.mult, None, [bq_b, sq_b], [cq_b])
                kb.ts(kb.dve, ck_t[:, :], bkv_[:, 0:256], sq_t[:, 1:2], None, ALU.mult, None, [bkv_b, sq_b], [ck_b])
                kb.tt(kb.dve, tk_t[:, 0:32], bkv_[:, 256:288], cs_t[:, j, 0:32], ALU.mult, [bkv_b, cs_b], [tk_b])
                kb.tt(kb.dve, tk_t[:, 32:64], bkv_[:, 256:288], cs_t[:, j, 32:64], ALU.mult, [bkv_b, cs_b], [tk_b],
                      add=True)
                kb.tt(kb.dve, kp_t[:, 0:16], tk_t[:, 0:16], tk_t[:, 48:64], ALU.subtract, [tk_b], [kp_b])
                kb.tt(kb.dve, kp_t[:, 16:32], tk_t[:, 32:48], tk_t[:, 16:32], ALU.add, [tk_b], [kp_b], add=True)

            def stage2(j):
                tk = slice(j * 128, (j + 1) * 128)
                p2 = j % 2
                cq_t, cq_b = cqh[p2]
                ck_t, ck_b = ckvh[p2]
                kp_t, kp_b = kpe[p2]
                bt, bt_b = kb.next_bank()
                btv = bt[:, :].bitcast(BF16)
                for c in range(3):
                    kb.tr(btv[:, c * 128:(c + 1) * 128], cq_t[:, c * 128:(c + 1) * 128], kb.ident[:, :],
                          [cq_b, kb.const_b], [bt_b])
                for c in range(2):
                    kb.tr(btv[:, 384 + c * 128:384 + (c + 1) * 128], ck_t[:, c * 128:(c + 1) * 128], kb.ident[:, :],
                          [ck_b, kb.const_b], [bt_b])
                kb.tr(btv[0:32, 640:768], kp_t[:, :], kb.ident[:, :], [kp_b, kb.const_b], [bt_b])
                kb.actf(cqT[:, :, tk], btv[:, 0:384].rearrange("p (c t) -> p c t", c=3), AF.Copy, [bt_b], [cqT_b],
                        add=(j > 0))
                kb.actf(ckvT[:, :, tk], btv[:, 384:640].rearrange("p (c t) -> p c t", c=2), AF.Copy, [bt_b], [ckvT_b],
                        add=(j > 0))
                kb.actf(kpeT[:, tk], btv[0:32, 640:768], AF.Copy, [bt_b], [kpeT_b], add=(j > 0))
                br, br_b = kb.next_bank()
                for c in range(3):
                    kb.mm(br[:, :], cqT[:, c, tk], wqb[:, c, 1024:1536], c == 0, c == 2, [cqT_b, wqb_b], [br_b])
                ta_t, ta_b = ta[p2]
                tb_t, tb_b = tb[p2]
                qr_t, qr_b = qr[p2]
                x3 = br[:, :].rearrange("p (h e) -> p h e", h=16)
                cc = cs_t[:, j:j + 1, 0:32].broadcast_to([128, 16, 32])
                sn = cs_t[:, j:j + 1, 32:64].broadcast_to([128, 16, 32])
                ta3 = ta_t[:, :].rearrange("p (h e) -> p h e", h=16)
                tb3 = tb_t[:, :].rearrange("p (h e) -> p h e", h=16)
                qr3 = qr_t[:, :].rearrange("p (h e) -> p h e", h=16)
                kb.tt(kb.dve, ta3, x3, cc, ALU.mult, [br_b, cs_b], [ta_b])
                kb.tt(kb.dve, tb3, x3, sn, ALU.mult, [br_b, cs_b], [tb_b])
                kb.tt(kb.pool, qr3[:, :, 0:16], ta3[:, :, 0:16], tb3[:, :, 16:32], ALU.subtract, [ta_b, tb_b], [qr_b])
                kb.tt(kb.dve, qr3[:, :, 16:32], tb3[:, :, 0:16], ta3[:, :, 16:32], ALU.add, [ta_b, tb_b], [qr_b],
                      add=True)

            def stage3(j):
                tk = slice(j * 128, (j + 1) * 128)
                qr_t, qr_b = qr[j % 2]
                bt2, bt2_b = kb.next_bank()
                bt2v = bt2[:, :].bitcast(BF16)
                for c in range(4):
                    kb.tr(bt2v[:, c * 128:(c + 1) * 128], qr_t[:, c * 128:(c + 1) * 128], kb.ident[:, :],
                          [qr_b, kb.const_b], [bt2_b])
                kb.actf(qrT[:, :, tk], bt2v[:, 0:512].rearrange("p (c t) -> p c t", c=4), AF.Copy, [bt2_b], [qrT_b],
                        add=(j > 0))

            if t + 1 < ntiles:
                norm_stats_scale(kb, hx[1 - s][0], hx[1 - s][1], 4, a, a_b, ss, ss_b, 4 * (1 - s))
            if t + 2 < ntiles:
                load_h(t + 2)
            for step in range(6):
                if step < 4:
                    stage1(step)
                if 0 <= step - 1 < 4:
                    stage2(step - 1)
                if 0 <= step - 2 < 4:
                    stage3(step - 2)
            tcol = slice(t * T, (t + 1) * T)
            for ft in range(8):
                bk, bk_b = kb.next_bank()
                for c in range(3):
                    kb.mm(bk[:, :], wqb[:, c, ft * 128:(ft + 1) * 128], cqT[:, c, :], c == 0, c == 2,
                          [wqb_b, cqT_b], [bk_b])
                o_t, o_b, o_s = ost.next()
                evac(kb, st["ek"], o_t[:, :], bk[:, :], [bk_b], [o_b]); st["ek"] += 1
                kb.dma(scr['QN'][ft * 128:(ft + 1) * 128, tcol], o_t[:, :], o_s, reads=[o_b])
            for ft in range(8):
                bk, bk_b = kb.next_bank()
                for c in range(2):
                    kb.mm(bk[:, :], wkvb[:, c, ft * 128:(ft + 1) * 128], ckvT[:, c, :], c == 0, c == 1,
                          [wkvb_b, ckvT_b], [bk_b])
                o_t, o_b, o_s = ost.next()
                evac(kb, st["ek"], o_t[:, :], bk[:, :], [bk_b], [o_b]); st["ek"] += 1
                kb.dma(scr['KN'][ft * 128:(ft + 1) * 128, tcol], o_t[:, :], o_s, reads=[o_b])
            for j in range(4):
                for hf in range(2):
                    bk, bk_b = kb.next_bank()
                    for c in range(2):
                        kb.mm(bk[:, :], ckvT[:, c, j * 128:(j + 1) * 128],
                              wkvb[:, c, 1024 + hf * 512:1024 + (hf + 1) * 512], c == 0, c == 1,
                              [wkvb_b, ckvT_b], [bk_b])
                    evac(kb, st["ek"], vsb[:, j, hf * 512:(hf + 1) * 512], bk[:, :], [bk_b], [vsb_b], add=(j + hf > 0))
                    st["ek"] += 1
            if t + 1 < ntiles:
                transpose_evac(kb, 4, a, a_b, aTs[1 - s][0], aTs[1 - s][1])
            kb.dma(scr['V'][tcol, :].rearrange("(j p) f -> p j f", p=128), vsb[:, :, :], kb.st_sems[4], reads=[vsb_b])
            kb.dma(scr['QR'].rearrange("(c p) t -> p c t", p=128)[:, :, tcol], qrT[:, :, :], kb.st_sems[5],
                   reads=[qrT_b])
            kb.dma(scr['KPE'][:, tcol], kpeT[:, :], kb.st_sems[6], reads=[kpeT_b])
            if t + 2 < ntiles:
                load_cs(t + 2)
        kb.barrier()
        kb.flush()
    kb.nb = 6


def phase_attn(kb, scr, KD, scale, loader, nheads=16, nqt=8, DEPTH=3, pieces=None):
    kb.nb = 6
    from collections import deque
    with Ctx(kb) as cx:
        QT = [cx.sb([KD, S], BF16) for _ in range(2)]
        KT = [cx.sb([KD, S], BF16) for _ in range(2)]
        VA = [cx.sb([128, 32, 128], BF16) for _ in range(2)]
        pT = [cx.sb([128, 1024], BF16) for _ in range(4)]
        rd, rd_b = cx.sb([128, 512], F32)
        osb = [cx.sb([64, 512], BF16) for _ in range(2)]
        mask, mask_b = cx.sb([128, 128], BF16)
        kb.dma(mask[:, :], scr['cbf'][:, 128:256], kb.misc_sem, writes=[mask_b])
        for i in range(2):
            kb.op(kb.pool, lambda h, i=i: h.memset(VA[i][0][:, :, 64:128], 1.0), [], [VA[i][1]])

        def load(h):
            s = h % 2
            loader(kb, h, QT[s], KT[s], kb.at_sems[s], kb.at_sems[2 + s])
            kb.dma(VA[s][0][:, :, 0:64], scr['V'][:, h * 64:(h + 1) * 64].rearrange("(c p) d -> p c d", p=128),
                   kb.at_sems[4 + s], writes=[VA[s][1]], add=True)

        items = []
        for h in range(nheads):
            for qt in range(nqt):
                first = True
                for kc in range(0, 4 * qt, 2):
                    items.append((h, qt, 'pair', kc, first))
                    first = False
                for c in range(4):
                    items.append((h, qt, 'diag', c, first))
                    first = False

        state = {"wi": 0, "pi": 0}

        def emit(item):
            h, qt, kind, k, first = item
            s = h % 2
            q_t, q_b = QT[s]
            k_t, k_b = KT[s]
            v_t, v_b = VA[s]
            oi = 6 + (qt % 2)
            oacc, oacc_b = kb.banks[oi], kb.bank_bufs[oi]
            w = state["wi"] % 3
            state["wi"] += 1
            W = kb.wide[w]
            b0, b1 = kb.bank_bufs[2 * w], kb.bank_bufs[2 * w + 1]
            p_t, p_b = pT[state["pi"] % 4]
            state["pi"] += 1
            qs = slice(qt * 512, (qt + 1) * 512)
            if kind == 'pair':
                for u in range(2):
                    kcs = slice((k + u) * 128, (k + u + 1) * 128)
                    kb.mm(W[:, u * 512:(u + 1) * 512], k_t[:, kcs], q_t[:, qs], True, True, [k_b, q_b], [(b0, b1)[u]])
                kb.actf(p_t[:, :], W[:, :], AF.Exp, [b0, b1], [p_b], scale=scale)

                def pv():
                    for u in range(2):
                        kb.mm(oacc[:, :], v_t[:, k + u, :], p_t[:, u * 512:(u + 1) * 512], first and u == 0, False,
                              [v_b, p_b], [oacc_b])
                return pv
            c = k
            kc = 4 * qt + c
            kcs = slice(kc * 128, (kc + 1) * 128)
            c0 = c * 128
            q0 = qt * 512 + c0
            kb.mm(W[:, c0:c0 + 128], k_t[:, kcs], q_t[:, q0:q0 + 128], True, False, [k_b, q_b], [b0])
            kb.mm(W[:, c0:c0 + 128], kb.ident[:, :], mask[:, :], False, True, [kb.const_b, mask_b], [b0])
            if c < 3:
                kb.mm(W[:, c0 + 128:512], k_t[:, kcs], q_t[:, q0 + 128:(qt + 1) * 512], True, True, [k_b, q_b], [b0])
            kb.actf(p_t[:, c0:512], W[:, c0:512], AF.Exp, [b0], [p_b], scale=scale)

            def pv():
                kb.mm(oacc[:, c0:512], v_t[:, kc, :], p_t[:, c0:512], first, c == 3, [v_b, p_b], [oacc_b])
                if c == 3:
                    kb.op(kb.dve, lambda h_: h_.reciprocal(out=rd[64:128, :], in_=oacc[64:128, :]), [oacc_b], [rd_b])
                    o_t, o_b = osb[qt % 2]
                    kb.tt(kb.dve, o_t[:, :], oacc[0:64, :], rd[64:128, :], ALU.mult, [oacc_b, rd_b], [o_b])
                    kb.dma(scr['OT'][h * 64:(h + 1) * 64, qt * 512:(qt + 1) * 512], o_t[:, :], kb.st_sems[qt % 2],
                           reads=[o_b])
            return pv

        load(0)
        pend = deque()
        cur_h, since = -1, 0
        for it in items:
            if it[0] != cur_h:
                cur_h, since = it[0], 0
            since += 1
            if since == DEPTH + 2 and cur_h + 1 < nheads:
                load(cur_h + 1)
            if since in (DEPTH + 6, DEPTH + 26, DEPTH + 46, DEPTH + 66):
                take(pieces, 1)
            pend.append(emit(it))
            if len(pend) > DEPTH:
                pend.popleft()()
        while pend:
            pend.popleft()()
        kb.barrier()
        kb.flush()


def mla_loader(scr):
    def f(kb, h, QTt, KTt, qs, ks):
        q_t, q_b = QTt
        k_t, k_b = KTt
        kb.dma(q_t[0:64, :], scr['QN'][h * 64:(h + 1) * 64, :], qs, writes=[q_b])
        kb.dma(q_t[64:96, :], scr['QR'][h * 32:(h + 1) * 32, :], qs, writes=[q_b], add=True)
        kb.dma(k_t[0:64, :], scr['KN'][h * 64:(h + 1) * 64, :], ks, writes=[k_b])
        kb.dma(k_t[64:96, :], scr['KPE'][:, :], ks, writes=[k_b], add=True)
    return f


def fox_loader(scr):
    def f(kb, h, QTt, KTt, qs, ks):
        q_t, q_b = QTt
        k_t, k_b = KTt
        kb.dma(q_t[0:64, :], scr['QN'][h * 64:(h + 1) * 64, :], qs, writes=[q_b])
        kb.dma(q_t[64:70, :], scr['QAUG'][:, h, :], qs, writes=[q_b], add=True)
        kb.dma(k_t[0:64, :], scr['KN'][h * 64:(h + 1) * 64, :], ks, writes=[k_b])
        kb.dma(k_t[64:70, :], scr['KAUG'][:, h, :], ks, writes=[k_b], add=True)
    return f


def take(pieces, n):
    if pieces is None:
        return
    for _ in range(n):
        p = next(pieces, None)
        if p is None:
            return
        p()


def phase_oproj(kb, H_in, H_out, wo_d, scr, ntiles=8, stg=None, pieces=None):
    kb.nb = 8
    T = 512
    with Ctx(kb) as cx:
        wo, wo_b = cx.sb([128, 8, D], BF16)
        if stg is None:
            stg = make_stg(kb, cx)
        hx = [cx.sb([128, 4, D], F32) for _ in range(2)]
        ot = [cx.sb([128, 8, T], BF16) for _ in range(2)]

        def load(t):
            s = t % 2
            kb.dma(hx[s][0][:, :, :], H_in[t * T:(t + 1) * T, :].rearrange("(j p) f -> p j f", p=128),
                   kb.ld_sems[s], writes=[hx[s][1]])
            kb.dma(ot[s][0][:, :, :], scr['OT'].rearrange("(c p) t -> p c t", p=128)[:, :, t * T:(t + 1) * T],
                   kb.ld_sems[2 + s], writes=[ot[s][1]])

        load(0)
        load_cast_weight(kb, cx, stg, wo_d, 0, 8, D, wo, wo_b)
        for t in range(ntiles):
            s = t % 2
            h_t, h_b = hx[s]
            o_t, o_b = ot[s]
            if t + 1 < ntiles:
                load(t + 1)
            take(pieces, 8)
            for j in range(4):
                for hf in range(2):
                    bk, bk_b = kb.next_bank()
                    for c in range(8):
                        kb.mm(bk[:, :], o_t[:, c, j * 128:(j + 1) * 128], wo[:, c, hf * 512:(hf + 1) * 512],
                              c == 0, c == 7, [o_b, wo_b], [bk_b])
                    kb.tt(kb.dve, h_t[:, j, hf * 512:(hf + 1) * 512], bk[:, :], h_t[:, j, hf * 512:(hf + 1) * 512],
                          ALU.add, [bk_b, h_b], [h_b])
            kb.dma(H_out[t * T:(t + 1) * T, :].rearrange("(j p) f -> p j f", p=128), h_t[:, :, :], kb.st_sems[s],
                   reads=[h_b])
        kb.barrier()
        kb.flush()


LAYER_KINDS = ["mla", "fox", "dil", "mla"]
WSHAPES = {
    "mla": [("attn_norm", [D]), ("mla_wq_a", [D, 384]), ("mla_q_norm", [384]), ("mla_wq_b", [384, 1536]),
            ("mla_wkv_a", [D, 288]), ("mla_kv_norm", [256]), ("mla_wkv_b", [256, 2048]), ("mla_wo", [D, D])],
    "fox": [("attn_norm", [D]), ("fox_w_qkv", [D, 3072]), ("fox_w_f", [D, 16]), ("fox_b_f", [16]),
            ("fox_wo", [D, D])],
    "dil": [("attn_norm", [D]), ("dil_w_qkv", [D, 9216]), ("dil_wo", [D, D])],
}
MLP_W = [("mlp_norm", [D]), ("w_up", [D, DFF]), ("w_down", [DFF, D])]
CBF_W = 128 + 128 + 256 + 256


def build_program(nlayers=4, stop_after=None):
    nc = bass.Bass("TRN2", target_bir_lowering=False)
    inp = {}

    def din(name, shape, dt=F32):
        inp[name] = nc.dram_tensor(name, shape, dt, kind="ExternalInput").ap()
        return inp[name]

    x = din("x", [S, D])
    for i in range(nlayers):
        for n, shp in WSHAPES[LAYER_KINDS[i]] + MLP_W:
            din("l%d_%s" % (i, n), shp)
    din("final_norm", [D])
    cbf = din("cbf", [128, CBF_W], BF16)
    rope_mla = din("rope_mla", [S, 64])
    rope_dil = din("rope_dil", [3, 2, 32, S])
    cf32 = din("cf32", [128, 128])
    y = nc.dram_tensor("y", [S, D], F32, kind="ExternalOutput").ap()

    def scratch(name, shape, dt):
        return nc.dram_tensor(name, shape, dt, kind="Internal").ap()

    scr = {
        "cbf": cbf, "cf32": cf32,
        "H": scratch("H", [S, D], F32),
        "QN": scratch("QN", [1024, S], BF16), "QR": scratch("QR", [512, S], BF16),
        "KN": scratch("KN", [1024, S], BF16), "KPE": scratch("KPE", [32, S], BF16),
        "V": scratch("V", [S, 1024], BF16), "OT": scratch("OT", [1024, S], BF16),
        "QAUG": scratch("QAUG", [6, 16, S], BF16), "KAUG": scratch("KAUG", [6, 16, S], BF16),
        "LSP": scratch("LSP", [16, S], F32),
        "QG": scratch("QG", [3, 4, 2, 128, S], BF16), "KG": scratch("KG", [3, 4, 2, 128, S], BF16),
        "VG": scratch("VG", [3, S, 1024], BF16), "OG": scratch("OG", [3, S, 16 * 65], F32),
    }
    with ExitStack() as es:
        kb = setup_kb(nc, es, cbf)
        kb.at_sems = [kb.dsem("at%d" % i) for i in range(6)]
        H = scr["H"]
        for i in range(nlayers):
            kind = LAYER_KINDS[i]
            W = {n: inp["l%d_%s" % (i, n)] for n, _ in WSHAPES[kind] + MLP_W}
            h_in = x if i == 0 else H
            last = (i == nlayers - 1)
            if kind == "mla":
                Wm = {"attn_norm": W["attn_norm"], "wq_a": W["mla_wq_a"], "q_norm": W["mla_q_norm"],
                      "wq_b": W["mla_wq_b"], "wkv_a": W["mla_wkv_a"], "kv_norm": W["mla_kv_norm"],
                      "wkv_b": W["mla_wkv_b"]}
                phase_mla_proj(kb, h_in, Wm, scr, rope_mla)
            elif kind == "fox":
                phase_fox_proj(kb, h_in, W, scr)
            else:
                phase_dil_proj(kb, h_in, W, scr, rope_dil)
            with Ctx(kb) as wcx:
                wu, wu_b = wcx.sb([128, 8, DFF], BF16)
                wd, wd_b = wcx.sb([128, 32, D], BF16)
                stg = make_stg(kb, wcx)
                g = load_gain(kb, wcx, W["mlp_norm"], 8)
                pieces = mlp_weight_pieces(kb, stg, W["w_up"], W["w_down"], g, wu, wu_b, wd, wd_b)
                kb.cast_engs = (kb.pool,)
                kb.piece_q = kb.pool
                if kind == "mla":
                    phase_attn(kb, scr, 96, 96 ** -0.5, mla_loader(scr), pieces=pieces)
                elif kind == "fox":
                    phase_attn(kb, scr, 70, 0.125, fox_loader(scr), pieces=pieces)
                else:
                    phase_dil_attn(kb, scr)
                kb.cast_engs = (kb.dve, kb.act)
                kb.piece_q = None
                if kind == "dil":
                    kb.piece_q = kb.pool
                    phase_dil_oproj(kb, h_in, H, W["dil_wo"], scr, stg=stg, pieces=pieces)
                    kb.piece_q = None
                else:
                    phase_oproj(kb, h_in, H, W[kind + "_wo"], scr, stg=stg, pieces=pieces)
                if stop_after == (i, "attn"):
                    break
                phase_mlp(kb, H, y if last else H, W["w_up"], W["w_down"], W["mlp_norm"],
                          final=(inp["final_norm"], y) if last else None, pre=(wu, wu_b, wd, wd_b, pieces))
        if stop_after is not None:
            copy_out(kb, H, y)
    return nc


def copy_out(kb, H, y):
    with Ctx(kb) as cx:
        t, b = cx.sb([128, 8, D], F32)
        for i in range(4):
            kb.dma(t[:, :, :], H[i * 1024:(i + 1) * 1024, :].rearrange("(j p) f -> p j f", p=128), kb.ld_sems[0],
                   writes=[b])
            kb.dma(y[i * 1024:(i + 1) * 1024, :].rearrange("(j p) f -> p j f", p=128), t[:, :, :], kb.st_sems[0],
                   reads=[b])
        kb.barrier()
        kb.flush()


def dil_perm(g):
    d = (1, 4, 16)[g]
    n = np.arange(S)
    L = S // d
    return (n // L) + d * (n % L)


def host_constants():
    bf = ml_dtypes.bfloat16
    k = np.arange(128)[:, None]
    q = np.arange(128)[None, :]
    NEG = -30000.0
    ident = np.eye(128, dtype=np.float32)
    causal = np.where(k <= q, 0.0, NEG).astype(np.float32)
    prev = np.where(k >= q, 0.0, NEG).astype(np.float32)
    prev01 = (k >= q).astype(np.float32)
    diag01 = (k <= q).astype(np.float32)
    cbf = np.concatenate([ident, causal, prev, causal, prev01, diag01], axis=1).astype(bf)
    t = np.arange(S, dtype=np.float32)[:, None]
    inv = (1.0 / (np.float32(10000.0) ** (np.arange(0, 32, 2, dtype=np.float32) / np.float32(32)))).astype(np.float32)
    ang = (t * inv[None, :]).astype(np.float32)
    c, s = np.cos(ang).astype(np.float32), np.sin(ang).astype(np.float32)
    rope_mla = np.concatenate([c, c, s, s], axis=1).astype(np.float32)
    inv2 = (1.0 / (np.float32(10000.0) ** (np.arange(0, 64, 2, dtype=np.float32) / np.float32(64)))).astype(np.float32)
    ang2 = (t * inv2[None, :]).astype(np.float32)
    c2, s2 = np.cos(ang2).astype(np.float32), np.sin(ang2).astype(np.float32)
    rope_dil = np.zeros((3, 2, 32, S), np.float32)
    for g in range(3):
        p = dil_perm(g)
        rope_dil[g, 0] = c2[p].T
        rope_dil[g, 1] = s2[p].T
    kk = np.arange(128)[:, None]
    mm_ = np.arange(128)[None, :]
    segtri = ((kk // 8 == mm_ // 8) & (kk % 8 < mm_ % 8)).astype(np.float32)
    return {"cbf": cbf, "rope_mla": rope_mla, "rope_dil": rope_dil, "cf32": segtri}


def host_weights(inputs, nlayers=4):
    out = {}
    for i in range(nlayers):
        kind = LAYER_KINDS[i]
        for n, _ in WSHAPES[kind] + MLP_W:
            key = "l%d_%s" % (i, n)
            w = np.asarray(inputs[key], dtype=np.float32)
            if n == "mla_wq_b":
                w3 = w.reshape(384, 16, 96)
                w = np.concatenate([w3[:, :, :64].reshape(384, 1024), w3[:, :, 64:].reshape(384, 512)], axis=1)
            elif n == "mla_wkv_b":
                w3 = w.reshape(256, 16, 128)
                w = np.concatenate([w3[:, :, :64].reshape(256, 1024), w3[:, :, 64:].reshape(256, 1024)], axis=1)
            elif n == "dil_w_qkv":
                w6 = w.reshape(D, 3, 3, 4, 4, 2, 32)
                qk = w6[:, 0:2].transpose(0, 1, 2, 3, 5, 4, 6)
                w = np.concatenate([qk.reshape(D, 2 * 3072), w6[:, 2].reshape(D, 3072)], axis=1)
            out[key] = np.ascontiguousarray(w)
    out["final_norm"] = np.asarray(inputs["final_norm"], dtype=np.float32)
    return out


INPUT_NAMES = (
    "x", "l0_attn_norm", "l0_mla_wq_a", "l0_mla_q_norm",
    "l0_mla_wq_b", "l0_mla_wkv_a", "l0_mla_kv_norm", "l0_mla_wkv_b",
    "l0_mla_wo", "l0_mlp_norm", "l0_w_up", "l0_w_down",
    "l1_attn_norm", "l1_fox_w_qkv", "l1_fox_w_f", "l1_fox_b_f",
    "l1_fox_wo", "l1_mlp_norm", "l1_w_up", "l1_w_down",
    "l2_attn_norm", "l2_dil_w_qkv", "l2_dil_wo", "l2_mlp_norm",
    "l2_w_up", "l2_w_down", "l3_attn_norm", "l3_mla_wq_a",
    "l3_mla_q_norm", "l3_mla_wq_b", "l3_mla_wkv_a", "l3_mla_kv_norm",
    "l3_mla_wkv_b", "l3_mla_wo", "l3_mlp_norm", "l3_w_up",
    "l3_w_down", "final_norm",
)


_PROG = {}


def kernel(**inputs):
    missing = [n for n in INPUT_NAMES if n not in inputs]
    assert not missing, missing
    if "full" not in _PROG:
        _PROG["full"] = build_program()
    nc = _PROG["full"]
    consts = host_constants()
    wts = host_weights(inputs)
    x = np.asarray(inputs["x"], dtype=np.float32)
    in_maps = []
    for b in range(NCORES):
        m = dict(wts)
        m.update(consts)
        m["x"] = np.ascontiguousarray(x[b])
        in_maps.append(m)
    res = run_bass_kernel_spmd(nc, in_maps, core_ids=list(range(NCORES)))
    return np.stack([np.asarray(r["y"], dtype=np.float32) for r in res.results], axis=0)


def phase_fox_proj(kb, H_in, W, scr, ntiles=8):
    kb.nb = 8
    T = 512
    with Ctx(kb) as cx:
        wqkv, wqkv_b = cx.sb([128, 8, 3072], BF16)
        wf, wf_b = cx.sb([128, 8, 16], BF16)
        stg = make_stg(kb, cx, 6)
        g_attn = load_gain(kb, cx, W['attn_norm'], 8)
        negb, negb_b = cx.sb([16, 1], F32)
        kb.dma(negb[:, :], W['fox_b_f'].rearrange("(p o) -> p o", o=1), kb.misc_sem, writes=[negb_b],
               allow_slow_non_contiguous=True)
        kb.ts(kb.dve, negb[:, :], negb[:, :], -1.0, None, ALU.mult, None, [negb_b], [negb_b])
        hx = [cx.sb([128, 4, D], F32) for _ in range(2)]
        a, a_b = cx.sb([128, 4, D], BF16)
        aTs = [cx.sb([128, 8, T], BF16) for _ in range(2)]
        ss, ss_b = cx.sb([128, 8], F32)
        vsb, vsb_b = cx.sb([128, 4, D], BF16)
        ef, ef_b = cx.sb([16, T], F32)
        lsp = [cx.sb([16, T], F32) for _ in range(2)]
        ost = OutStage(kb, cx, 4)
        ek = 0

        def load(t):
            s = t % 2
            kb.dma(hx[s][0][:, :, :], H_in[t * T:(t + 1) * T, :].rearrange("(j p) f -> p j f", p=128),
                   kb.ld_sems[s], writes=[hx[s][1]])

        load(0)
        if ntiles > 1:
            load(1)
        load_cast_weight(kb, cx, stg, W['fox_w_qkv'], 0, 8, 3072, wqkv, wqkv_b, gain=g_attn)
        load_cast_weight(kb, cx, stg, W['fox_w_f'], 0, 8, 16, wf, wf_b, gain=g_attn)
        norm_stats_scale(kb, hx[0][0], hx[0][1], 4, a, a_b, ss, ss_b, 0)
        transpose_evac(kb, 4, a, a_b, aTs[0][0], aTs[0][1])
        for t in range(ntiles):
            s = t % 2
            aT, aT_b = aTs[s]
            if t + 1 < ntiles:
                norm_stats_scale(kb, hx[1 - s][0], hx[1 - s][1], 4, a, a_b, ss, ss_b, 4 * (1 - s))
            if t + 2 < ntiles:
                load(t + 2)
            tcol = slice(t * T, (t + 1) * T)
            for which, dst in ((0, 'QN'), (1, 'KN')):
                for ft in range(8):
                    bk, bk_b = kb.next_bank()
                    c0 = which * 1024 + ft * 128
                    for c in range(8):
                        kb.mm(bk[:, :], wqkv[:, c, c0:c0 + 128], aT[:, c, :], c == 0, c == 7, [wqkv_b, aT_b], [bk_b])
                    o_t, o_b, o_s = ost.next()
                    evac(kb, ek, o_t[:, :], bk[:, :], [bk_b], [o_b]); ek += 1
                    kb.dma(scr[dst][ft * 128:(ft + 1) * 128, tcol], o_t[:, :], o_s, reads=[o_b])
            for j in range(4):
                for hf in range(2):
                    bk, bk_b = kb.next_bank()
                    for c in range(8):
                        kb.mm(bk[:, :], aT[:, c, j * 128:(j + 1) * 128], wqkv[:, c, 2048 + hf * 512:2048 + (hf + 1) * 512],
                              c == 0, c == 7, [wqkv_b, aT_b], [bk_b])
                    evac(kb, ek, vsb[:, j, hf * 512:(hf + 1) * 512], bk[:, :], [bk_b], [vsb_b], add=(j + hf > 0)); ek += 1
            kb.dma(scr['V'][tcol, :].rearrange("(j p) f -> p j f", p=128), vsb[:, :, :], kb.st_sems[4], reads=[vsb_b])
            bk, bk_b = kb.next_bank()
            for c in range(8):
                kb.mm(bk[0:16, :], wf[:, c, :], aT[:, c, :], c == 0, c == 7, [wf_b, aT_b], [bk_b])
            kb.actf(ef[:, :], bk[0:16, :], AF.Exp, [bk_b, negb_b], [ef_b], scale=-1.0, bias=negb[:, 0:1])
            l_t, l_b = lsp[s]
            kb.actf(l_t[:, :], ef[:, :], AF.Ln, [ef_b], [l_b], bias=1.0)
            kb.dma(scr['LSP'][:, tcol], l_t[:, :], kb.st_sems[5 + s], reads=[l_b])
            if t + 1 < ntiles:
                transpose_evac(kb, 4, a, a_b, aTs[1 - s][0], aTs[1 - s][1])
        kb.barrier()
        kb.flush()
    with Ctx(kb) as cx:
        def seg(ap2d):
            return ap2d.rearrange("h (s t) -> (h s) t", s=8)
        L, L_b = cx.sb([128, 512], F32)
        R, R_b = cx.sb([128, 512], F32)
        t2, t2_b = cx.sb([128, 512], F32)
        tri, tri_b = cx.sb([128, 128], F32)
        off, off_b = cx.sb([128, 1], F32)
        P = [cx.sb([128, 512], BF16) for _ in range(3)]
        Nn = [cx.sb([128, 512], BF16) for _ in range(3)]
        ones, ones_b = cx.sb([128, 512], BF16)
        kb.dma(L[:, :], seg(scr['LSP'][:, :]), kb.ld_sems[0], writes=[L_b])
        kb.dma(tri[:, :], scr['cf32'][:, :], kb.ld_sems[1], writes=[tri_b])
        kb.op(kb.pool, lambda h: h.memset(ones[:, :], 1.0), [], [ones_b])
        kb.op(kb.dve, lambda h: h.tensor_tensor_scan(out=R[:, :], data0=L[:, :], data1=L[:, :], initial=0.0,
                                                     op0=ALU.add, op1=ALU.add), [L_b], [R_b])
        bk, bk_b = kb.next_bank()
        kb.mm(bk[:, 0:1], tri[:, :], R[:, 511:512], True, True, [tri_b, R_b], [bk_b])
        kb.cp(kb.dve, off[:, :], bk[:, 0:1], [bk_b], [off_b])
        kb.ts(kb.dve, R[:, :], R[:, :], off[:, 0:1], -4.0, ALU.add, ALU.mult, [R_b, off_b], [R_b])
        for k in range(3):
            p_t, p_b = P[k]
            n_t, n_b = Nn[k]
            kb.cp(kb.dve, p_t[:, :], R[:, :], [R_b], [p_b])
            kb.dma(seg(scr['QAUG'][k, :, :]), p_t[:, :], kb.st_sems[k], reads=[p_b])
            kb.ts(kb.pool, n_t[:, :], p_t[:, :], -1.0, None, ALU.mult, None, [p_b], [n_b])
            kb.dma(seg(scr['KAUG'][3 + k, :, :]), n_t[:, :], kb.st_sems[3 + k], reads=[n_b])
            kb.dma(seg(scr['QAUG'][3 + k, :, :]), ones[:, :], kb.st_sems[6], reads=[ones_b])
            kb.dma(seg(scr['KAUG'][k, :, :]), ones[:, :], kb.st_sems[7], reads=[ones_b])
            if k < 2:
                kb.cp(kb.dve, t2[:, :], p_t[:, :], [p_b], [t2_b])
                kb.tt(kb.dve, R[:, :], R[:, :], t2[:, :], ALU.subtract, [R_b, t2_b], [R_b])
        kb.barrier()
        kb.flush()


DIL_D = (1, 4, 16)


def phase_dil_proj(kb, H_in, W, scr, rope_dil, ntiles=8):
    kb.nb = 8
    T = 512
    with Ctx(kb) as cx:
        wsets = [[cx.sb([128, 8, 1024], BF16) for _ in range(3)] for _ in range(2)]
        stg = make_stg(kb, cx, 4)
        g_attn = load_gain(kb, cx, W['attn_norm'], 8)
        wall = W['dil_w_qkv']
        hx = [cx.sb([128, 4, D], F32) for _ in range(2)]
        ct = [cx.sb([128, T], F32) for _ in range(2)]
        st_ = [cx.sb([128, T], F32) for _ in range(2)]
        a, a_b = cx.sb([128, 4, D], BF16)
        aTs = [cx.sb([128, 8, T], BF16) for _ in range(2)]
        ss, ss_b = cx.sb([128, 8], F32)
        vsb, vsb_b = cx.sb([128, 4, D], BF16)
        m8 = [cx.sb([128, T], F32) for _ in range(8)]
        mi = [0]
        ost = OutStage(kb, cx, 4)

        def weight_pieces(g):
            ws = wsets[g % 2]
            for which in range(3):
                w_t, w_b = ws[which]
                col0 = which * 3072 + g * 1024
                for c in range(8):
                    yield lambda c=c, w_t=w_t, w_b=w_b, col0=col0: load_cast_piece(
                        kb, stg, wall[c * 128:(c + 1) * 128, col0:col0 + 1024], w_t[:, c, :], w_b,
                        (g_attn[0][:, c:c + 1], g_attn[1]))

        seq = [(g, t) for g in range(3) for t in range(ntiles)]

        def load(i):
            g, t = seq[i]
            d = DIL_D[g]
            L = S // d
            Hg = H_in.rearrange("(i r) f -> r i f", r=d)
            s = i % 2
            for j in range(4):
                n = t * T + j * 128
                kb.dma(hx[s][0][:, j, :], Hg[n // L, (n % L):(n % L) + 128, :], kb.ld_sems[s],
                       writes=[hx[s][1]], add=(j > 0))
            for rep in range(4):
                kb.dma(ct[s][0][rep * 32:(rep + 1) * 32, :], rope_dil[g, 0, :, t * T:(t + 1) * T], kb.ld_sems[2 + s],
                       writes=[ct[s][1]], add=(rep > 0))
                kb.dma(st_[s][0][rep * 32:(rep + 1) * 32, :], rope_dil[g, 1, :, t * T:(t + 1) * T], kb.ld_sems[2 + s],
                       writes=[st_[s][1]], add=True)

        load(0)
        if len(seq) > 1:
            load(1)
        kb.piece_q = kb.pool
        for p in weight_pieces(0):
            p()
        norm_stats_scale(kb, hx[0][0], hx[0][1], 4, a, a_b, ss, ss_b, 0)
        transpose_evac(kb, 4, a, a_b, aTs[0][0], aTs[0][1])
        nxt = None
        for i, (g, t) in enumerate(seq):
            s = i % 2
            if t == 0:
                nxt = weight_pieces(g + 1) if g < 2 else None
            (wq, wq_b), (wk, wk_b), (wv, wv_b) = wsets[g % 2]
            aT, aT_b = aTs[s]
            C, C_b = ct[s]
            Sn, Sn_b = st_[s]
            if i + 1 < len(seq):
                norm_stats_scale(kb, hx[1 - s][0], hx[1 - s][1], 4, a, a_b, ss, ss_b, 4 * (1 - s))
            take(nxt, 3)
            tcol = slice(t * T, (t + 1) * T)
            for w_t, w_b, dst in ((wq, wq_b, 'QG'), (wk, wk_b, 'KG')):
                for pr in range(4):
                    bA, bA_b = kb.next_bank()
                    for c in range(8):
                        kb.mm(bA[:, :], w_t[:, c, pr * 256:pr * 256 + 128], aT[:, c, :], c == 0, c == 7,
                              [w_b, aT_b], [bA_b])
                    bB, bB_b = kb.next_bank()
                    for c in range(8):
                        kb.mm(bB[:, :], w_t[:, c, pr * 256 + 128:pr * 256 + 256], aT[:, c, :], c == 0, c == 7,
                              [w_b, aT_b], [bB_b])
                    m = m8[4 * (mi[0] % 2):4 * (mi[0] % 2) + 4]
                    mi[0] += 1
                    kb.tt(kb.dve, m[0][0][:, :], bA[:, :], C[:, :], ALU.mult, [bA_b, C_b], [m[0][1]])
                    kb.tt(kb.dve, m[1][0][:, :], bB[:, :], Sn[:, :], ALU.mult, [bB_b, Sn_b], [m[1][1]])
                    kb.tt(kb.dve, m[2][0][:, :], bA[:, :], Sn[:, :], ALU.mult, [bA_b, Sn_b], [m[2][1]])
                    kb.tt(kb.dve, m[3][0][:, :], bB[:, :], C[:, :], ALU.mult, [bB_b, C_b], [m[3][1]])
                    o_t, o_b, o_s = ost.next()
                    kb.tt(kb.pool, o_t[:, :], m[0][0][:, :], m[1][0][:, :], ALU.subtract, [m[0][1], m[1][1]], [o_b])
                    kb.dma(scr[dst][g, pr, 0, :, tcol], o_t[:, :], o_s, reads=[o_b])
                    o_t, o_b, o_s = ost.next()
                    kb.tt(kb.pool, o_t[:, :], m[2][0][:, :], m[3][0][:, :], ALU.add, [m[2][1], m[3][1]], [o_b])
                    kb.dma(scr[dst][g, pr, 1, :, tcol], o_t[:, :], o_s, reads=[o_b])
            for j in range(4):
                for hf in range(2):
                    bk, bk_b = kb.next_bank()
                    for c in range(8):
                        kb.mm(bk[:, :], aT[:, c, j * 128:(j + 1) * 128], wv[:, c, hf * 512:(hf + 1) * 512],
                              c == 0, c == 7, [wv_b, aT_b], [bk_b])
                    kb.actf(vsb[:, j, hf * 512:(hf + 1) * 512], bk[:, :], AF.Copy, [bk_b], [vsb_b], add=(j + hf > 0))
            kb.dma(scr['VG'][g, tcol, :].rearrange("(j p) f -> p j f", p=128), vsb[:, :, :], kb.st_sems[4],
                   reads=[vsb_b])
            if i + 1 < len(seq):
                transpose_evac(kb, 4, a, a_b, aTs[1 - s][0], aTs[1 - s][1])
            if i + 2 < len(seq):
                load(i + 2)
            if t == ntiles - 1 and nxt is not None:
                for p in nxt:
                    p()
        kb.piece_q = None
        kb.barrier()
        kb.flush()


def phase_dil_attn(kb, scr, nidx=48, DEPTH=3, pieces=None):
    from collections import deque
    kb.nb = 6
    with Ctx(kb) as cx:
        QT = [cx.sb([64, S], BF16) for _ in range(2)]
        KT = [cx.sb([64, S], BF16) for _ in range(2)]
        VA = [cx.sb([128, 32, 66], BF16) for _ in range(2)]
        pT = [cx.sb([128, 256], BF16) for _ in range(6)]
        UG = [cx.sb([128, 32, 65], F32) for _ in range(2)]
        mask, mask_b = cx.sb([128, 256], BF16)
        kb.dma(mask[:, :], scr['cbf'][:, 512:768], kb.misc_sem, writes=[mask_b])
        for i in range(2):
            kb.op(kb.pool, lambda h, i=i: h.memset(VA[i][0][:, :, 64:65], 1.0), [], [VA[i][1]])

        def load(idx):
            s = idx % 2
            g, h = idx // 16, idx % 16
            hb, hh = h // 4, h % 4
            rows = slice(hh * 32, (hh + 1) * 32)
            q_t, q_b = QT[s]
            k_t, k_b = KT[s]
            kb.dma(q_t[0:32, :], scr['QG'][g, hb, 0, rows, :], kb.at_sems[s], writes=[q_b])
            kb.dma(q_t[32:64, :], scr['QG'][g, hb, 1, rows, :], kb.at_sems[s], writes=[q_b], add=True)
            kb.dma(k_t[0:32, :], scr['KG'][g, hb, 0, rows, :], kb.at_sems[2 + s], writes=[k_b])
            kb.dma(k_t[32:64, :], scr['KG'][g, hb, 1, rows, :], kb.at_sems[2 + s], writes=[k_b], add=True)
            kb.dma(VA[s][0][:, :, 0:64], scr['VG'][g, :, h * 64:(h + 1) * 64].rearrange("(c p) d -> p c d", p=128),
                   kb.at_sems[4 + s], writes=[VA[s][1]], add=True)

        st = {"pi": 0, "ek": 0, "obi": 0}

        def emit(idx, B):
            s = idx % 2
            g, h = idx // 16, idx % 16
            d = DIL_D[g]
            Lb = S // d // 128
            q_t, q_b = QT[s]
            k_t, k_b = KT[s]
            v_t, v_b = VA[s]
            u_t, u_b = UG[s]
            j = B % Lb
            cur = slice(B * 128, (B + 1) * 128)
            prv = slice((B - 1) * 128, B * 128)
            sb_, sb_b = kb.next_bank()
            p_t, p_b = pT[st["pi"] % 6]
            st["pi"] += 1
            c0 = 0 if j > 0 else 128
            if j > 0:
                kb.mm(sb_[:, 0:128], k_t[:, prv], q_t[:, cur], True, True, [k_b, q_b], [sb_b])
            kb.mm(sb_[:, 128:256], k_t[:, cur], q_t[:, cur], True, True, [k_b, q_b], [sb_b])
            kb.actf(p_t[:, c0:256], sb_[:, c0:256], AF.Exp, [sb_b], [p_b], scale=0.125)
            kb.tt(kb.dve, p_t[:, c0:256], p_t[:, c0:256], mask[:, c0:256], ALU.mult, [p_b, mask_b], [p_b])
            B0 = (B // 7) * 7
            nb = min(7, 32 - B0)
            obi = (B // 7) % 2

            def pv():
                ob, ob_b = kb.banks[6 + obi], kb.bank_bufs[6 + obi]
                reg = ob[:, (B - B0) * 65:(B - B0 + 1) * 65]
                if j > 0:
                    kb.mm(reg, p_t[:, 0:128], v_t[:, B - 1, 0:65], True, False, [p_b, v_b], [ob_b])
                    kb.mm(reg, p_t[:, 128:256], v_t[:, B, 0:65], False, True, [p_b, v_b], [ob_b])
                else:
                    kb.mm(reg, p_t[:, 128:256], v_t[:, B, 0:65], True, True, [p_b, v_b], [ob_b])
                if B - B0 == nb - 1:
                    evac(kb, st["ek"], u_t[:, B0:B0 + nb, :], ob[:, 0:nb * 65].rearrange("p (b e) -> p b e", e=65),
                         [ob_b], [u_b], add=(B0 > 0))
                    st["ek"] += 1
                if B == 31:
                    OGv = scr['OG'][g].rearrange("(i r) (h e) -> r i h e", r=d, e=65)
                    for r in range(d):
                        kb.dma(OGv[r, :, h, :].rearrange("(j p) e -> p j e", p=128), u_t[:, r * Lb:(r + 1) * Lb, :],
                               kb.st_sems[s], reads=[u_b], add=(r > 0))
            return pv

        load(0)
        pend = deque()
        for idx in range(nidx):
            for B in range(32):
                if B == DEPTH + 1 and idx + 1 < nidx:
                    load(idx + 1)
                if B in (10, 24):
                    take(pieces, 1)
                pend.append(emit(idx, B))
                if len(pend) > DEPTH:
                    pend.popleft()()
        while pend:
            pend.popleft()()
        kb.barrier()
        kb.flush()


def phase_dil_oproj(kb, H_in, H_out, wo_d, scr, ntiles=32, stg=None, pieces=None):
    kb.nb = 8
    T = 128
    J = T // 128
    with Ctx(kb) as cx:
        wo, wo_b = cx.sb([128, 8, D], BF16)
        if stg is None:
            stg = make_stg(kb, cx)
        hx = [cx.sb([128, J, D], F32) for _ in range(2)]
        og = [cx.sb([128, 3, 1040], F32) for _ in range(2)]
        rden, rden_b = cx.sb([128, 2, 16], F32)
        ob, ob_b = cx.sb([128, J, D], BF16)
        oTs = [cx.sb([128, 8, T], BF16) for _ in range(2)]
        nsub = ntiles * J

        def load_h(t):
            s = t % 2
            kb.dma(hx[s][0][:, :, :], H_in[t * T:(t + 1) * T, :].rearrange("(j p) f -> p j f", p=128),
                   kb.ld_sems[s], writes=[hx[s][1]])

        def load_og(k):
            s = k % 2
            for g in range(3):
                kb.dma(og[s][0][:, g, :], scr['OG'][g, k * 128:(k + 1) * 128, :], kb.ld_sems[2 + s], writes=[og[s][1]],
                       add=(g > 0))

        def prep1(t):
            for j in range(J):
                k = t * J + j
                g_t, g_b = og[k % 2]
                kb.tt(kb.dve, g_t[:, 0, :], g_t[:, 0, :], g_t[:, 1, :], ALU.add, [g_b], [g_b])
                kb.tt(kb.dve, g_t[:, 0, :], g_t[:, 0, :], g_t[:, 2, :], ALU.add, [g_b], [g_b])
                g3 = g_t[:, 0, :].rearrange("p (h e) -> p h e", e=65)
                kb.op(kb.dve, lambda h_, g3=g3, j=j: h_.reciprocal(out=rden[:, j, :], in_=g3[:, :, 64]), [g_b], [rden_b],
                      add=(j > 0))
                kb.tt(kb.dve, ob[:, j, :].rearrange("p (h e) -> p h e", e=64), g3[:, :, 0:64],
                      rden[:, j, :].unsqueeze(2).broadcast_to([128, 16, 64]), ALU.mult, [g_b, rden_b], [ob_b],
                      add=(j > 0))
                if k + 2 < nsub:
                    load_og(k + 2)

        def prep2(t):
            oT, oT_b = oTs[t % 2]
            for j in range(J):
                bk, bk_b = kb.next_bank()
                bkv = bk[:, :].bitcast(BF16)
                for c in range(8):
                    kb.tr(bkv[:, c * 128:(c + 1) * 128], ob[:, j, c * 128:(c + 1) * 128], kb.ident[:, :],
                          [ob_b, kb.const_b], [bk_b])
                kb.actf(oT[:, :, j * 128:(j + 1) * 128], bkv.rearrange("p (c t) -> p c t", c=8), AF.Copy, [bk_b], [oT_b],
                        add=(j > 0))

        load_h(0)
        load_og(0)
        load_cast_weight(kb, cx, stg, wo_d, 0, 8, D, wo, wo_b)
        if nsub > 1:
            load_og(1)
        if ntiles > 1:
            load_h(1)
        prep1(0)
        prep2(0)
        for t in range(ntiles):
            s = t % 2
            h_t, h_b = hx[s]
            oT, oT_b = oTs[s]
            take(pieces, 2)
            if t + 1 < ntiles:
                prep1(t + 1)
            for j in range(J):
                for hf in range(2):
                    bk, bk_b = kb.next_bank()
                    for c in range(8):
                        kb.mm(bk[:, :], oT[:, c, j * 128:(j + 1) * 128], wo[:, c, hf * 512:(hf + 1) * 512],
                              c == 0, c == 7, [oT_b, wo_b], [bk_b])
                    kb.tt(kb.dve, h_t[:, j, hf * 512:(hf + 1) * 512], bk[:, :], h_t[:, j, hf * 512:(hf + 1) * 512],
                          ALU.add, [bk_b, h_b], [h_b])
            if t + 1 < ntiles:
                prep2(t + 1)
            kb.dma(H_out[t * T:(t + 1) * T, :].rearrange("(j p) f -> p j f", p=128), h_t[:, :, :], kb.st_sems[s],
                   reads=[h_b])
            if t + 2 < ntiles:
                load_h(t + 2)
        kb.barrier()
        kb.flush()
```
